# Optimizing a Trainium2 kernel written in Bass

```python
import jax, jax.numpy as jnp
from jax import lax
import numpy as np

D_MODEL = 1024
BATCH = 4
SEQ = 4096
DEPTH = 1

SSM_HEADS = 16
SSM_HEAD_DIM = 64
SSM_WIDTH = SSM_HEADS * SSM_HEAD_DIM
SSM_GROUPS = 2
SSM_STATE = 128
CONV_WIDTH = 4
CHUNK = 128
CONV_DIM = SSM_WIDTH + 2 * SSM_GROUPS * SSM_STATE
ATT_HEADS = 16
ATT_HEAD_DIM = 64
ATT_WIDTH = ATT_HEADS * ATT_HEAD_DIM
Q_BLOCK = 128
MIX_WIDTH = SSM_WIDTH + ATT_WIDTH
D_FF = 4 * D_MODEL
SPLITS = [SSM_WIDTH,
          SSM_WIDTH + CONV_DIM,
          SSM_WIDTH + CONV_DIM + SSM_HEADS,
          SSM_WIDTH + CONV_DIM + SSM_HEADS + ATT_WIDTH,
          SSM_WIDTH + CONV_DIM + SSM_HEADS + 2 * ATT_WIDTH,
          SSM_WIDTH + CONV_DIM + SSM_HEADS + 3 * ATT_WIDTH]
IN_COLS = SPLITS[-1] + ATT_HEADS
DEEPNORM_ALPHA = (2.0 * DEPTH) ** 0.25
DEEPNORM_BETA = (8.0 * DEPTH) ** -0.25
LN_EPS = 1e-5
RMS_EPS = 1e-5

kernel_name = "hymba_ssd_fox_deepnorm_adaln"


def layer_norm(x, g, b):
    xf = x.astype(jnp.float32)
    mu = jnp.mean(xf, axis=-1, keepdims=True)
    var = jnp.mean(jnp.square(xf - mu), axis=-1, keepdims=True)
    return ((xf - mu) * lax.rsqrt(var + LN_EPS) * g + b).astype(x.dtype)


def rms_norm(x, w):
    xf = x.astype(jnp.float32)
    return xf * lax.rsqrt(jnp.mean(xf * xf, axis=-1, keepdims=True) + RMS_EPS) * w


def causal_depthwise_conv(u, w, b):
    out = lax.conv_general_dilated(
        u, w[:, None, :].astype(u.dtype), window_strides=(1,),
        padding=[(CONV_WIDTH - 1, 0)], dimension_numbers=('NWC', 'WIO', 'NWC'),
        feature_group_count=u.shape[-1])
    return out + b


def segsum(a):
    cs = jnp.cumsum(a, axis=-1)
    diff = cs[..., :, None] - cs[..., None, :]
    t = a.shape[-1]
    mask = jnp.tril(jnp.ones((t, t), dtype=bool))
    return jnp.where(mask, diff, -jnp.inf)


def ssd_chunked(xh, dt, A, Bm, Cm):
    b, s, h, p = xh.shape
    g, n = Bm.shape[-2], Bm.shape[-1]
    e = h // g
    nc = s // CHUNK
    xc = (xh.astype(jnp.float32) * dt[..., None]).reshape(b, nc, CHUNK, g, e, p)
    Bc = Bm.astype(jnp.float32).reshape(b, nc, CHUNK, g, n)
    Cc = Cm.astype(jnp.float32).reshape(b, nc, CHUNK, g, n)
    a = (dt * A).reshape(b, nc, CHUNK, g, e).transpose(0, 3, 4, 1, 2)
    a_cs = jnp.cumsum(a, axis=-1)
    decay_in = jnp.exp(segsum(a))
    cb = jnp.einsum('bclgn,bcsgn->bgcls', Cc, Bc)
    scores = cb[:, :, None] * decay_in
    y_diag = jnp.einsum('bgecls,bcsgep->bclgep', scores, xc)
    decay_to_end = jnp.exp(a_cs[..., -1:] - a_cs).transpose(0, 3, 4, 1, 2)
    states = jnp.einsum('bclgn,bclgep->bcgepn', Bc, xc * decay_to_end[..., None])
    chunk_tot = jnp.pad(a_cs[..., -1], [(0, 0), (0, 0), (0, 0), (1, 0)])
    decay_chunk = jnp.exp(segsum(chunk_tot))
    states = jnp.concatenate([jnp.zeros_like(states[:, :1]), states], axis=1)
    new_states = jnp.einsum('bgezc,bcgepn->bzgepn', decay_chunk, states)
    prev_states = new_states[:, :-1]
    decay_out = jnp.exp(a_cs).transpose(0, 3, 4, 1, 2)
    y_off = jnp.einsum('bclgn,bcgepn->bclgep', Cc, prev_states) * decay_out[..., None]
    return (y_diag + y_off).reshape(b, s, h, p)


def forgetting_attention(q, k, v, log_f):
    s, d = q.shape[1], q.shape[-1]
    scale = d ** -0.5
    cum = jnp.cumsum(log_f, axis=1).transpose(0, 2, 1)
    outs = []
    for i in range(s // Q_BLOCK):
        q0, q1 = i * Q_BLOCK, (i + 1) * Q_BLOCK
        logits = jnp.einsum('bqhd,bkhd->bhqk', q[:, q0:q1], k[:, :q1],
                            preferred_element_type=jnp.float32) * scale
        logits = logits + (cum[:, :, q0:q1, None] - cum[:, :, None, :q1])
        mask = jnp.arange(q0, q1)[:, None] >= jnp.arange(q1)[None, :]
        logits = jnp.where(mask, logits, -jnp.inf)
        probs = jax.nn.softmax(logits, axis=-1)
        outs.append(jnp.einsum('bhqk,bkhd->bqhd', probs.astype(v.dtype), v[:, :q1]))
    return jnp.concatenate(outs, axis=1)


def hybrid_mixer(h, w_in, conv_w, conv_b, dt_bias, a_log, d_skip, ssm_norm_w, f_bias,
                 attn_norm_w, w_out):
    b, s, _ = h.shape
    proj = jnp.einsum('bsd,dk->bsk', h, w_in)
    z, xbc, dt_raw, q, k, v, f_raw = jnp.split(proj, SPLITS, axis=-1)
    xbc = jax.nn.silu(causal_depthwise_conv(xbc, conv_w, conv_b))
    xs, Bm, Cm = jnp.split(xbc, [SSM_WIDTH, SSM_WIDTH + SSM_GROUPS * SSM_STATE], axis=-1)
    xs = xs.reshape(b, s, SSM_HEADS, SSM_HEAD_DIM)
    Bm = Bm.reshape(b, s, SSM_GROUPS, SSM_STATE)
    Cm = Cm.reshape(b, s, SSM_GROUPS, SSM_STATE)
    dt = jax.nn.softplus(dt_raw.astype(jnp.float32) + dt_bias)
    A = -jnp.exp(a_log.astype(jnp.float32))
    y = ssd_chunked(xs, dt, A, Bm, Cm) + d_skip[:, None] * xs
    y_ssm = rms_norm(y.reshape(b, s, SSM_WIDTH) * jax.nn.silu(z), ssm_norm_w)
    q = q.reshape(b, s, ATT_HEADS, ATT_HEAD_DIM)
    k = k.reshape(b, s, ATT_HEADS, ATT_HEAD_DIM)
    v = v.reshape(b, s, ATT_HEADS, ATT_HEAD_DIM)
    log_f = jax.nn.log_sigmoid(f_raw.astype(jnp.float32) + f_bias)
    y_att = forgetting_attention(q, k, v, log_f).reshape(b, s, ATT_WIDTH)
    y_att = rms_norm(y_att, attn_norm_w)
    y_mix = jnp.concatenate([y_ssm, y_att.astype(y_ssm.dtype)], axis=-1)
    return jnp.einsum('bsk,kd->bsd', y_mix.astype(h.dtype), w_out)


def setup_inputs(seed: int = 0) -> dict:
    key = jax.random.key(seed)
    ks = jax.random.split(key, 24)
    f32 = jnp.float32
    nrm = lambda k, shape, s: jax.random.normal(k, shape, f32) * s
    dt0 = jnp.exp(jax.random.uniform(ks[6], (DEPTH, SSM_HEADS), f32,
                                     np.log(1e-3).astype(np.float32), np.log(1e-1).astype(np.float32)))
    return {
        "x": nrm(ks[0], (BATCH, SEQ, D_MODEL), 1.0),
        "c": nrm(ks[1], (BATCH, D_MODEL), 1.0),
        "w_ada": nrm(ks[2], (DEPTH, D_MODEL, 6 * D_MODEL), 0.5 * D_MODEL ** -0.5),
        "b_ada": nrm(ks[3], (DEPTH, 6 * D_MODEL), 0.01),
        "w_in": nrm(ks[4], (DEPTH, D_MODEL, IN_COLS), D_MODEL ** -0.5),
        "conv_w": nrm(ks[5], (DEPTH, CONV_WIDTH, CONV_DIM), CONV_WIDTH ** -0.5),
        "conv_b": nrm(ks[7], (DEPTH, CONV_DIM), 0.01),
        "dt_bias": dt0 + jnp.log(-jnp.expm1(-dt0)),
        "a_log": jnp.log(jax.random.uniform(ks[8], (DEPTH, SSM_HEADS), f32, 1.0, 16.0)),
        "d_skip": 1.0 + nrm(ks[9], (DEPTH, SSM_HEADS), 0.1),
        "ssm_norm_w": 1.0 + nrm(ks[10], (DEPTH, SSM_WIDTH), 0.05),
        "f_bias": jax.random.uniform(ks[11], (DEPTH, ATT_HEADS), f32, 1.0, 4.0),
        "attn_norm_w": 1.0 + nrm(ks[12], (DEPTH, ATT_WIDTH), 0.05),
        "w_out": nrm(ks[13], (DEPTH, MIX_WIDTH, D_MODEL), DEEPNORM_BETA * MIX_WIDTH ** -0.5),
        "ln1_g": 1.0 + nrm(ks[14], (DEPTH, D_MODEL), 0.05),
        "ln1_b": nrm(ks[15], (DEPTH, D_MODEL), 0.01),
        "w_ff_in": nrm(ks[16], (DEPTH, D_MODEL, D_FF), D_MODEL ** -0.5),
        "w_ff_out": nrm(ks[17], (DEPTH, D_FF, D_MODEL), DEEPNORM_BETA * D_FF ** -0.5),
        "ln2_g": 1.0 + nrm(ks[18], (DEPTH, D_MODEL), 0.05),
        "ln2_b": nrm(ks[19], (DEPTH, D_MODEL), 0.01),
    }


def reference(x, c, w_ada, b_ada, w_in, conv_w, conv_b, dt_bias, a_log, d_skip, ssm_norm_w,
              f_bias, attn_norm_w, w_out, ln1_g, ln1_b, w_ff_in, w_ff_out, ln2_g, ln2_b):
    c_act = jax.nn.silu(c)
    for l in range(DEPTH):
        mod = jnp.einsum('bd,de->be', c_act, w_ada[l]) + b_ada[l]
        sh1, sc1, g1, sh2, sc2, g2 = [m[:, None, :] for m in jnp.split(mod, 6, axis=-1)]
        h = x * (1.0 + sc1) + sh1
        y = hybrid_mixer(h, w_in[l], conv_w[l], conv_b[l], dt_bias[l], a_log[l], d_skip[l],
                         ssm_norm_w[l], f_bias[l], attn_norm_w[l], w_out[l])
        x = layer_norm(DEEPNORM_ALPHA * x + (1.0 + g1) * y, ln1_g[l], ln1_b[l])
        h = x * (1.0 + sc2) + sh2
        ff = jnp.einsum('bsf,fd->bsd',
                        jnp.square(jax.nn.relu(jnp.einsum('bsd,df->bsf', h, w_ff_in[l]))),
                        w_ff_out[l])
        x = layer_norm(DEEPNORM_ALPHA * x + (1.0 + g2) * ff, ln2_g[l], ln2_b[l])
    return x
```

```python
import contextlib
import numpy as np
import concourse.bass as bass
import concourse.mybir as mybir
from concourse.bass_utils import run_bass_kernel_spmd

F32 = mybir.dt.float32
BF16 = mybir.dt.bfloat16
AF = mybir.ActivationFunctionType
ALU = mybir.AluOpType

COMPUTE = ("pe", "act", "dve", "pool")
SELF_SYNC = ("act", "dve", "pool")
QUEUES = COMPUTE + ("sync",)


class Buf:
    __slots__ = ("name", "t", "tb", "lw", "rd")

    def __init__(self, name, t, tb=None):
        self.name = name
        self.t = t
        self.tb = tb
        self.lw = None
        self.rd = []

    def __getitem__(self, k):
        return self.t[k]


class Prog:
    def __init__(self, nc):
        self.nc = nc
        self.ops = {e: [] for e in QUEUES}
        self.cnt = {e: 0 for e in COMPUTE}
        self.waited = {e: {} for e in QUEUES}
        self.semvals = {}
        self.semkeys = []

    def _need(self, eng, waits, tok):
        if tok is None:
            return
        if tok[0] == "e":
            _, e2, idx = tok
            if e2 == eng and eng not in SELF_SYNC:
                return
            key = ("e", e2)
            val = idx
        else:
            _, sk, val = tok
            key = ("s", sk)
        if self.waited[eng].get(key, 0) >= val:
            return
        if waits.get(key, 0) < val:
            waits[key] = val

    def _collect(self, eng, reads, writes):
        waits = {}
        for b in reads:
            self._need(eng, waits, b.lw)
        for b in writes:
            self._need(eng, waits, b.lw)
            for tok in b.rd:
                self._need(eng, waits, tok)
        for k, v in waits.items():
            self.waited[eng][k] = v
        return waits

    def _mark(self, tok, reads, writes):
        for b in reads:
            b.rd.append(tok)
            if len(b.rd) > 12:
                best = {}
                for t in b.rd:
                    k = (t[0], t[1])
                    if k not in best or best[k][2] < t[2]:
                        best[k] = t
                b.rd = list(best.values())
        for b in writes:
            b.lw = tok
            b.rd = []

    def op(self, eng, fn, reads=(), writes=()):
        waits = self._collect(eng, reads, writes)
        self.cnt[eng] += 1
        tok = ("e", eng, self.cnt[eng])
        self._mark(tok, reads, writes)
        self.ops[eng].append((waits, fn, None))
        return tok

    def dma(self, q, fn, semkey, reads=(), writes=()):
        waits = self._collect(q, reads, writes)
        if semkey not in self.semvals:
            self.semvals[semkey] = 0
            self.semkeys.append(semkey)
        self.semvals[semkey] += 16
        tok = ("s", semkey, self.semvals[semkey])
        self._mark(tok, reads, writes)
        self.ops[q].append((waits, fn, semkey))
        return tok

    def emit(self):
        nc = self.nc
        with contextlib.ExitStack() as st:
            esem = {e: st.enter_context(nc.semaphore("se_" + e)) for e in COMPUTE}
            ssem = {k: st.enter_context(nc.semaphore("sd_%d" % i)) for i, k in enumerate(self.semkeys)}
            block = st.enter_context(nc.Block())

            def semof(key):
                return esem[key[1]] if key[0] == "e" else ssem[key[1]]

            def replay(engobj, name, extra=None):
                for waits, fn, semkey in self.ops[name]:
                    for key, val in waits.items():
                        engobj.wait_ge(semof(key), val)
                    ins = fn(engobj)
                    if semkey is not None:
                        ins.then_inc(ssem[semkey], 16)
                    else:
                        ins.then_inc(esem[name], 1)
                if extra:
                    extra(engobj)

            def final(engobj):
                for k in self.semkeys:
                    engobj.wait_ge(ssem[k], self.semvals[k])
                for e in COMPUTE:
                    if self.cnt[e]:
                        engobj.wait_ge(esem[e], self.cnt[e])

            @block.tensor
            def _(e):
                replay(e, "pe")

            @block.vector
            def _(e):
                replay(e, "dve")

            @block.gpsimd
            def _(e):
                replay(e, "pool")

            @block.scalar
            def _(e):
                replay(e, "act")

            @block.sync
            def _(e):
                replay(e, "sync", extra=final)


DM = 1024
TO = 2048
TP = 2048
TT = TO + TP
NH = 16
CZ, CX, CB, CC, CDT, CQ, CK, CV, CF = 0, 1024, 2048, 2304, 2560, 2576, 3600, 4624, 5648
ALPHA = 2.0 ** 0.25
EPS = 1e-5
NW = 52600
NEG = -30000.0

WEIGHT_NAMES = ["w_ada", "b_ada", "w_in", "conv_w", "conv_b", "dt_bias", "a_log", "d_skip", "ssm_norm_w",
                "f_bias", "attn_norm_w", "w_out", "ln1_g", "ln1_b", "w_ff_in", "w_ff_out", "ln2_g", "ln2_b"]
WEIGHT_SHAPES = {"w_ada": [1024, 6144], "b_ada": [1, 6144], "w_in": [1024, 5664], "conv_w": [4, 1536],
                 "conv_b": [1, 1536], "dt_bias": [1, 16], "a_log": [1, 16], "d_skip": [1, 16],
                 "ssm_norm_w": [1, 1024], "f_bias": [1, 16], "attn_norm_w": [1, 1024], "w_out": [2048, 1024],
                 "ln1_g": [1, 1024], "ln1_b": [1, 1024], "w_ff_in": [1024, 4096], "w_ff_out": [4096, 1024],
                 "ln2_g": [1, 1024], "ln2_b": [1, 1024]}


def build(stop_after=99):
    nc = bass.Bass("TRN2", target_bir_lowering=False)
    D = {}
    for n, shp in [("xo", [TO, DM]), ("xp", [TP, DM]), ("c", [1, DM]), ("flags", [128, 2])]:
        D[n] = nc.dram_tensor(n, shp, F32, kind="ExternalInput").ap()
    for n in WEIGHT_NAMES:
        D[n] = nc.dram_tensor(n, WEIGHT_SHAPES[n], F32, kind="ExternalInput").ap()
    out_d = nc.dram_tensor("out", [TO, DM], F32, kind="ExternalOutput").ap()
    x1s = nc.dram_tensor("x1s", [TO, DM], F32, kind="Internal").ap()
    c3d = nc.dram_tensor("c3d", [16, 3 * TT], BF16, kind="Internal").ap()
    ymsd = nc.dram_tensor("ymsd", [128, 8 * 2048], BF16, kind="Internal").ap()
    w1b = nc.dram_tensor("w1b", [1024, 4096], BF16, kind="Internal").ap()
    w2b = nc.dram_tensor("w2b", [4096, 1024], BF16, kind="Internal").ap()
    wob = nc.dram_tensor("wob", [2048, 1024], BF16, kind="Internal").ap()
    w_in = D["w_in"]

    st = contextlib.ExitStack()
    S = st.enter_context(nc.sbuf_tensor("S", [128, NW], F32))
    PS = st.enter_context(nc.psum_tensor("PS", [128, 4096], F32))
    P = Prog(nc)

    class A:
        top = 0

    regs = []

    def register(buf, lo, hi):
        for (l2, h2, b2) in regs:
            if l2 < hi and lo < h2 and b2 is not buf:
                if b2.lw is not None:
                    buf.rd.append(b2.lw)
                buf.rd.extend(b2.rd)
        regs.append((lo, hi, buf))
        return buf

    def alias(name, lo, hi, dt=F32, pat=None, **kw):
        assert hi <= NW, (name, hi)
        v = S[:, lo:hi]
        if dt == BF16:
            v = v.bitcast(BF16)
        if pat:
            v = v.rearrange(pat, **kw)
        return register(Buf(name, v), lo, hi)

    def sbw(name, words, dt=F32, pat=None, **kw):
        off = A.top
        A.top += words
        return alias(name, off, off + words, dt, pat, **kw)

    BK = [Buf("bk%d" % i, PS[:, 512 * i:512 * (i + 1)], PS[:, 512 * i:512 * (i + 1)].bitcast(BF16)) for i in range(8)]
    bkc = [0]

    ROT = [0, 1, 2, 3, 4, 7]

    def nb():
        b = BK[ROT[bkc[0] % len(ROT)]]
        bkc[0] += 1
        return b

    def MM(bank, out, lhsT, rhs, rd, start, stop=True):
        P.op("pe", lambda e: e.matmul(out, lhsT=lhsT, rhs=rhs, start=start, stop=stop, skip_group_check=True),
             reads=rd, writes=[bank])

    def TR(bank, out, in_, ident, rd):
        P.op("pe", lambda e: e.transpose(out=out, in_=in_, identity=ident), reads=rd, writes=[bank])

    def ACT(out, in_, func, rd, wr, bias=0.0, scale=1.0, accum=None):
        if accum is None:
            P.op("act", lambda e: e.activation(out=out, in_=in_, func=func, bias=bias, scale=scale), reads=rd, writes=wr)
        else:
            P.op("act", lambda e: e.activation(out=out, in_=in_, func=func, bias=bias, scale=scale, accum_out=accum),
                 reads=rd, writes=wr)

    def TT_(eng, out, in0, in1, op, rd, wr):
        P.op(eng, lambda e: e.tensor_tensor(out=out, in0=in0, in1=in1, op=op), reads=rd, writes=wr)

    def TS(eng, out, in0, s1, s2, op0, op1, rd, wr):
        if s2 is None:
            P.op(eng, lambda e: e.tensor_scalar(out=out, in0=in0, scalar1=s1, scalar2=None, op0=op0), reads=rd, writes=wr)
        else:
            P.op(eng, lambda e: e.tensor_scalar(out=out, in0=in0, scalar1=s1, scalar2=s2, op0=op0, op1=op1),
                 reads=rd, writes=wr)

    def STT(out, in0, scalar, in1, op0, op1, rd, wr):
        P.op("dve", lambda e: e.scalar_tensor_tensor(out=out, in0=in0, scalar=scalar, in1=in1, op0=op0, op1=op1),
             reads=rd, writes=wr)

    def CP(eng, out, in_, rd, wr):
        if eng == "act":
            P.op("act", lambda e: e.copy(out=out, in_=in_), reads=rd, writes=wr)
        else:
            P.op(eng, lambda e: e.tensor_copy(out=out, in_=in_), reads=rd, writes=wr)

    def MEMSET(eng, out, val, wr):
        P.op(eng, lambda e: e.memset(out, val), writes=wr)

    def DMA(q, out, in_, key, rd=(), wr=(), slow=False):
        if slow:
            P.dma(q, lambda e: e.dma_start(out=out, in_=in_, allow_slow_non_contiguous=True), key, reads=rd, writes=wr)
        else:
            P.dma(q, lambda e: e.dma_start(out=out, in_=in_), key, reads=rd, writes=wr)

    def wload(dst_buf, dst_ap_fn, rows0, c0, ncols, nk, key, src=None):
        src = w_in if src is None else src
        step = 8
        for k0 in range(0, nk, step):
            DMA("pool", dst_ap_fn(slice(k0, k0 + step)),
                src[rows0 + k0 * 128: rows0 + (k0 + step) * 128, c0:c0 + ncols].rearrange("(k p) n -> p k n", p=128),
                key, wr=[dst_buf])

    hT_all = sbw("hT", 16384, BF16, "p (k t) -> p k t", k=8)
    hTg = [register(Buf("hT%d" % g, hT_all.t[:, :, g * 512:(g + 1) * 512]), 0, 16384) for g in range(8)]
    YM0 = A.top
    ymT_all = sbw("ymT", 16384, BF16, "p (k t) -> p k t", k=16)
    ymS = [register(Buf("ymS%d" % b, ymT_all.t[:, 0:8, b * 128:(b + 1) * 128]), YM0, YM0 + 8192) for b in range(16)]
    ymA = [register(Buf("ymA%d" % b, ymT_all.t[:, 8:16, b * 128:(b + 1) * 128]), YM0 + 8192, YM0 + 16384) for b in range(16)]
    AT0 = YM0 + 8192

    identF = sbw("identF", 128)
    tri = sbw("tri", 128)
    onesF = sbw("onesF", 128)
    maskneg = sbw("maskneg", 128)
    identB = sbw("identB", 64, BF16)
    maskB = sbw("maskB", 64, BF16)
    convw = sbw("convw", 48, F32, "p (c k) -> p c k", k=4)
    convb = sbw("convb", 12)
    dtb = sbw("dtb", 16)
    Abc = sbw("Abc", 16)
    dsk = sbw("dsk", 16)
    fbb = sbw("fbb", 16)
    ssw = sbw("ssw", 8)
    atw = sbw("atw", 8)
    flg = sbw("flg", 2)
    ccol = sbw("ccol", 8)
    csil = sbw("csil", 4, BF16)
    modp1 = sbw("modp1", 16)
    modp2 = sbw("modp2", 16)
    modp3 = sbw("modp3", 16)
    atwh = sbw("atwh", 16)
    sscol = sbw("sscol", 16)
    valid = flg[:, 0:1]

    MEMSET("pool", identF[:], 0.0, [identF])
    P.op("pool", lambda e: e.affine_select(out=identF[:], in_=identF[:], pattern=[[-1, 128]], compare_op=ALU.not_equal,
                                           fill=1.0, base=0, channel_multiplier=1), reads=[identF], writes=[identF])
    MEMSET("pool", onesF[:], 1.0, [onesF])
    MEMSET("pool", tri[:], 1.0, [tri])
    P.op("pool", lambda e: e.affine_select(out=tri[:], in_=tri[:], pattern=[[1, 128]], compare_op=ALU.is_ge,
                                           fill=0.0, base=0, channel_multiplier=-1), reads=[tri], writes=[tri])
    MEMSET("pool", maskneg[:], 0.0, [maskneg])
    P.op("pool", lambda e: e.affine_select(out=maskneg[:], in_=maskneg[:], pattern=[[1, 128]], compare_op=ALU.is_ge,
                                           fill=NEG, base=0, channel_multiplier=-1), reads=[maskneg], writes=[maskneg])
    CP("pool", identB[:], identF[:], [identF], [identB])
    CP("pool", maskB[:], maskneg[:], [maskneg], [maskB])

    for k in range(4):
        DMA("sync", convw[:, :, k], D["conv_w"][k].rearrange("(c p) -> p c", p=128), "cw%d" % k, wr=[convw], slow=True)
    DMA("sync", convb[:], D["conv_b"][0].rearrange("(c p) -> p c", p=128), "cb", wr=[convb], slow=True)
    DMA("sync", ssw[:], D["ssm_norm_w"][0].rearrange("(c p) -> p c", p=128), "ssw", wr=[ssw], slow=True)
    DMA("sync", atw[:], D["attn_norm_w"][0].rearrange("(c p) -> p c", p=128), "atw", wr=[atw], slow=True)
    DMA("sync", atwh[0:64, :], D["attn_norm_w"][0].rearrange("(h d) -> d h", d=64), "atwh", wr=[atwh], slow=True)
    DMA("sync", ccol[:], D["c"][0].rearrange("(c p) -> p c", p=128), "ccol", wr=[ccol], slow=True)
    DMA("sync", dtb[:], D["dt_bias"][0:1, :].partition_broadcast(128), "dtb", wr=[dtb])
    DMA("sync", Abc[:], D["a_log"][0:1, :].partition_broadcast(128), "alog", wr=[Abc])
    DMA("sync", dsk[:], D["d_skip"][0:1, :].partition_broadcast(128), "dsk", wr=[dsk])
    DMA("sync", fbb[:], D["f_bias"][0:1, :].partition_broadcast(128), "fbb", wr=[fbb])
    DMA("sync", flg[:], D["flags"], "flg", wr=[flg])
    ACT(Abc[:], Abc[:], AF.Exp, [Abc], [Abc])
    TS("dve", Abc[:], Abc[:], -1.0, None, ALU.mult, None, [Abc], [Abc])
    ACT(csil[:], ccol[:], AF.Silu, [ccol], [csil])

    mark0 = A.top

    def ada_cols(c0, ncols, row, wt, brow):
        DMA("sync", brow[0:1, 0:ncols], D["b_ada"][0:1, c0:c0 + ncols], "brow", wr=[brow])
        for j in range(ncols // 512):
            DMA("pool", wt[:, :, :], D["w_ada"][:, c0 + j * 512: c0 + (j + 1) * 512].rearrange("(k p) n -> p k n", p=128),
                "wada", wr=[wt])
            bank = nb()
            for k in range(8):
                MM(bank, bank[0:1, 0:512], csil[:, k:k + 1], wt[:, k, :], [csil, wt], k == 0, k == 7)
            TT_("dve", row[0:1, j * 512:(j + 1) * 512], bank[0:1, 0:512], brow[0:1, j * 512:(j + 1) * 512], ALU.add,
                [bank, brow], [row])

    def row_to_cols(row, seg0, nseg, dst, dcol0, add1_from=None):
        bank = nb()
        for s in range(nseg):
            MM(bank, bank[:, s:s + 1], row[0:1, (seg0 + s) * 128:(seg0 + s + 1) * 128], onesF[0:1, 0:1], [row, onesF], s == 0)
        CP("dve", dst[:, dcol0:dcol0 + nseg], bank[:, 0:nseg], [bank], [dst])
        if add1_from is not None:
            TS("dve", dst[:, dcol0 + add1_from:dcol0 + nseg], dst[:, dcol0 + add1_from:dcol0 + nseg], 1.0, None, ALU.add, None,
               [dst], [dst])

    A.top = mark0
    Wz = sbw("Wz", 4096, BF16, "p (k n) -> p k n", k=8)
    markE = A.top
    row = sbw("row", 2048)
    brow = sbw("brow", 2048)
    wt_ada = sbw("wt_ada", 2048, BF16, "p (k n) -> p k n", k=8)
    ada_cols(0, 2048, row, wt_ada, brow)
    row_to_cols(row, 0, 16, modp1, 0, add1_from=8)
    Wc = alias("Wc", AT0 + 0, AT0 + 6144, BF16, "p (k n) -> p k n", k=8)
    Wdt = alias("Wdt", AT0 + 6144, AT0 + 6208, BF16, "p (k n) -> p k n", k=8)
    wload(Wc, lambda k: Wc[:, k, :], 0, CX, 1536, 8, "Wc")
    wload(Wdt, lambda k: Wdt[:, k, :], 0, CDT, 16, 8, "Wdt")
    wload(Wz, lambda k: Wz[:, k, :], 0, CZ, 1024, 8, "Wz")

    xin = [sbw("xin%d" % i, 1024) for i in range(2)]
    for blk in range(32):
        src = D["xp"][blk * 128:(blk + 1) * 128, :] if blk < 16 else D["xo"][(blk - 16) * 128:(blk - 15) * 128, :]
        xt = xin[blk % 2]
        DMA("sync", xt[:], src, "xin%d" % (blk % 2), wr=[xt])
        g = blk // 4
        t0 = blk * 128
        for half in range(2):
            bank = nb()
            for j in range(4):
                k = half * 4 + j
                TR(bank, bank[:, j * 128:(j + 1) * 128], xt[:, k * 128:(k + 1) * 128], identF[:], [xt, identF])
            for j in range(4):
                k = half * 4 + j
                if j % 2 == 0:
                    ACT(hT_all.t[:, k, t0:t0 + 128], bank[:, j * 128:(j + 1) * 128], AF.Identity, [bank, modp1], [hTg[g]],
                        bias=modp1[:, k:k + 1], scale=modp1[:, 8 + k:9 + k])
                else:
                    TS("dve", hT_all.t[:, k, t0:t0 + 128], bank[:, j * 128:(j + 1) * 128], modp1[:, 8 + k:9 + k],
                       modp1[:, k:k + 1], ALU.mult, ALU.add, [bank, modp1], [hTg[g]])

    rowB = alias("rowB", YM0, YM0 + 1024)
    browB = alias("browB", YM0 + 1024, YM0 + 2048)
    wtB = [alias("wtB%d" % i, YM0 + 2048 + i * 2048, YM0 + 4096 + i * 2048, BF16, "p (k n) -> p k n", k=8) for i in range(2)]

    def ada_part_load(part):
        c0 = 2048 + part * 1024
        DMA("sync", browB[0:1, 0:1024], D["b_ada"][0:1, c0:c0 + 1024], "browB", wr=[browB])
        for j in range(2):
            DMA("pool", wtB[j][:, :, :], D["w_ada"][:, c0 + j * 512: c0 + (j + 1) * 512].rearrange("(k p) n -> p k n", p=128),
                "wadaB%d" % j, wr=[wtB[j]])

    def ada_part_compute(part):
        for j in range(2):
            bank = nb()
            for k in range(8):
                MM(bank, bank[0:1, 0:512], csil[:, k:k + 1], wtB[j][:, k, :], [csil, wtB[j]], k == 0, k == 7)
            TT_("dve", rowB[0:1, j * 512:(j + 1) * 512], bank[0:1, 0:512], browB[0:1, j * 512:(j + 1) * 512], ALU.add,
                [bank, browB], [rowB])
        dst, dcol, add1 = [(modp3, 0, 0), (modp2, 0, None), (modp2, 8, 0), (modp3, 8, 0)][part]
        row_to_cols(rowB, 0, 8, dst, dcol, add1_from=add1)

    A.top = markE
    BT = alias("BT", AT0 + 6208, AT0 + 6720, BF16, "p (g t) -> p g t", g=2)
    CT = alias("CT", AT0 + 6720, AT0 + 7232, BF16, "p (g t) -> p g t", g=2)
    Btok = alias("Btok", AT0 + 7232, AT0 + 7744, BF16, "p (j n) -> p j n", j=4)
    xs_tok = sbw("xs_tok", 4096, F32, "p (j c) -> p j c", j=4)
    ubuf = [sbw("ubuf%d" % i, 516) for i in range(2)]
    halo = sbw("halo", 36, F32, "p (c k) -> p c k", k=3)
    acc0_off = A.top
    acc = [sbw("acc%d" % i, 512) for i in range(3)]
    xsT0_off = A.top
    xsT = [sbw("xsT%d" % i, 512) for i in range(2)]
    TAv = [S[:, acc0_off:acc0_off + 1024], S[:, xsT0_off:xsT0_off + 1024]]
    TAb = [[acc[0], acc[1]], [xsT[0], xsT[1]]]
    assert xsT0_off == acc0_off + 1536
    xc = sbw("xc", 512, BF16)
    xcd = sbw("xcd", 512, BF16)
    Sst = sbw("Sst", 1024)
    Sbf = sbw("Sbf", 512, BF16)
    zs = sbw("zs", 1024)
    y1 = sbw("y1", 1024)
    y2 = sbw("y2", 1024)
    MTb = sbw("MTb", 512, BF16, "p (h l) -> p h l", h=8)
    cbm = sbw("cbm", 256)
    Ust = sbw("Ust", 128)
    sms = [sbw("sm%d" % i, 64 * 8) for i in range(1)]
    sq = sbw("sq", 8)
    print("[kernel] SSD words free:", NW - A.top, flush=True)
    MEMSET("pool", Ust[:], 1.0, [Ust])
    P.op("pool", lambda e: e.affine_select(out=Ust[:], in_=Ust[:], pattern=[[-1, 128]], compare_op=ALU.is_gt,
                                           fill=0.0, base=0, channel_multiplier=1), reads=[Ust], writes=[Ust])

    MEMSET("pool", halo[:], 0.0, [halo])
    MEMSET("pool", Sst[:], 0.0, [Sst])

    cnt = [0]
    pend_fin = []
    SPAIRS = [(BK[0], BK[1], PS[:, 0:1024]), (BK[2], BK[3], PS[:, 1024:2048])]
    for G in range(8):
        own = G >= 4
        hg = hTg[G]
        tcol = slice(G * 512, (G + 1) * 512)
        if G == 4:
            hv = halo.t.rearrange("p c k -> p (c k)")
            TS("dve", hv, hv, valid, None, ALU.mult, None, [halo, flg], [halo])
            TS("dve", Sst[:], Sst[:], valid, None, ALU.mult, None, [Sst, flg], [Sst])
        cts = list(range(12)) if (own or G == 3) else list(range(10))
        if G < 4:
            ada_part_load(G)
        pend_c = []
        pend_d = []
        for ct in cts:
            i2 = cnt[0] % 2
            i3 = cnt[0] % 3
            cnt[0] += 1
            bank = nb()
            for k in range(8):
                MM(bank, bank[:, 0:512], Wc[:, k, ct * 128:(ct + 1) * 128], hT_all.t[:, k, tcol], [Wc, hg], k == 0, k == 7)
            while len(pend_d) > 2:
                pend_d.pop(0)()
            ub = ubuf[i2]
            CP("pool", ub[:, 0:3], halo[:, ct, :], [halo], [ub])
            CP("act", ub[:, 3:515], bank[:, 0:512], [bank], [ub])
            CP("pool", halo[:, ct, :], ub[:, 512:515], [ub], [halo])
            ac = acc[i3]
            ACT(ac[:], ub[:, 0:512], AF.Identity, [ub, convw, convb], [ac], bias=convb[:, ct:ct + 1], scale=convw[:, ct, 0:1])
            while len(pend_c) > 1:
                pend_c.pop(0)()
            for kk in range(1, 4):
                STT(ac[:], ub[:, kk:kk + 512], convw[:, ct, kk:kk + 1], ac[:], ALU.mult, ALU.add, [ub, convw, ac], [ac])
            if ct < 8:
                xt_ = xsT[i2]

                def cx(xt_=xt_, ac=ac):
                    ACT(xt_[:], ac[:], AF.Silu, [ac], [xt_])

                def dx(xt_=xt_, ct=ct):
                    b2 = nb()
                    for j in range(4):
                        TR(b2, b2[:, j * 128:(j + 1) * 128], xt_[:, j * 128:(j + 1) * 128], identF[:], [xt_, identF])
                    CP("dve" if ct % 2 else "act", xs_tok[:, :, ct * 128:(ct + 1) * 128],
                       b2[:, 0:512].rearrange("p (j c) -> p j c", j=4), [b2], [xs_tok])
                pend_c.append(cx)
                pend_d.append(dx)
            elif ct < 10:
                g = ct - 8

                def cb_(g=g, ac=ac):
                    ACT(BT[:, g, :], ac[:], AF.Silu, [ac], [BT])

                def db_(g=g):
                    b2 = nb()
                    for j in range(4):
                        TR(b2, b2.tb[:, j * 128:(j + 1) * 128], BT[:, g, j * 128:(j + 1) * 128], identB[:], [BT, identB])
                    CP("dve", Btok[:, :, g * 128:(g + 1) * 128], b2.tb[:, 0:512].rearrange("p (j c) -> p j c", j=4), [b2], [Btok])
                pend_c.append(cb_)
                pend_d.append(db_)
            else:
                g = ct - 10

                def cc_(g=g, ac=ac):
                    ACT(CT[:, g, :], ac[:], AF.Silu, [ac], [CT])
                pend_c.append(cc_)

        def flush_conv():
            while pend_c:
                pend_c.pop(0)()
            while pend_d:
                pend_d.pop(0)()
        if G < 4:
            ada_part_compute(G)
        bdt = nb()
        for j in range(4):
            ctok = slice((G * 4 + j) * 128, (G * 4 + j + 1) * 128)
            for k in range(8):
                MM(bdt, bdt[:, j * 16:(j + 1) * 16], hT_all.t[:, k, ctok], Wdt[:, k, :], [hg, Wdt], (j == 0 and k == 0), k == 7)
        flush_conv()
        sm = sms[0]
        xr4, dtt4, aa4, acs4, dout4, dte4, dch4, wv4 = [sm[:, i * 64:(i + 1) * 64] for i in range(8)]

        def v3(ap):
            return ap.rearrange("p (j h) -> p j h", j=4)
        TT_("dve", v3(xr4), v3(bdt[:, 0:64]), dtb[:].unsqueeze(1).to_broadcast([128, 4, 16]), ALU.add, [bdt, dtb], [sm])
        ACT(xr4, xr4, AF.Exp, [sm], [sm])
        ACT(dtt4, xr4, AF.Ln, [sm], [sm], bias=1.0)
        TT_("dve", v3(aa4), v3(dtt4), Abc[:].unsqueeze(1).to_broadcast([128, 4, 16]), ALU.mult, [sm, Abc], [sm])
        bcs = nb()
        MM(bcs, bcs[:, 0:64], tri[:], aa4, [tri, sm], True)
        MM(bcs, bcs[:, 64:128], onesF[:], aa4, [onesF, sm], False)
        CP("dve", acs4, bcs[:, 0:64], [bcs], [sm])
        ACT(dout4, bcs[:, 0:64], AF.Exp, [bcs], [sm])
        ACT(dch4, bcs[:, 64:128], AF.Exp, [bcs], [sm])
        TT_("dve", dte4, bcs[:, 64:128], acs4, ALU.subtract, [bcs, sm], [sm])
        ACT(dte4, dte4, AF.Exp, [sm], [sm])
        TT_("dve", wv4, dtt4, dte4, ALU.mult, [sm], [sm])
        for j in range(4):
            ch = G * 4 + j
            ctok = slice(ch * 128, (ch + 1) * 128)
            xr, dtt, aa, acs, dout, dte, dch, wv = [sm[:, i * 64 + j * 16:i * 64 + (j + 1) * 16] for i in range(8)]
            xsj = xs_tok[:, j, :].rearrange("p (h d) -> p h d", h=16)
            if not own:
                TT_("pool", xcd[:].rearrange("p (h d) -> p h d", h=16), xsj, wv.unsqueeze(2).to_broadcast([128, 16, 64]),
                    ALU.mult, [xs_tok, sm], [xcd])
            if own:
                CP("act", Sbf[:], Sst[:], [Sst], [Sbf])
                prs = []
                for g in range(2):
                    TT_("dve", TAv[g].rearrange("p (h l) -> p h l", h=8), tri[:].unsqueeze(1).to_broadcast([128, 8, 128]),
                        aa[:, g * 8:(g + 1) * 8].unsqueeze(2).to_broadcast([128, 8, 128]), ALU.mult, [tri, sm], TAb[g])
                TT_("pool", xc[:].rearrange("p (h d) -> p h d", h=16), xsj, dtt.unsqueeze(2).to_broadcast([128, 16, 64]),
                    ALU.mult, [xs_tok, sm] + TAb[1], [xc])
                TT_("pool", y2[:].rearrange("p (h d) -> p h d", h=16), xsj, dsk[:].unsqueeze(2).to_broadcast([128, 16, 64]),
                    ALU.mult, [xs_tok, dsk], [y2])
                TT_("pool", xcd[:].rearrange("p (h d) -> p h d", h=16), xsj, wv.unsqueeze(2).to_broadcast([128, 16, 64]),
                    ALU.mult, [xs_tok, sm], [xcd])
                for g in range(2):
                    b0, b1, pview = SPAIRS[g]
                    MM(b0, b0[:, 0:512], Ust[:], TAv[g][:, 0:512], [Ust] + TAb[g], True)
                    MM(b1, b1[:, 0:512], Ust[:], TAv[g][:, 512:1024], [Ust] + TAb[g], True)
                    ACT(pview, pview, AF.Exp, [b0, b1], [b0, b1])
                    prs.append((b0, b1, pview))
                bcb = BK[5]
                for g in range(2):
                    MM(bcb, bcb[:, g * 128:(g + 1) * 128], BT[:, g, j * 128:(j + 1) * 128], CT[:, g, j * 128:(j + 1) * 128],
                       [BT, CT], g == 0)
                TT_("dve", cbm[:].rearrange("p (g l) -> p g l", g=2), bcb[:, 0:256].rearrange("p (g l) -> p g l", g=2),
                    tri[:].unsqueeze(1).to_broadcast([128, 2, 128]), ALU.mult, [bcb, tri], [cbm])
                for half in range(2):
                    bz = BK[4 + 3 * half]
                    for k in range(8):
                        MM(bz, bz[:, 0:512], hT_all.t[:, k, ctok], Wz[:, k, half * 512:(half + 1) * 512], [hg, Wz], k == 0, k == 7)
                    ACT(zs[:, half * 512:(half + 1) * 512], bz[:, 0:512], AF.Silu, [bz], [zs])
                for g in range(2):
                    b0, b1, pview = prs[g]
                    TT_("dve", MTb[:], pview.rearrange("p (h l) -> p h l", h=8),
                        cbm[:, g * 128:(g + 1) * 128].unsqueeze(1).to_broadcast([128, 8, 128]), ALU.mult, [b0, b1, cbm], [MTb])
                    byd = BK[6]
                    for hh in range(8):
                        h = g * 8 + hh
                        MM(byd, byd[:, hh * 64:(hh + 1) * 64], MTb[:, hh, :], xc[:, h * 64:(h + 1) * 64], [MTb, xc], hh == 0)
                    if g == 0:
                        while pend_fin:
                            pend_fin.pop(0)()
                    boff = BK[4 + 3 * g]
                    MM(boff, boff[:, 0:512], CT[:, g, j * 128:(j + 1) * 128], Sbf[:, g * 512:(g + 1) * 512], [CT, Sbf], True)
                    ysl = slice(g * 512, (g + 1) * 512)
                    TT_("dve", y1[:, ysl].rearrange("p (h d) -> p h d", h=8), boff[:, 0:512].rearrange("p (h d) -> p h d", h=8),
                        dout[:, g * 8:(g + 1) * 8].unsqueeze(2).to_broadcast([128, 8, 64]), ALU.mult, [boff, sm], [y1])
                    TT_("dve", y1[:, ysl], byd[:, 0:512], y1[:, ysl], ALU.add, [byd, y1], [y1])
                TT_("dve", y2[:], y2[:], y1[:], ALU.add, [y2, y1], [y2])
                TT_("dve", y2[:], y2[:], zs[:], ALU.mult, [y2, zs], [y2])
                MEMSET("pool", sq[:, 0:1], 0.0, [sq])
                ACT(y1[:], y2[:], AF.Square, [y2], [y1, sq], accum=sq[:, 0:1])
                ACT(sq[:, 1:2], sq[:, 0:1], AF.Sqrt, [sq], [sq], bias=EPS, scale=1.0 / 1024.0)
                P.op("dve", lambda e: e.reciprocal(out=sq[:, 2:3], in_=sq[:, 1:2]), reads=[sq], writes=[sq])
                TS("dve", y1[:], y2[:], sq[:, 2:3], None, ALU.mult, None, [y2, sq], [y1])

                def fin(ob=ch - 16):
                    for half in range(2):
                        bt_ = BK[4 + 3 * half]
                        for jj in range(4):
                            kt = half * 4 + jj
                            TR(bt_, bt_[:, jj * 128:(jj + 1) * 128], y1[:, kt * 128:(kt + 1) * 128], identF[:], [y1, identF])
                        for jj in range(4):
                            kt = half * 4 + jj
                            ACT(ymT_all.t[:, kt, ob * 128:(ob + 1) * 128], bt_[:, jj * 128:(jj + 1) * 128], AF.Identity,
                                [bt_, ssw], [ymS[ob]], scale=ssw[:, kt:kt + 1])
                pend_fin.append(fin)
            for g in range(2):
                bs = nb()
                MM(bs, bs[:, 0:512], Btok[:, j, g * 128:(g + 1) * 128], xcd[:, g * 512:(g + 1) * 512], [Btok, xcd], True)
                ssl = slice(g * 512, (g + 1) * 512)
                TT_("pool", Sst[:, ssl].rearrange("p (h d) -> p h d", h=8), Sst[:, ssl].rearrange("p (h d) -> p h d", h=8),
                    dch[:, g * 8:(g + 1) * 8].unsqueeze(2).to_broadcast([128, 8, 64]), ALU.mult, [Sst, sm], [Sst])
                TT_("dve", Sst[:, ssl], bs[:, 0:512], Sst[:, ssl], ALU.add, [bs, Sst], [Sst])
    while pend_fin:
        pend_fin.pop(0)()

    A.top = mark0
    markF = A.top
    DMA("sync", ymsd.rearrange("p (k t) -> p k t", k=8), ymT_all.t[:, 0:8, :], "ymsp", rd=ymS)
    KaS = [[sbw("Ka0%d" % i, 2048, BF16) for i in range(2)],
           [alias("Ka1%d" % i, YM0 + i * 2048, YM0 + (i + 1) * 2048, BF16) for i in range(2)]]
    VaS = [sbw("Va0", 4096, BF16, "p (b h d) -> p b h d", b=32, h=2),
           alias("Va1", YM0 + 4096, YM0 + 8192, BF16, "p (b h d) -> p b h d", b=32, h=2)]
    QaS = [[sbw("Qa0%d" % i, 1024, BF16) for i in range(2)], None]
    Wqkv = sbw("Wqkv", 1536, BF16, "p (k n) -> p k n", k=8)
    pT = [sbw("pT%d" % i, 512, BF16) for i in range(3)]
    vts = [sbw("vts%d" % i, 256, BF16) for i in range(2)]
    print("[kernel] attention words free:", NW - A.top, flush=True)
    markF2 = A.top
    Wf = sbw("Wf", 64, BF16, "p (k n) -> p k n", k=8)
    lfa = sbw("lfa", 512)
    cumT = sbw("cumT", 512)
    r1 = sbw("r1", 512)
    c3s = sbw("c3s", 768, BF16, "p (r t) -> p r t", r=3)
    carry = sbw("carry", 1)

    wload(Wf, lambda k: Wf[:, k, :], 0, CF, 16, 8, "Wf")
    MEMSET("dve", carry[:], 0.0, [carry])
    MEMSET("pool", sscol[:], 0.0, [sscol])
    bf = nb()
    for blk in range(32):
        for k in range(8):
            MM(bf, bf[:, blk * 16:(blk + 1) * 16], hT_all.t[:, k, blk * 128:(blk + 1) * 128], Wf[:, k, :], [hTg[blk // 4], Wf],
               (blk == 0 and k == 0), k == 7)
    TT_("dve", lfa[:].rearrange("p (b h) -> p b h", b=32), bf[:, 0:512].rearrange("p (b h) -> p b h", b=32),
        fbb[:].unsqueeze(1).to_broadcast([128, 32, 16]), ALU.add, [bf, fbb], [lfa])
    ACT(lfa[:], lfa[:], AF.Exp, [lfa], [lfa], scale=-1.0)
    ACT(lfa[:], lfa[:], AF.Ln, [lfa], [lfa], bias=1.0)
    TS("dve", lfa[:], lfa[:], -1.0, None, ALU.mult, None, [lfa], [lfa])
    for q4 in range(8):
        bc = nb()
        for bq in range(4):
            blk = q4 * 4 + bq
            MM(bc, bc[0:16, bq * 128:(bq + 1) * 128], lfa[:, blk * 16:(blk + 1) * 16], tri[:], [lfa, tri], bq == 0)
        for bq in range(4):
            TS("dve", cumT[0:16, bq * 128:(bq + 1) * 128], bc[0:16, bq * 128:(bq + 1) * 128], carry[0:16, 0:1], None, ALU.add, None,
               [bc, carry], [cumT])
            CP("dve", carry[0:16, 0:1], cumT[0:16, bq * 128 + 127:bq * 128 + 128], [cumT], [carry])
        CP("dve", c3s[0:16, 0, :], cumT[0:16, :], [cumT], [c3s])
        TT_("dve", r1[0:16, :], cumT[0:16, :], c3s[0:16, 0, :], ALU.subtract, [cumT, c3s], [r1])
        CP("dve", c3s[0:16, 1, :], r1[0:16, :], [r1], [c3s])
        TT_("dve", r1[0:16, :], r1[0:16, :], c3s[0:16, 1, :], ALU.subtract, [r1, c3s], [r1])
        CP("dve", c3s[0:16, 2, :], r1[0:16, :], [r1], [c3s])
        DMA("sync", c3d.rearrange("h (r t) -> h r t", r=3)[:, :, q4 * 512:(q4 + 1) * 512], c3s[0:16, :, :], "c3w", rd=[c3s])
    c3tok = Buf("c3dram", None)
    c3tok.lw = ("s", "c3w", P.semvals["c3w"])
    c3v = c3d.rearrange("h (r t) -> h r t", r=3)
    A.top = markF2
    rbc = sbw("rbc", 512)
    yv = [sbw("yv%d" % i, 512) for i in range(2)]
    ysq = sbw("ysq", 512)
    QaS[1] = [sbw("Qa1%d" % i, 1024, BF16) for i in range(2)]

    for Va in VaS:
        MEMSET("pool", Va[:, :, :, 64:128], 1.0, [Va])
        TS("pool", Va[:, 0:16, :, 64:128], Va[:, 0:16, :, 64:128], valid, None, ALU.mult, None, [Va, flg], [Va])
    for sl in range(2):
        for i in range(2):
            MEMSET("pool", KaS[sl][i][64:128, :], 0.0, [KaS[sl][i]])
            MEMSET("pool", KaS[sl][i][64:70, :], 1.0, [KaS[sl][i]])
            MEMSET("pool", QaS[sl][i][64:128, :], 0.0, [QaS[sl][i]])
            MEMSET("pool", QaS[sl][i][64:70, :], -1.0, [QaS[sl][i]])

    pcnt = [0]
    ecnt = [0]
    fcnt = [0]
    deferred = []
    filler = []
    tcnt = [0]

    def tick():
        for d_ in deferred:
            d_[0] -= 1
        while deferred and deferred[0][0] <= 0:
            deferred.pop(0)[1]()
        tcnt[0] += 1
        if filler and tcnt[0] % 5 == 0:
            filler.pop(0)()

    def flush():
        while deferred:
            deferred.pop(0)[1]()

    def fbank():
        b = BK[4 + 3 * (fcnt[0] % 2)]
        fcnt[0] += 1
        return b

    def proj_units(hp):
        sl = hp % 2
        Ka, Qa, Va, wq = KaS[sl], QaS[sl], VaS[sl], Wqkv
        units = []

        def u_load():
            for i, c0 in enumerate((CQ, CK, CV)):
                wload(wq, (lambda i: (lambda k: wq[:, k, i * 128:(i + 1) * 128]))(i), 0, c0 + hp * 128, 128, 8, "Wq")
            if hp == 0:
                for r0 in range(0, 1024, 128):
                    DMA("pool", w1b[r0:r0 + 128, :], D["w_ff_in"][r0:r0 + 128, :], "precast")
                for r0 in range(0, 2048, 128):
                    DMA("pool", wob[r0:r0 + 128, :], D["w_out"][r0:r0 + 128, :], "precast")
                for r0 in range(0, 4096, 128):
                    DMA("pool", w2b[r0:r0 + 128, :], D["w_ff_out"][r0:r0 + 128, :], "precast")
            for hh in range(2):
                h = hp * 2 + hh
                DMA("sync", Ka[hh][67:70, :], c3v[h, :, :], "ka%d%d" % (sl, hh), rd=[c3tok], wr=[Ka[hh]])
                DMA("sync", Qa[hh][64:67, :], c3v[h, :, TP:TT], "qa%d%d" % (sl, hh), rd=[c3tok], wr=[Qa[hh]])
        units.append(u_load)

        def u_k(G):
            bk_ = fbank()
            for k in range(8):
                MM(bk_, bk_[:, 0:512], wq[:, k, 128:256], hT_all.t[:, k, G * 512:(G + 1) * 512], [wq, hTg[G]], k == 0, k == 7)
            CP("dve", Ka[0][0:64, G * 512:(G + 1) * 512], bk_[0:64, 0:512], [bk_], [Ka[0]])
            CP("dve", Ka[1][0:64, G * 512:(G + 1) * 512], bk_[64:128, 0:512], [bk_], [Ka[1]])

        def u_q(G):
            bq_ = fbank()
            for k in range(8):
                MM(bq_, bq_[:, 0:512], wq[:, k, 0:128], hT_all.t[:, k, TP + G * 512:TP + (G + 1) * 512], [wq, hTg[4 + G]],
                   k == 0, k == 7)
            TS("dve", Qa[0][0:64, G * 512:(G + 1) * 512], bq_[0:64, 0:512], 0.125, None, ALU.mult, None, [bq_], [Qa[0]])
            TS("dve", Qa[1][0:64, G * 512:(G + 1) * 512], bq_[64:128, 0:512], 0.125, None, ALU.mult, None, [bq_], [Qa[1]])

        def u_v(G):
            bv = fbank()
            for k in range(8):
                MM(bv, bv[:, 0:512], wq[:, k, 256:384], hT_all.t[:, k, G * 512:(G + 1) * 512], [wq, hTg[G]], k == 0, k == 7)
            vt = vts[G % 2]
            CP("dve", vt[:, 0:512], bv[:, 0:512], [bv], [vt])
            b2 = fbank()
            for j in range(4):
                TR(b2, b2.tb[:, j * 128:(j + 1) * 128], vt[:, j * 128:(j + 1) * 128], identB[:], [vt, identB])
            src = b2.tb[:, 0:512].rearrange("p (b h d) -> p b h d", b=4, h=2)
            if G < 4:
                TS("dve", Va[:, G * 4:(G + 1) * 4, :, 0:64], src, valid, None, ALU.mult, None, [b2, flg], [Va])
            else:
                CP("dve", Va[:, G * 4:(G + 1) * 4, :, 0:64], src, [b2], [Va])
        for G in range(8):
            units.append((lambda G: (lambda: u_k(G)))(G))
        for G in range(4):
            units.append((lambda G: (lambda: u_q(G)))(G))
        for G in range(8):
            units.append((lambda G: (lambda: u_v(G)))(G))
        return units

    PAIRS = [(BK[0], BK[1], PS[:, 0:1024]), (BK[2], BK[3], PS[:, 1024:2048])]
    for u in proj_units(0):
        u()
    for hp in range(NH // 2):
        sl = hp % 2
        Va = VaS[sl]
        if hp + 1 < NH // 2:
            filler.extend(proj_units(hp + 1))
        for hh in range(2):
            h = hp * 2 + hh
            ka, qa = KaS[sl][hh], QaS[sl][hh]
            for G in range(4):
                po = BK[5 + ecnt[0] % 2]
                npre = 16 + 4 * G
                nkb = npre + 4
                steps = [([kb, kb + 1], 0, 512) for kb in range(0, npre, 2)] + \
                        [([npre + r], r, (4 - r) * 128) for r in range(4)]
                pend = None
                for si in range(len(steps) + 1):
                    cur = None
                    if si < len(steps):
                        kbs, q0, ncol = steps[si]
                        pt = pT[pcnt[0] % 3]
                        if len(kbs) == 2:
                            b0, b1, pview = PAIRS[pcnt[0] % 2]
                            for bi, kb in zip((b0, b1), kbs):
                                MM(bi, bi[:, 0:512], ka[:, kb * 128:(kb + 1) * 128], qa[:, G * 512:(G + 1) * 512],
                                   [ka, qa], True, True)
                            ACT(pt[:, 0:1024], pview, AF.Exp, [b0, b1], [pt])
                        else:
                            kb = kbs[0]
                            bs_ = BK[4 + 3 * (q0 % 2)]
                            MM(bs_, bs_[:, 0:ncol], ka[:, kb * 128:(kb + 1) * 128],
                               qa[:, G * 512 + q0 * 128:(G + 1) * 512], [ka, qa], True, False)
                            MM(bs_, bs_[:, 0:128], identB[:], maskB[:], [identB, maskB], False, True)
                            ACT(pt[:, 0:ncol], bs_[:, 0:ncol], AF.Exp, [bs_], [pt])
                        pcnt[0] += 1
                        cur = (kbs, q0, ncol, pt)
                    if pend is not None:
                        kbs_, q0_, ncol_, pt_ = pend
                        for i_, kb_ in enumerate(kbs_):
                            MM(po, po[:, q0_ * 128:512], Va[:, kb_, hh, :], pt_[:, i_ * 512:i_ * 512 + ncol_], [Va, pt_],
                               kb_ == 0, kb_ == nkb - 1)
                    pend = cur
                    tick()
                P.op("dve", (lambda po_: (lambda e: e.reciprocal(out=rbc[0:64, :], in_=po_[64:128, 0:512])))(po),
                     reads=[po], writes=[rbc])
                yv_ = yv[ecnt[0] % 2]
                TT_("dve", yv_[0:64, :], po[0:64, 0:512], rbc[0:64, :], ALU.mult, [po, rbc], [yv_])
                pr = (h % 2) * 64
                TT_("pool", ysq[0:64, :], yv_[0:64, :], yv_[0:64, :], ALU.mult, [yv_], [ysq])
                TS("pool", ymT_all.t[pr:pr + 64, 8 + h // 2, G * 512:(G + 1) * 512], yv_[0:64, :], atwh[0:64, h:h + 1], None,
                   ALU.mult, None, [yv_, atwh], [ymA[G * 4 + j] for j in range(4)])

                def ep2(G=G, e_=ecnt[0]):
                    bss = BK[4 + 3 * (e_ % 2)]
                    for j4 in range(4):
                        MM(bss, bss[:, j4:j4 + 1], ysq[0:64, j4 * 128:(j4 + 1) * 128], onesF[0:64, 0:1], [onesF, ysq], j4 == 0)
                    TT_("dve", sscol[:, G * 4:(G + 1) * 4], bss[:, 0:4], sscol[:, G * 4:(G + 1) * 4], ALU.add,
                        [bss, sscol], [sscol])

                deferred.append([8, ep2])
                ecnt[0] += 1
        while filler:
            filler.pop(0)()
    flush()

    A.top = markF
    g1b = alias("g1b", 0, 1024)
    g2b = alias("g2b", 1024, 2048)
    lnb = alias("lnb", 2048, 6144, F32, "p (v n) -> p v n", v=4)
    h2Tb = alias("h2Tb", 6144, 14336, BF16, "p (k t) -> p k t", k=8)
    xr_ = [sbw("xr%d" % i, 1024) for i in range(2)]
    t1s = [sbw("t1_%d" % i, 1024) for i in range(2)]
    t2 = sbw("t2", 1024)
    sts = [sbw("st%d" % i, 16) for i in range(2)]
    t1 = t1s[0]
    st_ = sts[0]
    markGH = A.top
    Wo = sbw("Wo", 8192, BF16, "p (k n) -> p k n", k=16)
    dg = [sbw("dg%d" % i, 128) for i in range(2)]

    ymS = [register(Buf("ymS2_%d" % b, ymT_all.t[:, 0:8, b * 128:(b + 1) * 128]), YM0, YM0 + 8192) for b in range(16)]
    spill = Buf("ymsd", None)
    spill.lw = ("s", "ymsp", P.semvals["ymsp"])
    DMA("sync", ymT_all.t[:, 0:8, :], ymsd.rearrange("p (k t) -> p k t", k=8), "ymrl", rd=[spill], wr=ymS)
    pctok = Buf("precast", None)
    pctok.lw = ("s", "precast", P.semvals["precast"])
    for v, nme in enumerate(["ln1_g", "ln1_b", "ln2_g", "ln2_b"]):
        DMA("sync", lnb[:, v, :], D[nme][0:1, :].partition_broadcast(128), "lnb%d" % v, wr=[lnb])
    for k0 in (0, 8):
        DMA("sync", Wo[:, k0:k0 + 8, :], wob[k0 * 128:(k0 + 8) * 128, :].rearrange("(k p) n -> p k n", p=128), "Wo",
            rd=[pctok], wr=[Wo])

    def col_bcast(c0, dst):
        for half in range(2):
            bank = nb()
            for j in range(4):
                k = half * 4 + j
                d_ = dg[k % 2]
                TS("dve", d_[:], identF[:], modp3[:, c0 + k:c0 + k + 1], None, ALU.mult, None, [identF, modp3], [d_])
                MM(bank, bank[:, j * 128:(j + 1) * 128], onesF[:], d_[:], [onesF, d_], j == 0)
            CP("act", dst[:, half * 512:(half + 1) * 512], bank[:, 0:512], [bank], [dst])

    col_bcast(0, g1b)
    col_bcast(8, g2b)

    def layer_norm(src, dst, gi, st_):
        MEMSET("dve", st_[:, 0:2], 0.0, [st_])
        ACT(t2[:], src[:], AF.Identity, [src], [st_], accum=st_[:, 0:1])
        ACT(t2[:], src[:], AF.Square, [src], [st_], accum=st_[:, 1:2])
        TS("dve", st_[:, 2:3], st_[:, 0:1], 1.0 / 1024.0, None, ALU.mult, None, [st_], [st_])
        TT_("dve", st_[:, 3:4], st_[:, 2:3], st_[:, 2:3], ALU.mult, [st_], [st_])
        STT(st_[:, 4:5], st_[:, 1:2], 1.0 / 1024.0, st_[:, 3:4], ALU.mult, ALU.subtract, [st_], [st_])
        ACT(st_[:, 5:6], st_[:, 4:5], AF.Sqrt, [st_], [st_], bias=EPS)
        P.op("dve", lambda e: e.reciprocal(out=st_[:, 6:7], in_=st_[:, 5:6]), reads=[st_], writes=[st_])
        TS("dve", dst[:], src[:], st_[:, 2:3], st_[:, 6:7], ALU.subtract, ALU.mult, [src, st_], [dst])
        TT_("dve", dst[:], dst[:], lnb[:, gi, :], ALU.mult, [dst, lnb], [dst])
        TT_("dve", dst[:], dst[:], lnb[:, gi + 1, :], ALU.add, [dst, lnb], [dst])

    x1toks = [Buf("x1dram%d" % i, None) for i in range(2)]
    pend_g = []
    for ob in range(16):
        xt = xr_[ob % 2]
        t1 = t1s[ob % 2]
        st_ = sts[ob % 2]
        DMA("sync", xt[:], D["xo"][ob * 128:(ob + 1) * 128, :], "xr%d" % (ob % 2), wr=[xt])
        ACT(st_[:, 9:10], sscol[:, ob:ob + 1], AF.Sqrt, [sscol], [st_], bias=EPS, scale=1.0 / 1024.0)
        P.op("dve", lambda e, st_=st_: e.reciprocal(out=st_[:, 10:11], in_=st_[:, 9:10]), reads=[st_], writes=[st_])
        for half in range(2):
            b1 = nb()
            for kt in range(8):
                MM(b1, b1[:, 0:512], ymT_all.t[:, kt, ob * 128:(ob + 1) * 128], Wo[:, kt, half * 512:(half + 1) * 512],
                   [ymS[ob], Wo], kt == 0, kt == 7)
            b2 = nb()
            for kt in range(8, 16):
                MM(b2, b2[:, 0:512], ymT_all.t[:, kt, ob * 128:(ob + 1) * 128], Wo[:, kt, half * 512:(half + 1) * 512],
                   [ymA[ob], Wo], kt == 8, kt == 15)
            hs = slice(half * 512, (half + 1) * 512)
            CP("act", t1[:, hs], b1[:, 0:512], [b1], [t1])
            STT(t1[:, hs], b2[:, 0:512], st_[:, 10:11], t1[:, hs], ALU.mult, ALU.add, [b2, st_, t1], [t1])
        while pend_g:
            pend_g.pop(0)()
        TT_("dve", t1[:], t1[:], g1b[:], ALU.mult, [t1, g1b], [t1])
        STT(t1[:], xt[:], ALPHA, t1[:], ALU.mult, ALU.add, [xt, t1], [t1])
        layer_norm(t1, xt, 0, st_)
        DMA("sync", x1s[ob * 128:(ob + 1) * 128, :], xt[:], "x1w%d" % (ob % 2), rd=[xt])

        def trg(xt=xt, ob=ob):
            for half in range(2):
                bt_ = nb()
                for jj in range(4):
                    k = half * 4 + jj
                    TR(bt_, bt_[:, jj * 128:(jj + 1) * 128], xt[:, k * 128:(k + 1) * 128], identF[:], [xt, identF])
                for jj in range(4):
                    k = half * 4 + jj
                    if jj % 2 == 0:
                        ACT(h2Tb[:, k, ob * 128:(ob + 1) * 128], bt_[:, jj * 128:(jj + 1) * 128], AF.Identity, [bt_, modp2],
                            [h2Tb], bias=modp2[:, k:k + 1], scale=modp2[:, 8 + k:9 + k])
                    else:
                        TS("dve", h2Tb[:, k, ob * 128:(ob + 1) * 128], bt_[:, jj * 128:(jj + 1) * 128], modp2[:, 8 + k:9 + k],
                           modp2[:, k:k + 1], ALU.mult, ALU.add, [bt_, modp2], [h2Tb])
        pend_g.append(trg)
    while pend_g:
        pend_g.pop(0)()
    for i in range(2):
        x1toks[i].lw = ("s", "x1w%d" % i, P.semvals["x1w%d" % i])

    W2 = alias("W2", YM0, YM0 + 16384, BF16, "p (f n) -> p f n", f=32)
    for f0 in range(0, 32, 8):
        DMA("sync", W2[:, f0:f0 + 8, :], w2b[f0 * 128:(f0 + 8) * 128, :].rearrange("(f p) n -> p f n", p=128), "W2",
            rd=[pctok], wr=[W2])
    A.top = markGH
    a1T = sbw("a1T", 8192, BF16, "p (f t) -> p f t", f=32)
    W1t = [sbw("W1t%d" % i, 2048, BF16, "p (k n) -> p k n", k=8) for i in range(2)]
    rl = [sbw("rl%d" % i, 512) for i in range(2)]
    wcnt = [0]
    for G in range(4):
        for c4 in range(8):
            w1 = W1t[wcnt[0] % len(W1t)]
            DMA("sync", w1[:, :, :], w1b[:, c4 * 512:(c4 + 1) * 512].rearrange("(k p) n -> p k n", p=128),
                "W1t%d" % (wcnt[0] % len(W1t)), rd=[pctok], wr=[w1])
            wcnt[0] += 1
            for f4 in range(4):
                f = c4 * 4 + f4
                bf_ = nb()
                for k in range(8):
                    MM(bf_, bf_[:, 0:512], w1[:, k, f4 * 128:(f4 + 1) * 128], h2Tb[:, k, G * 512:(G + 1) * 512], [w1, h2Tb],
                       k == 0, k == 7)
                r_ = rl[f % 2]
                ACT(r_[:], bf_[:, 0:512], AF.Relu, [bf_], [r_])
                TT_("dve", a1T[:, f, :], r_[:], r_[:], ALU.mult, [r_], [a1T])
        for tb in range(4):
            ob = G * 4 + tb
            xt = xr_[ob % 2]
            t1 = t1s[ob % 2]
            st_ = sts[ob % 2]
            DMA("sync", xt[:], x1s[ob * 128:(ob + 1) * 128, :], "xr%d" % (ob % 2), rd=x1toks, wr=[xt])
            for half in range(2):
                bo = nb()
                for f in range(32):
                    MM(bo, bo[:, 0:512], a1T[:, f, tb * 128:(tb + 1) * 128], W2[:, f, half * 512:(half + 1) * 512], [a1T, W2],
                       f == 0, f == 31)
                hs = slice(half * 512, (half + 1) * 512)
                TT_("dve", t1[:, hs], bo[:, 0:512], g2b[:, hs], ALU.mult, [bo, g2b], [t1])
            STT(t1[:], xt[:], ALPHA, t1[:], ALU.mult, ALU.add, [xt, t1], [t1])
            layer_norm(t1, xt, 2, st_)
            DMA("sync", out_d[ob * 128:(ob + 1) * 128, :], xt[:], "ow%d" % (ob % 2), rd=[xt])

    print("[kernel] ops:", {k: len(v) for k, v in P.ops.items()}, "cnt:", P.cnt, "nsem:", len(P.semkeys), flush=True)
    P.emit()
    st.close()
    return nc


_NC = [None]


def kernel(**inputs):
    x = np.ascontiguousarray(np.asarray(inputs["x"], dtype=np.float32))
    c = np.asarray(inputs["c"], dtype=np.float32)
    if _NC[0] is None:
        _NC[0] = build()
    nc = _NC[0]
    wmap = {}
    for n in WEIGHT_NAMES:
        wmap[n] = np.ascontiguousarray(np.asarray(inputs[n], dtype=np.float32).reshape(WEIGHT_SHAPES[n]))
    in_maps = []
    for core in range(8):
        b, h = core // 2, core % 2
        m = dict(wmap)
        m["xo"] = np.ascontiguousarray(x[b, h * TO:(h + 1) * TO])
        m["xp"] = np.ascontiguousarray(x[b, 0:TP])
        m["c"] = np.ascontiguousarray(c[b:b + 1])
        fl = np.zeros((128, 2), np.float32)
        fl[:, 0] = float(h)
        m["flags"] = fl
        in_maps.append(m)
    res = run_bass_kernel_spmd(nc, in_maps, core_ids=list(range(8)))
    out = np.empty((4, 4096, DM), np.float32)
    for core in range(8):
        b, h = core // 2, core % 2
        out[b, h * TO:(h + 1) * TO] = res.results[core]["out"]
    return out
```

```python
import contextlib
import numpy as np
import concourse.bass as bass
import concourse.mybir as mybir
from concourse.bass_utils import run_bass_kernel_spmd

F32 = mybir.dt.float32
BF16 = mybir.dt.bfloat16
AF = mybir.ActivationFunctionType
ALU = mybir.AluOpType

COMPUTE = ("pe", "act", "dve", "pool")
SELF_SYNC = ("act", "dve", "pool")
QUEUES = COMPUTE + ("sync",)


class Buf:
    __slots__ = ("name", "t", "tb", "lw", "rd")

    def __init__(self, name, t, tb=None):
        self.name = name
        self.t = t
        self.tb = tb
        self.lw = None
        self.rd = []

    def __getitem__(self, k):
        return self.t[k]


class Prog:
    def __init__(self, nc):
        self.nc = nc
        self.ops = {e: [] for e in QUEUES}
        self.cnt = {e: 0 for e in COMPUTE}
        self.waited = {e: {} for e in QUEUES}
        self.semvals = {}
        self.semkeys = []

    def _need(self, eng, waits, tok):
        if tok is None:
            return
        if tok[0] == "e":
            _, e2, idx = tok
            if e2 == eng and eng not in SELF_SYNC:
                return
            key = ("e", e2)
            val = idx
        else:
            _, sk, val = tok
            key = ("s", sk)
        if self.waited[eng].get(key, 0) >= val:
            return
        if waits.get(key, 0) < val:
            waits[key] = val

    def _collect(self, eng, reads, writes):
        waits = {}
        for b in reads:
            self._need(eng, waits, b.lw)
        for b in writes:
            self._need(eng, waits, b.lw)
            for tok in b.rd:
                self._need(eng, waits, tok)
        for k, v in waits.items():
            self.waited[eng][k] = v
        return waits

    def _mark(self, tok, reads, writes):
        for b in reads:
            b.rd.append(tok)
            if len(b.rd) > 12:
                best = {}
                for t in b.rd:
                    k = (t[0], t[1])
                    if k not in best or best[k][2] < t[2]:
                        best[k] = t
                b.rd = list(best.values())
        for b in writes:
            b.lw = tok
            b.rd = []

    def op(self, eng, fn, reads=(), writes=()):
        waits = self._collect(eng, reads, writes)
        self.cnt[eng] += 1
        tok = ("e", eng, self.cnt[eng])
        self._mark(tok, reads, writes)
        self.ops[eng].append((waits, fn, None))
        return tok

    def dma(self, q, fn, semkey, reads=(), writes=()):
        waits = self._collect(q, reads, writes)
        if semkey not in self.semvals:
            self.semvals[semkey] = 0
            self.semkeys.append(semkey)
        self.semvals[semkey] += 16
        tok = ("s", semkey, self.semvals[semkey])
        self._mark(tok, reads, writes)
        self.ops[q].append((waits, fn, semkey))
        return tok

    def emit(self):
        nc = self.nc
        with contextlib.ExitStack() as st:
            esem = {e: st.enter_context(nc.semaphore("se_" + e)) for e in COMPUTE}
            ssem = {k: st.enter_context(nc.semaphore("sd_%d" % i)) for i, k in enumerate(self.semkeys)}
            block = st.enter_context(nc.Block())

            def semof(key):
                return esem[key[1]] if key[0] == "e" else ssem[key[1]]

            def replay(engobj, name, extra=None):
                for waits, fn, semkey in self.ops[name]:
                    for key, val in waits.items():
                        engobj.wait_ge(semof(key), val)
                    ins = fn(engobj)
                    if semkey is not None:
                        ins.then_inc(ssem[semkey], 16)
                    else:
                        ins.then_inc(esem[name], 1)
                if extra:
                    extra(engobj)

            def final(engobj):
                for k in self.semkeys:
                    engobj.wait_ge(ssem[k], self.semvals[k])
                for e in COMPUTE:
                    if self.cnt[e]:
                        engobj.wait_ge(esem[e], self.cnt[e])

            @block.tensor
            def _(e):
                replay(e, "pe")

            @block.vector
            def _(e):
                replay(e, "dve")

            @block.gpsimd
            def _(e):
                replay(e, "pool")

            @block.scalar
            def _(e):
                replay(e, "act")

            @block.sync
            def _(e):
                replay(e, "sync", extra=final)


DM = 1024
TO = 2048
TP = 2048
TT = TO + TP
NH = 16
CZ, CX, CB, CC, CDT, CQ, CK, CV, CF = 0, 1024, 2048, 2304, 2560, 2576, 3600, 4624, 5648
ALPHA = 2.0 ** 0.25
EPS = 1e-5
NW = 52600
NEG = -30000.0

WEIGHT_NAMES = ["w_ada", "b_ada", "w_in", "conv_w", "conv_b", "dt_bias", "a_log", "d_skip", "ssm_norm_w",
                "f_bias", "attn_norm_w", "w_out", "ln1_g", "ln1_b", "w_ff_in", "w_ff_out", "ln2_g", "ln2_b"]
WEIGHT_SHAPES = {"w_ada": [1024, 6144], "b_ada": [1, 6144], "w_in": [1024, 5664], "conv_w": [4, 1536],
                 "conv_b": [1, 1536], "dt_bias": [1, 16], "a_log": [1, 16], "d_skip": [1, 16],
                 "ssm_norm_w": [1, 1024], "f_bias": [1, 16], "attn_norm_w": [1, 1024], "w_out": [2048, 1024],
                 "ln1_g": [1, 1024], "ln1_b": [1, 1024], "w_ff_in": [1024, 4096], "w_ff_out": [4096, 1024],
                 "ln2_g": [1, 1024], "ln2_b": [1, 1024]}


def build(stop_after=99):
    nc = bass.Bass("TRN2", target_bir_lowering=False)
    D = {}
    for n, shp in [("xo", [TO, DM]), ("xp", [TP, DM]), ("c", [1, DM]), ("flags", [128, 2])]:
        D[n] = nc.dram_tensor(n, shp, F32, kind="ExternalInput").ap()
    for n in WEIGHT_NAMES:
        D[n] = nc.dram_tensor(n, WEIGHT_SHAPES[n], F32, kind="ExternalInput").ap()
    out_d = nc.dram_tensor("out", [TO, DM], F32, kind="ExternalOutput").ap()
    x1s = nc.dram_tensor("x1s", [TO, DM], F32, kind="Internal").ap()
    c3d = nc.dram_tensor("c3d", [16, 3 * TT], BF16, kind="Internal").ap()
    ymsd = nc.dram_tensor("ymsd", [128, 8 * 2048], BF16, kind="Internal").ap()
    w1b = nc.dram_tensor("w1b", [1024, 4096], BF16, kind="Internal").ap()
    w2b = nc.dram_tensor("w2b", [4096, 1024], BF16, kind="Internal").ap()
    wob = nc.dram_tensor("wob", [2048, 1024], BF16, kind="Internal").ap()
    w_in = D["w_in"]

    st = contextlib.ExitStack()
    S = st.enter_context(nc.sbuf_tensor("S", [128, NW], F32))
    PS = st.enter_context(nc.psum_tensor("PS", [128, 4096], F32))
    P = Prog(nc)

    class A:
        top = 0

    regs = []

    def register(buf, lo, hi):
        for (l2, h2, b2) in regs:
            if l2 < hi and lo < h2 and b2 is not buf:
                if b2.lw is not None:
                    buf.rd.append(b2.lw)
                buf.rd.extend(b2.rd)
        regs.append((lo, hi, buf))
        return buf

    def alias(name, lo, hi, dt=F32, pat=None, **kw):
        assert hi <= NW, (name, hi)
        v = S[:, lo:hi]
        if dt == BF16:
            v = v.bitcast(BF16)
        if pat:
            v = v.rearrange(pat, **kw)
        return register(Buf(name, v), lo, hi)

    def sbw(name, words, dt=F32, pat=None, **kw):
        off = A.top
        A.top += words
        return alias(name, off, off + words, dt, pat, **kw)

    BK = [Buf("bk%d" % i, PS[:, 512 * i:512 * (i + 1)], PS[:, 512 * i:512 * (i + 1)].bitcast(BF16)) for i in range(8)]
    bkc = [0]

    ROT = [0, 1, 2, 3, 4, 7]

    def nb():
        b = BK[ROT[bkc[0] % len(ROT)]]
        bkc[0] += 1
        return b

    def MM(bank, out, lhsT, rhs, rd, start, stop=True):
        P.op("pe", lambda e: e.matmul(out, lhsT=lhsT, rhs=rhs, start=start, stop=stop, skip_group_check=True),
             reads=rd, writes=[bank])

    def TR(bank, out, in_, ident, rd):
        P.op("pe", lambda e: e.transpose(out=out, in_=in_, identity=ident), reads=rd, writes=[bank])

    def ACT(out, in_, func, rd, wr, bias=0.0, scale=1.0, accum=None):
        if accum is None:
            P.op("act", lambda e: e.activation(out=out, in_=in_, func=func, bias=bias, scale=scale), reads=rd, writes=wr)
        else:
            P.op("act", lambda e: e.activation(out=out, in_=in_, func=func, bias=bias, scale=scale, accum_out=accum),
                 reads=rd, writes=wr)

    def TT_(eng, out, in0, in1, op, rd, wr):
        P.op(eng, lambda e: e.tensor_tensor(out=out, in0=in0, in1=in1, op=op), reads=rd, writes=wr)

    def TS(eng, out, in0, s1, s2, op0, op1, rd, wr):
        if s2 is None:
            P.op(eng, lambda e: e.tensor_scalar(out=out, in0=in0, scalar1=s1, scalar2=None, op0=op0), reads=rd, writes=wr)
        else:
            P.op(eng, lambda e: e.tensor_scalar(out=out, in0=in0, scalar1=s1, scalar2=s2, op0=op0, op1=op1),
                 reads=rd, writes=wr)

    def STT(out, in0, scalar, in1, op0, op1, rd, wr):
        P.op("dve", lambda e: e.scalar_tensor_tensor(out=out, in0=in0, scalar=scalar, in1=in1, op0=op0, op1=op1),
             reads=rd, writes=wr)

    def CP(eng, out, in_, rd, wr):
        if eng == "act":
            P.op("act", lambda e: e.copy(out=out, in_=in_), reads=rd, writes=wr)
        else:
            P.op(eng, lambda e: e.tensor_copy(out=out, in_=in_), reads=rd, writes=wr)

    def MEMSET(eng, out, val, wr):
        P.op(eng, lambda e: e.memset(out, val), writes=wr)

    def DMA(q, out, in_, key, rd=(), wr=(), slow=False):
        if slow:
            P.dma(q, lambda e: e.dma_start(out=out, in_=in_, allow_slow_non_contiguous=True), key, reads=rd, writes=wr)
        else:
            P.dma(q, lambda e: e.dma_start(out=out, in_=in_), key, reads=rd, writes=wr)

    def wload(dst_buf, dst_ap_fn, rows0, c0, ncols, nk, key, src=None):
        src = w_in if src is None else src
        step = 8
        for k0 in range(0, nk, step):
            DMA("pool", dst_ap_fn(slice(k0, k0 + step)),
                src[rows0 + k0 * 128: rows0 + (k0 + step) * 128, c0:c0 + ncols].rearrange("(k p) n -> p k n", p=128),
                key, wr=[dst_buf])

    hT_all = sbw("hT", 16384, BF16, "p (k t) -> p k t", k=8)
    hTg = [register(Buf("hT%d" % g, hT_all.t[:, :, g * 512:(g + 1) * 512]), 0, 16384) for g in range(8)]
    YM0 = A.top
    ymT_all = sbw("ymT", 16384, BF16, "p (k t) -> p k t", k=16)
    ymS = [register(Buf("ymS%d" % b, ymT_all.t[:, 0:8, b * 128:(b + 1) * 128]), YM0, YM0 + 8192) for b in range(16)]
    ymA = [register(Buf("ymA%d" % b, ymT_all.t[:, 8:16, b * 128:(b + 1) * 128]), YM0 + 8192, YM0 + 16384) for b in range(16)]
    AT0 = YM0 + 8192

    identF = sbw("identF", 128)
    tri = sbw("tri", 128)
    onesF = sbw("onesF", 128)
    maskneg = sbw("maskneg", 128)
    identB = sbw("identB", 64, BF16)
    maskB = sbw("maskB", 64, BF16)
    convw = sbw("convw", 48, F32, "p (c k) -> p c k", k=4)
    convb = sbw("convb", 12)
    dtb = sbw("dtb", 16)
    Abc = sbw("Abc", 16)
    dsk = sbw("dsk", 16)
    fbb = sbw("fbb", 16)
    ssw = sbw("ssw", 8)
    atw = sbw("atw", 8)
    flg = sbw("flg", 2)
    ccol = sbw("ccol", 8)
    csil = sbw("csil", 4, BF16)
    modp1 = sbw("modp1", 16)
    modp2 = sbw("modp2", 16)
    modp3 = sbw("modp3", 16)
    atwh = sbw("atwh", 16)
    sscol = sbw("sscol", 16)
    valid = flg[:, 0:1]

    MEMSET("pool", identF[:], 0.0, [identF])
    P.op("pool", lambda e: e.affine_select(out=identF[:], in_=identF[:], pattern=[[-1, 128]], compare_op=ALU.not_equal,
                                           fill=1.0, base=0, channel_multiplier=1), reads=[identF], writes=[identF])
    MEMSET("pool", onesF[:], 1.0, [onesF])
    MEMSET("pool", tri[:], 1.0, [tri])
    P.op("pool", lambda e: e.affine_select(out=tri[:], in_=tri[:], pattern=[[1, 128]], compare_op=ALU.is_ge,
                                           fill=0.0, base=0, channel_multiplier=-1), reads=[tri], writes=[tri])
    MEMSET("pool", maskneg[:], 0.0, [maskneg])
    P.op("pool", lambda e: e.affine_select(out=maskneg[:], in_=maskneg[:], pattern=[[1, 128]], compare_op=ALU.is_ge,
                                           fill=NEG, base=0, channel_multiplier=-1), reads=[maskneg], writes=[maskneg])
    CP("pool", identB[:], identF[:], [identF], [identB])
    CP("pool", maskB[:], maskneg[:], [maskneg], [maskB])

    for k in range(4):
        DMA("sync", convw[:, :, k], D["conv_w"][k].rearrange("(c p) -> p c", p=128), "cw%d" % k, wr=[convw], slow=True)
    DMA("sync", convb[:], D["conv_b"][0].rearrange("(c p) -> p c", p=128), "cb", wr=[convb], slow=True)
    DMA("sync", ssw[:], D["ssm_norm_w"][0].rearrange("(c p) -> p c", p=128), "ssw", wr=[ssw], slow=True)
    DMA("sync", atw[:], D["attn_norm_w"][0].rearrange("(c p) -> p c", p=128), "atw", wr=[atw], slow=True)
    DMA("sync", atwh[0:64, :], D["attn_norm_w"][0].rearrange("(h d) -> d h", d=64), "atwh", wr=[atwh], slow=True)
    DMA("sync", ccol[:], D["c"][0].rearrange("(c p) -> p c", p=128), "ccol", wr=[ccol], slow=True)
    DMA("sync", dtb[:], D["dt_bias"][0:1, :].partition_broadcast(128), "dtb", wr=[dtb])
    DMA("sync", Abc[:], D["a_log"][0:1, :].partition_broadcast(128), "alog", wr=[Abc])
    DMA("sync", dsk[:], D["d_skip"][0:1, :].partition_broadcast(128), "dsk", wr=[dsk])
    DMA("sync", fbb[:], D["f_bias"][0:1, :].partition_broadcast(128), "fbb", wr=[fbb])
    DMA("sync", flg[:], D["flags"], "flg", wr=[flg])
    ACT(Abc[:], Abc[:], AF.Exp, [Abc], [Abc])
    TS("dve", Abc[:], Abc[:], -1.0, None, ALU.mult, None, [Abc], [Abc])
    ACT(csil[:], ccol[:], AF.Silu, [ccol], [csil])

    mark0 = A.top

    def ada_cols(c0, ncols, row, wt, brow):
        DMA("sync", brow[0:1, 0:ncols], D["b_ada"][0:1, c0:c0 + ncols], "brow", wr=[brow])
        for j in range(ncols // 512):
            DMA("pool", wt[:, :, :], D["w_ada"][:, c0 + j * 512: c0 + (j + 1) * 512].rearrange("(k p) n -> p k n", p=128),
                "wada", wr=[wt])
            bank = nb()
            for k in range(8):
                MM(bank, bank[0:1, 0:512], csil[:, k:k + 1], wt[:, k, :], [csil, wt], k == 0, k == 7)
            TT_("dve", row[0:1, j * 512:(j + 1) * 512], bank[0:1, 0:512], brow[0:1, j * 512:(j + 1) * 512], ALU.add,
                [bank, brow], [row])

    def row_to_cols(row, seg0, nseg, dst, dcol0, add1_from=None):
        bank = nb()
        for s in range(nseg):
            MM(bank, bank[:, s:s + 1], row[0:1, (seg0 + s) * 128:(seg0 + s + 1) * 128], onesF[0:1, 0:1], [row, onesF], s == 0)
        CP("dve", dst[:, dcol0:dcol0 + nseg], bank[:, 0:nseg], [bank], [dst])
        if add1_from is not None:
            TS("dve", dst[:, dcol0 + add1_from:dcol0 + nseg], dst[:, dcol0 + add1_from:dcol0 + nseg], 1.0, None, ALU.add, None,
               [dst], [dst])

    A.top = mark0
    Wz = sbw("Wz", 4096, BF16, "p (k n) -> p k n", k=8)
    markE = A.top
    row = sbw("row", 2048)
    brow = sbw("brow", 2048)
    wt_ada = sbw("wt_ada", 2048, BF16, "p (k n) -> p k n", k=8)
    ada_cols(0, 2048, row, wt_ada, brow)
    row_to_cols(row, 0, 16, modp1, 0, add1_from=8)
    Wc = alias("Wc", AT0 + 0, AT0 + 6144, BF16, "p (k n) -> p k n", k=8)
    Wdt = alias("Wdt", AT0 + 6144, AT0 + 6208, BF16, "p (k n) -> p k n", k=8)
    wload(Wc, lambda k: Wc[:, k, :], 0, CX, 1536, 8, "Wc")
    wload(Wdt, lambda k: Wdt[:, k, :], 0, CDT, 16, 8, "Wdt")
    wload(Wz, lambda k: Wz[:, k, :], 0, CZ, 1024, 8, "Wz")

    xin = [sbw("xin%d" % i, 1024) for i in range(2)]
    for blk in range(32):
        src = D["xp"][blk * 128:(blk + 1) * 128, :] if blk < 16 else D["xo"][(blk - 16) * 128:(blk - 15) * 128, :]
        xt = xin[blk % 2]
        DMA("sync", xt[:], src, "xin%d" % (blk % 2), wr=[xt])
        g = blk // 4
        t0 = blk * 128
        for half in range(2):
            bank = nb()
            for j in range(4):
                k = half * 4 + j
                TR(bank, bank[:, j * 128:(j + 1) * 128], xt[:, k * 128:(k + 1) * 128], identF[:], [xt, identF])
            for j in range(4):
                k = half * 4 + j
                if j % 2 == 0:
                    ACT(hT_all.t[:, k, t0:t0 + 128], bank[:, j * 128:(j + 1) * 128], AF.Identity, [bank, modp1], [hTg[g]],
                        bias=modp1[:, k:k + 1], scale=modp1[:, 8 + k:9 + k])
                else:
                    TS("dve", hT_all.t[:, k, t0:t0 + 128], bank[:, j * 128:(j + 1) * 128], modp1[:, 8 + k:9 + k],
                       modp1[:, k:k + 1], ALU.mult, ALU.add, [bank, modp1], [hTg[g]])

    rowB = alias("rowB", YM0, YM0 + 1024)
    browB = alias("browB", YM0 + 1024, YM0 + 2048)
    wtB = [alias("wtB%d" % i, YM0 + 2048 + i * 2048, YM0 + 4096 + i * 2048, BF16, "p (k n) -> p k n", k=8) for i in range(2)]

    def ada_part_load(part):
        c0 = 2048 + part * 1024
        DMA("sync", browB[0:1, 0:1024], D["b_ada"][0:1, c0:c0 + 1024], "browB", wr=[browB])
        for j in range(2):
            DMA("pool", wtB[j][:, :, :], D["w_ada"][:, c0 + j * 512: c0 + (j + 1) * 512].rearrange("(k p) n -> p k n", p=128),
                "wadaB%d" % j, wr=[wtB[j]])

    def ada_part_compute(part):
        for j in range(2):
            bank = nb()
            for k in range(8):
                MM(bank, bank[0:1, 0:512], csil[:, k:k + 1], wtB[j][:, k, :], [csil, wtB[j]], k == 0, k == 7)
            TT_("dve", rowB[0:1, j * 512:(j + 1) * 512], bank[0:1, 0:512], browB[0:1, j * 512:(j + 1) * 512], ALU.add,
                [bank, browB], [rowB])
        dst, dcol, add1 = [(modp3, 0, 0), (modp2, 0, None), (modp2, 8, 0), (modp3, 8, 0)][part]
        row_to_cols(rowB, 0, 8, dst, dcol, add1_from=add1)

    A.top = markE
    BT = alias("BT", AT0 + 6208, AT0 + 6720, BF16, "p (g t) -> p g t", g=2)
    CT = alias("CT", AT0 + 6720, AT0 + 7232, BF16, "p (g t) -> p g t", g=2)
    Btok = alias("Btok", AT0 + 7232, AT0 + 7744, BF16, "p (j n) -> p j n", j=4)
    xs_tok = sbw("xs_tok", 4096, F32, "p (j c) -> p j c", j=4)
    ubuf = [sbw("ubuf%d" % i, 516) for i in range(2)]
    halo = sbw("halo", 36, F32, "p (c k) -> p c k", k=3)
    acc0_off = A.top
    acc = [sbw("acc%d" % i, 512) for i in range(3)]
    xsT0_off = A.top
    xsT = [sbw("xsT%d" % i, 512) for i in range(2)]
    TAv = [S[:, acc0_off:acc0_off + 1024], S[:, xsT0_off:xsT0_off + 1024]]
    TAb = [[acc[0], acc[1]], [xsT[0], xsT[1]]]
    assert xsT0_off == acc0_off + 1536
    xc = sbw("xc", 512, BF16)
    xcd = sbw("xcd", 512, BF16)
    Sst = sbw("Sst", 1024)
    Sbf = sbw("Sbf", 512, BF16)
    zs = sbw("zs", 1024)
    y1 = sbw("y1", 1024)
    y2 = sbw("y2", 1024)
    MTb = sbw("MTb", 512, BF16, "p (h l) -> p h l", h=8)
    cbm = sbw("cbm", 256)
    Ust = sbw("Ust", 128)
    sms = [sbw("sm%d" % i, 64 * 8) for i in range(1)]
    sq = sbw("sq", 8)
    print("[kernel] SSD words free:", NW - A.top, flush=True)
    MEMSET("pool", Ust[:], 1.0, [Ust])
    P.op("pool", lambda e: e.affine_select(out=Ust[:], in_=Ust[:], pattern=[[-1, 128]], compare_op=ALU.is_gt,
                                           fill=0.0, base=0, channel_multiplier=1), reads=[Ust], writes=[Ust])

    MEMSET("pool", halo[:], 0.0, [halo])
    MEMSET("pool", Sst[:], 0.0, [Sst])

    cnt = [0]
    pend_fin = []
    SPAIRS = [(BK[0], BK[1], PS[:, 0:1024]), (BK[2], BK[3], PS[:, 1024:2048])]
    for G in range(8):
        own = G >= 4
        hg = hTg[G]
        tcol = slice(G * 512, (G + 1) * 512)
        if G == 4:
            hv = halo.t.rearrange("p c k -> p (c k)")
            TS("dve", hv, hv, valid, None, ALU.mult, None, [halo, flg], [halo])
            TS("dve", Sst[:], Sst[:], valid, None, ALU.mult, None, [Sst, flg], [Sst])
        cts = list(range(12)) if (own or G == 3) else list(range(10))
        if G < 4:
            ada_part_load(G)
        pend_c = []
        pend_d = []
        for ct in cts:
            i2 = cnt[0] % 2
            i3 = cnt[0] % 3
            cnt[0] += 1
            bank = nb()
            for k in range(8):
                MM(bank, bank[:, 0:512], Wc[:, k, ct * 128:(ct + 1) * 128], hT_all.t[:, k, tcol], [Wc, hg], k == 0, k == 7)
            while len(pend_d) > 2:
                pend_d.pop(0)()
            ub = ubuf[i2]
            CP("pool", ub[:, 0:3], halo[:, ct, :], [halo], [ub])
            CP("act", ub[:, 3:515], bank[:, 0:512], [bank], [ub])
            CP("pool", halo[:, ct, :], ub[:, 512:515], [ub], [halo])
            ac = acc[i3]
            ACT(ac[:], ub[:, 0:512], AF.Identity, [ub, convw, convb], [ac], bias=convb[:, ct:ct + 1], scale=convw[:, ct, 0:1])
            while len(pend_c) > 1:
                pend_c.pop(0)()
            for kk in range(1, 4):
                STT(ac[:], ub[:, kk:kk + 512], convw[:, ct, kk:kk + 1], ac[:], ALU.mult, ALU.add, [ub, convw, ac], [ac])
            if ct < 8:
                xt_ = xsT[i2]

                def cx(xt_=xt_, ac=ac):
                    ACT(xt_[:], ac[:], AF.Silu, [ac], [xt_])

                def dx(xt_=xt_, ct=ct):
                    b2 = nb()
                    for j in range(4):
                        TR(b2, b2[:, j * 128:(j + 1) * 128], xt_[:, j * 128:(j + 1) * 128], identF[:], [xt_, identF])
                    CP("dve" if ct % 2 else "act", xs_tok[:, :, ct * 128:(ct + 1) * 128],
                       b2[:, 0:512].rearrange("p (j c) -> p j c", j=4), [b2], [xs_tok])
                pend_c.append(cx)
                pend_d.append(dx)
            elif ct < 10:
                g = ct - 8

                def cb_(g=g, ac=ac):
                    ACT(BT[:, g, :], ac[:], AF.Silu, [ac], [BT])

                def db_(g=g):
                    b2 = nb()
                    for j in range(4):
                        TR(b2, b2.tb[:, j * 128:(j + 1) * 128], BT[:, g, j * 128:(j + 1) * 128], identB[:], [BT, identB])
                    CP("dve", Btok[:, :, g * 128:(g + 1) * 128], b2.tb[:, 0:512].rearrange("p (j c) -> p j c", j=4), [b2], [Btok])
                pend_c.append(cb_)
                pend_d.append(db_)
            else:
                g = ct - 10

                def cc_(g=g, ac=ac):
                    ACT(CT[:, g, :], ac[:], AF.Silu, [ac], [CT])
                pend_c.append(cc_)

        def flush_conv():
            while pend_c:
                pend_c.pop(0)()
            while pend_d:
                pend_d.pop(0)()
        if G < 4:
            ada_part_compute(G)
        bdt = nb()
        for j in range(4):
            ctok = slice((G * 4 + j) * 128, (G * 4 + j + 1) * 128)
            for k in range(8):
                MM(bdt, bdt[:, j * 16:(j + 1) * 16], hT_all.t[:, k, ctok], Wdt[:, k, :], [hg, Wdt], (j == 0 and k == 0), k == 7)
        flush_conv()
        sm = sms[0]
        xr4, dtt4, aa4, acs4, dout4, dte4, dch4, wv4 = [sm[:, i * 64:(i + 1) * 64] for i in range(8)]

        def v3(ap):
            return ap.rearrange("p (j h) -> p j h", j=4)
        TT_("dve", v3(xr4), v3(bdt[:, 0:64]), dtb[:].unsqueeze(1).to_broadcast([128, 4, 16]), ALU.add, [bdt, dtb], [sm])
        ACT(xr4, xr4, AF.Exp, [sm], [sm])
        ACT(dtt4, xr4, AF.Ln, [sm], [sm], bias=1.0)
        TT_("dve", v3(aa4), v3(dtt4), Abc[:].unsqueeze(1).to_broadcast([128, 4, 16]), ALU.mult, [sm, Abc], [sm])
        bcs = nb()
        MM(bcs, bcs[:, 0:64], tri[:], aa4, [tri, sm], True)
        MM(bcs, bcs[:, 64:128], onesF[:], aa4, [onesF, sm], False)
        CP("dve", acs4, bcs[:, 0:64], [bcs], [sm])
        ACT(dout4, bcs[:, 0:64], AF.Exp, [bcs], [sm])
        ACT(dch4, bcs[:, 64:128], AF.Exp, [bcs], [sm])
        TT_("dve", dte4, bcs[:, 64:128], acs4, ALU.subtract, [bcs, sm], [sm])
        ACT(dte4, dte4, AF.Exp, [sm], [sm])
        TT_("dve", wv4, dtt4, dte4, ALU.mult, [sm], [sm])
        for j in range(4):
            ch = G * 4 + j
            ctok = slice(ch * 128, (ch + 1) * 128)
            xr, dtt, aa, acs, dout, dte, dch, wv = [sm[:, i * 64 + j * 16:i * 64 + (j + 1) * 16] for i in range(8)]
            xsj = xs_tok[:, j, :].rearrange("p (h d) -> p h d", h=16)
            if not own:
                TT_("pool", xcd[:].rearrange("p (h d) -> p h d", h=16), xsj, wv.unsqueeze(2).to_broadcast([128, 16, 64]),
                    ALU.mult, [xs_tok, sm], [xcd])
            if own:
                CP("act", Sbf[:], Sst[:], [Sst], [Sbf])
                prs = []
                for g in range(2):
                    TT_("dve", TAv[g].rearrange("p (h l) -> p h l", h=8), tri[:].unsqueeze(1).to_broadcast([128, 8, 128]),
                        aa[:, g * 8:(g + 1) * 8].unsqueeze(2).to_broadcast([128, 8, 128]), ALU.mult, [tri, sm], TAb[g])
                TT_("pool", xc[:].rearrange("p (h d) -> p h d", h=16), xsj, dtt.unsqueeze(2).to_broadcast([128, 16, 64]),
                    ALU.mult, [xs_tok, sm] + TAb[1], [xc])
                TT_("pool", y2[:].rearrange("p (h d) -> p h d", h=16), xsj, dsk[:].unsqueeze(2).to_broadcast([128, 16, 64]),
                    ALU.mult, [xs_tok, dsk], [y2])
                TT_("pool", xcd[:].rearrange("p (h d) -> p h d", h=16), xsj, wv.unsqueeze(2).to_broadcast([128, 16, 64]),
                    ALU.mult, [xs_tok, sm], [xcd])
                for g in range(2):
                    b0, b1, pview = SPAIRS[g]
                    MM(b0, b0[:, 0:512], Ust[:], TAv[g][:, 0:512], [Ust] + TAb[g], True)
                    MM(b1, b1[:, 0:512], Ust[:], TAv[g][:, 512:1024], [Ust] + TAb[g], True)
                    ACT(pview, pview, AF.Exp, [b0, b1], [b0, b1])
                    prs.append((b0, b1, pview))
                bcb = BK[5]
                for g in range(2):
                    MM(bcb, bcb[:, g * 128:(g + 1) * 128], BT[:, g, j * 128:(j + 1) * 128], CT[:, g, j * 128:(j + 1) * 128],
                       [BT, CT], g == 0)
                TT_("dve", cbm[:].rearrange("p (g l) -> p g l", g=2), bcb[:, 0:256].rearrange("p (g l) -> p g l", g=2),
                    tri[:].unsqueeze(1).to_broadcast([128, 2, 128]), ALU.mult, [bcb, tri], [cbm])
                for half in range(2):
                    bz = BK[4 + 3 * half]
                    for k in range(8):
                        MM(bz, bz[:, 0:512], hT_all.t[:, k, ctok], Wz[:, k, half * 512:(half + 1) * 512], [hg, Wz], k == 0, k == 7)
                    ACT(zs[:, half * 512:(half + 1) * 512], bz[:, 0:512], AF.Silu, [bz], [zs])
                for g in range(2):
                    b0, b1, pview = prs[g]
                    TT_("dve", MTb[:], pview.rearrange("p (h l) -> p h l", h=8),
                        cbm[:, g * 128:(g + 1) * 128].unsqueeze(1).to_broadcast([128, 8, 128]), ALU.mult, [b0, b1, cbm], [MTb])
                    byd = BK[6]
                    for hh in range(8):
                        h = g * 8 + hh
                        MM(byd, byd[:, hh * 64:(hh + 1) * 64], MTb[:, hh, :], xc[:, h * 64:(h + 1) * 64], [MTb, xc], hh == 0)
                    if g == 0:
                        while pend_fin:
                            pend_fin.pop(0)()
                    boff = BK[4 + 3 * g]
                    MM(boff, boff[:, 0:512], CT[:, g, j * 128:(j + 1) * 128], Sbf[:, g * 512:(g + 1) * 512], [CT, Sbf], True)
                    ysl = slice(g * 512, (g + 1) * 512)
                    TT_("dve", y1[:, ysl].rearrange("p (h d) -> p h d", h=8), boff[:, 0:512].rearrange("p (h d) -> p h d", h=8),
                        dout[:, g * 8:(g + 1) * 8].unsqueeze(2).to_broadcast([128, 8, 64]), ALU.mult, [boff, sm], [y1])
                    TT_("dve", y1[:, ysl], byd[:, 0:512], y1[:, ysl], ALU.add, [byd, y1], [y1])
                TT_("dve", y2[:], y2[:], y1[:], ALU.add, [y2, y1], [y2])
                TT_("dve", y2[:], y2[:], zs[:], ALU.mult, [y2, zs], [y2])
                MEMSET("pool", sq[:, 0:1], 0.0, [sq])
                ACT(y1[:], y2[:], AF.Square, [y2], [y1, sq], accum=sq[:, 0:1])
                ACT(sq[:, 1:2], sq[:, 0:1], AF.Sqrt, [sq], [sq], bias=EPS, scale=1.0 / 1024.0)
                P.op("dve", lambda e: e.reciprocal(out=sq[:, 2:3], in_=sq[:, 1:2]), reads=[sq], writes=[sq])
                TS("dve", y1[:], y2[:], sq[:, 2:3], None, ALU.mult, None, [y2, sq], [y1])

                def fin(ob=ch - 16):
                    for half in range(2):
                        bt_ = BK[4 + 3 * half]
                        for jj in range(4):
                            kt = half * 4 + jj
                            TR(bt_, bt_[:, jj * 128:(jj + 1) * 128], y1[:, kt * 128:(kt + 1) * 128], identF[:], [y1, identF])
                        for jj in range(4):
                            kt = half * 4 + jj
                            ACT(ymT_all.t[:, kt, ob * 128:(ob + 1) * 128], bt_[:, jj * 128:(jj + 1) * 128], AF.Identity,
                                [bt_, ssw], [ymS[ob]], scale=ssw[:, kt:kt + 1])
                pend_fin.append(fin)
            for g in range(2):
                bs = nb()
                MM(bs, bs[:, 0:512], Btok[:, j, g * 128:(g + 1) * 128], xcd[:, g * 512:(g + 1) * 512], [Btok, xcd], True)
                ssl = slice(g * 512, (g + 1) * 512)
                TT_("pool" if own else "dve", Sst[:, ssl].rearrange("p (h d) -> p h d", h=8),
                    Sst[:, ssl].rearrange("p (h d) -> p h d", h=8),
                    dch[:, g * 8:(g + 1) * 8].unsqueeze(2).to_broadcast([128, 8, 64]), ALU.mult, [Sst, sm], [Sst])
                TT_("dve", Sst[:, ssl], bs[:, 0:512], Sst[:, ssl], ALU.add, [bs, Sst], [Sst])
    while pend_fin:
        pend_fin.pop(0)()

    A.top = mark0
    markF = A.top
    DMA("sync", ymsd.rearrange("p (k t) -> p k t", k=8), ymT_all.t[:, 0:8, :], "ymsp", rd=ymS)
    KaS = [[sbw("Ka0%d" % i, 2048, BF16) for i in range(2)],
           [alias("Ka1%d" % i, YM0 + i * 2048, YM0 + (i + 1) * 2048, BF16) for i in range(2)]]
    VaS = [sbw("Va0", 4096, BF16, "p (b h d) -> p b h d", b=32, h=2),
           alias("Va1", YM0 + 4096, YM0 + 8192, BF16, "p (b h d) -> p b h d", b=32, h=2)]
    QaS = [[sbw("Qa0%d" % i, 1024, BF16) for i in range(2)], None]
    Wqkv = sbw("Wqkv", 1536, BF16, "p (k n) -> p k n", k=8)
    pT = [sbw("pT%d" % i, 512, BF16) for i in range(3)]
    vts = [sbw("vts%d" % i, 256, BF16) for i in range(2)]
    print("[kernel] attention words free:", NW - A.top, flush=True)
    markF2 = A.top
    Wf = sbw("Wf", 64, BF16, "p (k n) -> p k n", k=8)
    lfa = sbw("lfa", 512)
    cumT = sbw("cumT", 512)
    r1 = sbw("r1", 512)
    c3s = sbw("c3s", 768, BF16, "p (r t) -> p r t", r=3)
    carry = sbw("carry", 1)

    wload(Wf, lambda k: Wf[:, k, :], 0, CF, 16, 8, "Wf")
    MEMSET("dve", carry[:], 0.0, [carry])
    MEMSET("pool", sscol[:], 0.0, [sscol])
    bf = nb()
    for blk in range(32):
        for k in range(8):
            MM(bf, bf[:, blk * 16:(blk + 1) * 16], hT_all.t[:, k, blk * 128:(blk + 1) * 128], Wf[:, k, :], [hTg[blk // 4], Wf],
               (blk == 0 and k == 0), k == 7)
    TT_("dve", lfa[:].rearrange("p (b h) -> p b h", b=32), bf[:, 0:512].rearrange("p (b h) -> p b h", b=32),
        fbb[:].unsqueeze(1).to_broadcast([128, 32, 16]), ALU.add, [bf, fbb], [lfa])
    ACT(lfa[:], lfa[:], AF.Exp, [lfa], [lfa], scale=-1.0)
    ACT(lfa[:], lfa[:], AF.Ln, [lfa], [lfa], bias=1.0)
    TS("dve", lfa[:], lfa[:], -1.0, None, ALU.mult, None, [lfa], [lfa])
    for q4 in range(8):
        bc = nb()
        for bq in range(4):
            blk = q4 * 4 + bq
            MM(bc, bc[0:16, bq * 128:(bq + 1) * 128], lfa[:, blk * 16:(blk + 1) * 16], tri[:], [lfa, tri], bq == 0)
        for bq in range(4):
            TS("dve", cumT[0:16, bq * 128:(bq + 1) * 128], bc[0:16, bq * 128:(bq + 1) * 128], carry[0:16, 0:1], None, ALU.add, None,
               [bc, carry], [cumT])
            CP("dve", carry[0:16, 0:1], cumT[0:16, bq * 128 + 127:bq * 128 + 128], [cumT], [carry])
        CP("dve", c3s[0:16, 0, :], cumT[0:16, :], [cumT], [c3s])
        TT_("dve", r1[0:16, :], cumT[0:16, :], c3s[0:16, 0, :], ALU.subtract, [cumT, c3s], [r1])
        CP("dve", c3s[0:16, 1, :], r1[0:16, :], [r1], [c3s])
        TT_("dve", r1[0:16, :], r1[0:16, :], c3s[0:16, 1, :], ALU.subtract, [r1, c3s], [r1])
        CP("dve", c3s[0:16, 2, :], r1[0:16, :], [r1], [c3s])
        DMA("sync", c3d.rearrange("h (r t) -> h r t", r=3)[:, :, q4 * 512:(q4 + 1) * 512], c3s[0:16, :, :], "c3w", rd=[c3s])
    c3tok = Buf("c3dram", None)
    c3tok.lw = ("s", "c3w", P.semvals["c3w"])
    c3v = c3d.rearrange("h (r t) -> h r t", r=3)
    A.top = markF2
    rbc = sbw("rbc", 512)
    yv = [sbw("yv%d" % i, 512) for i in range(2)]
    ysq = sbw("ysq", 512)
    QaS[1] = [sbw("Qa1%d" % i, 1024, BF16) for i in range(2)]

    for Va in VaS:
        MEMSET("pool", Va[:, :, :, 64:128], 1.0, [Va])
        TS("pool", Va[:, 0:16, :, 64:128], Va[:, 0:16, :, 64:128], valid, None, ALU.mult, None, [Va, flg], [Va])
    for sl in range(2):
        for i in range(2):
            MEMSET("pool", KaS[sl][i][64:128, :], 0.0, [KaS[sl][i]])
            MEMSET("pool", KaS[sl][i][64:70, :], 1.0, [KaS[sl][i]])
            MEMSET("pool", QaS[sl][i][64:128, :], 0.0, [QaS[sl][i]])
            MEMSET("pool", QaS[sl][i][64:70, :], -1.0, [QaS[sl][i]])

    pcnt = [0]
    ecnt = [0]
    fcnt = [0]
    deferred = []
    filler = []
    tcnt = [0]

    def tick():
        for d_ in deferred:
            d_[0] -= 1
        while deferred and deferred[0][0] <= 0:
            deferred.pop(0)[1]()
        tcnt[0] += 1
        if filler and tcnt[0] % 5 == 0:
            filler.pop(0)()

    def flush():
        while deferred:
            deferred.pop(0)[1]()

    def fbank():
        b = BK[4 + 3 * (fcnt[0] % 2)]
        fcnt[0] += 1
        return b

    def proj_units(hp):
        sl = hp % 2
        Ka, Qa, Va, wq = KaS[sl], QaS[sl], VaS[sl], Wqkv
        units = []

        def u_load():
            for i, c0 in enumerate((CQ, CK, CV)):
                wload(wq, (lambda i: (lambda k: wq[:, k, i * 128:(i + 1) * 128]))(i), 0, c0 + hp * 128, 128, 8, "Wq")
            if hp == 0:
                for r0 in range(0, 1024, 128):
                    DMA("pool", w1b[r0:r0 + 128, :], D["w_ff_in"][r0:r0 + 128, :], "precast")
                for r0 in range(0, 2048, 128):
                    DMA("pool", wob[r0:r0 + 128, :], D["w_out"][r0:r0 + 128, :], "precast")
                for r0 in range(0, 4096, 128):
                    DMA("pool", w2b[r0:r0 + 128, :], D["w_ff_out"][r0:r0 + 128, :], "precast")
            for hh in range(2):
                h = hp * 2 + hh
                DMA("sync", Ka[hh][67:70, :], c3v[h, :, :], "ka%d%d" % (sl, hh), rd=[c3tok], wr=[Ka[hh]])
                DMA("sync", Qa[hh][64:67, :], c3v[h, :, TP:TT], "qa%d%d" % (sl, hh), rd=[c3tok], wr=[Qa[hh]])
        units.append(u_load)

        def u_k(G):
            bk_ = fbank()
            for k in range(8):
                MM(bk_, bk_[:, 0:512], wq[:, k, 128:256], hT_all.t[:, k, G * 512:(G + 1) * 512], [wq, hTg[G]], k == 0, k == 7)
            CP("dve", Ka[0][0:64, G * 512:(G + 1) * 512], bk_[0:64, 0:512], [bk_], [Ka[0]])
            CP("dve", Ka[1][0:64, G * 512:(G + 1) * 512], bk_[64:128, 0:512], [bk_], [Ka[1]])

        def u_q(G):
            bq_ = fbank()
            for k in range(8):
                MM(bq_, bq_[:, 0:512], wq[:, k, 0:128], hT_all.t[:, k, TP + G * 512:TP + (G + 1) * 512], [wq, hTg[4 + G]],
                   k == 0, k == 7)
            TS("dve", Qa[0][0:64, G * 512:(G + 1) * 512], bq_[0:64, 0:512], 0.125, None, ALU.mult, None, [bq_], [Qa[0]])
            TS("dve", Qa[1][0:64, G * 512:(G + 1) * 512], bq_[64:128, 0:512], 0.125, None, ALU.mult, None, [bq_], [Qa[1]])

        def u_v(G):
            bv = fbank()
            for k in range(8):
                MM(bv, bv[:, 0:512], wq[:, k, 256:384], hT_all.t[:, k, G * 512:(G + 1) * 512], [wq, hTg[G]], k == 0, k == 7)
            vt = vts[G % 2]
            CP("dve", vt[:, 0:512], bv[:, 0:512], [bv], [vt])
            b2 = fbank()
            for j in range(4):
                TR(b2, b2.tb[:, j * 128:(j + 1) * 128], vt[:, j * 128:(j + 1) * 128], identB[:], [vt, identB])
            src = b2.tb[:, 0:512].rearrange("p (b h d) -> p b h d", b=4, h=2)
            if G < 4:
                TS("dve", Va[:, G * 4:(G + 1) * 4, :, 0:64], src, valid, None, ALU.mult, None, [b2, flg], [Va])
            else:
                CP("dve", Va[:, G * 4:(G + 1) * 4, :, 0:64], src, [b2], [Va])
        for G in range(8):
            units.append((lambda G: (lambda: u_k(G)))(G))
        for G in range(4):
            units.append((lambda G: (lambda: u_q(G)))(G))
        for G in range(8):
            units.append((lambda G: (lambda: u_v(G)))(G))
        return units

    PAIRS = [(BK[0], BK[1], PS[:, 0:1024]), (BK[2], BK[3], PS[:, 1024:2048])]
    for u in proj_units(0):
        u()
    for hp in range(NH // 2):
        sl = hp % 2
        Va = VaS[sl]
        if hp + 1 < NH // 2:
            filler.extend(proj_units(hp + 1))
        for hh in range(2):
            h = hp * 2 + hh
            ka, qa = KaS[sl][hh], QaS[sl][hh]
            for G in range(4):
                po = BK[5 + ecnt[0] % 2]
                npre = 16 + 4 * G
                nkb = npre + 4
                steps = [([kb, kb + 1], 0, 512) for kb in range(0, npre, 2)] + \
                        [([npre + r], r, (4 - r) * 128) for r in range(4)]
                pend = None
                for si in range(len(steps) + 1):
                    cur = None
                    if si < len(steps):
                        kbs, q0, ncol = steps[si]
                        pt = pT[pcnt[0] % 3]
                        if len(kbs) == 2:
                            b0, b1, pview = PAIRS[pcnt[0] % 2]
                            for bi, kb in zip((b0, b1), kbs):
                                MM(bi, bi[:, 0:512], ka[:, kb * 128:(kb + 1) * 128], qa[:, G * 512:(G + 1) * 512],
                                   [ka, qa], True, True)
                            ACT(pt[:, 0:1024], pview, AF.Exp, [b0, b1], [pt])
                        else:
                            kb = kbs[0]
                            bs_ = BK[4 + 3 * (q0 % 2)]
                            MM(bs_, bs_[:, 0:ncol], ka[:, kb * 128:(kb + 1) * 128],
                               qa[:, G * 512 + q0 * 128:(G + 1) * 512], [ka, qa], True, False)
                            MM(bs_, bs_[:, 0:128], identB[:], maskB[:], [identB, maskB], False, True)
                            ACT(pt[:, 0:ncol], bs_[:, 0:ncol], AF.Exp, [bs_], [pt])
                        pcnt[0] += 1
                        cur = (kbs, q0, ncol, pt)
                    if pend is not None:
                        kbs_, q0_, ncol_, pt_ = pend
                        for i_, kb_ in enumerate(kbs_):
                            MM(po, po[:, q0_ * 128:512], Va[:, kb_, hh, :], pt_[:, i_ * 512:i_ * 512 + ncol_], [Va, pt_],
                               kb_ == 0, kb_ == nkb - 1)
                    pend = cur
                    tick()
                P.op("dve", (lambda po_: (lambda e: e.reciprocal(out=rbc[0:64, :], in_=po_[64:128, 0:512])))(po),
                     reads=[po], writes=[rbc])
                yv_ = yv[ecnt[0] % 2]
                TT_("dve", yv_[0:64, :], po[0:64, 0:512], rbc[0:64, :], ALU.mult, [po, rbc], [yv_])
                pr = (h % 2) * 64
                TT_("pool", ysq[0:64, :], yv_[0:64, :], yv_[0:64, :], ALU.mult, [yv_], [ysq])
                TS("pool", ymT_all.t[pr:pr + 64, 8 + h // 2, G * 512:(G + 1) * 512], yv_[0:64, :], atwh[0:64, h:h + 1], None,
                   ALU.mult, None, [yv_, atwh], [ymA[G * 4 + j] for j in range(4)])

                def ep2(G=G, e_=ecnt[0]):
                    bss = BK[4 + 3 * (e_ % 2)]
                    for j4 in range(4):
                        MM(bss, bss[:, j4:j4 + 1], ysq[0:64, j4 * 128:(j4 + 1) * 128], onesF[0:64, 0:1], [onesF, ysq], j4 == 0)
                    TT_("dve", sscol[:, G * 4:(G + 1) * 4], bss[:, 0:4], sscol[:, G * 4:(G + 1) * 4], ALU.add,
                        [bss, sscol], [sscol])

                deferred.append([8, ep2])
                ecnt[0] += 1
        while filler:
            filler.pop(0)()
    flush()

    A.top = markF
    g1b = alias("g1b", 0, 1024)
    g2b = alias("g2b", 1024, 2048)
    lnb = alias("lnb", 2048, 6144, F32, "p (v n) -> p v n", v=4)
    h2Tb = alias("h2Tb", 6144, 14336, BF16, "p (k t) -> p k t", k=8)
    xr_ = [sbw("xr%d" % i, 1024) for i in range(2)]
    t1s = [sbw("t1_%d" % i, 1024) for i in range(2)]
    t2 = sbw("t2", 1024)
    sts = [sbw("st%d" % i, 16) for i in range(2)]
    t1 = t1s[0]
    st_ = sts[0]
    markGH = A.top
    Wo = sbw("Wo", 8192, BF16, "p (k n) -> p k n", k=16)
    dg = [sbw("dg%d" % i, 128) for i in range(2)]

    ymS = [register(Buf("ymS2_%d" % b, ymT_all.t[:, 0:8, b * 128:(b + 1) * 128]), YM0, YM0 + 8192) for b in range(16)]
    spill = Buf("ymsd", None)
    spill.lw = ("s", "ymsp", P.semvals["ymsp"])
    DMA("sync", ymT_all.t[:, 0:8, :], ymsd.rearrange("p (k t) -> p k t", k=8), "ymrl", rd=[spill], wr=ymS)
    pctok = Buf("precast", None)
    pctok.lw = ("s", "precast", P.semvals["precast"])
    for v, nme in enumerate(["ln1_g", "ln1_b", "ln2_g", "ln2_b"]):
        DMA("sync", lnb[:, v, :], D[nme][0:1, :].partition_broadcast(128), "lnb%d" % v, wr=[lnb])
    for k0 in (0, 8):
        DMA("sync", Wo[:, k0:k0 + 8, :], wob[k0 * 128:(k0 + 8) * 128, :].rearrange("(k p) n -> p k n", p=128), "Wo",
            rd=[pctok], wr=[Wo])

    def col_bcast(c0, dst):
        for half in range(2):
            bank = nb()
            for j in range(4):
                k = half * 4 + j
                d_ = dg[k % 2]
                TS("dve", d_[:], identF[:], modp3[:, c0 + k:c0 + k + 1], None, ALU.mult, None, [identF, modp3], [d_])
                MM(bank, bank[:, j * 128:(j + 1) * 128], onesF[:], d_[:], [onesF, d_], j == 0)
            CP("act", dst[:, half * 512:(half + 1) * 512], bank[:, 0:512], [bank], [dst])

    col_bcast(0, g1b)
    col_bcast(8, g2b)

    def layer_norm(src, dst, gi, st_):
        MEMSET("dve", st_[:, 0:2], 0.0, [st_])
        ACT(t2[:], src[:], AF.Identity, [src], [st_], accum=st_[:, 0:1])
        ACT(t2[:], src[:], AF.Square, [src], [st_], accum=st_[:, 1:2])
        TS("dve", st_[:, 2:3], st_[:, 0:1], 1.0 / 1024.0, None, ALU.mult, None, [st_], [st_])
        TT_("dve", st_[:, 3:4], st_[:, 2:3], st_[:, 2:3], ALU.mult, [st_], [st_])
        STT(st_[:, 4:5], st_[:, 1:2], 1.0 / 1024.0, st_[:, 3:4], ALU.mult, ALU.subtract, [st_], [st_])
        ACT(st_[:, 5:6], st_[:, 4:5], AF.Sqrt, [st_], [st_], bias=EPS)
        P.op("dve", lambda e: e.reciprocal(out=st_[:, 6:7], in_=st_[:, 5:6]), reads=[st_], writes=[st_])
        TS("dve", dst[:], src[:], st_[:, 2:3], st_[:, 6:7], ALU.subtract, ALU.mult, [src, st_], [dst])
        TT_("dve", dst[:], dst[:], lnb[:, gi, :], ALU.mult, [dst, lnb], [dst])
        TT_("dve", dst[:], dst[:], lnb[:, gi + 1, :], ALU.add, [dst, lnb], [dst])

    x1toks = [Buf("x1dram%d" % i, None) for i in range(2)]
    pend_g = []
    for ob in range(16):
        xt = xr_[ob % 2]
        t1 = t1s[ob % 2]
        st_ = sts[ob % 2]
        DMA("sync", xt[:], D["xo"][ob * 128:(ob + 1) * 128, :], "xr%d" % (ob % 2), wr=[xt])
        ACT(st_[:, 9:10], sscol[:, ob:ob + 1], AF.Sqrt, [sscol], [st_], bias=EPS, scale=1.0 / 1024.0)
        P.op("dve", lambda e, st_=st_: e.reciprocal(out=st_[:, 10:11], in_=st_[:, 9:10]), reads=[st_], writes=[st_])
        for half in range(2):
            b1 = nb()
            for kt in range(8):
                MM(b1, b1[:, 0:512], ymT_all.t[:, kt, ob * 128:(ob + 1) * 128], Wo[:, kt, half * 512:(half + 1) * 512],
                   [ymS[ob], Wo], kt == 0, kt == 7)
            b2 = nb()
            for kt in range(8, 16):
                MM(b2, b2[:, 0:512], ymT_all.t[:, kt, ob * 128:(ob + 1) * 128], Wo[:, kt, half * 512:(half + 1) * 512],
                   [ymA[ob], Wo], kt == 8, kt == 15)
            hs = slice(half * 512, (half + 1) * 512)
            CP("act", t1[:, hs], b1[:, 0:512], [b1], [t1])
            STT(t1[:, hs], b2[:, 0:512], st_[:, 10:11], t1[:, hs], ALU.mult, ALU.add, [b2, st_, t1], [t1])
        while pend_g:
            pend_g.pop(0)()
        TT_("dve", t1[:], t1[:], g1b[:], ALU.mult, [t1, g1b], [t1])
        STT(t1[:], xt[:], ALPHA, t1[:], ALU.mult, ALU.add, [xt, t1], [t1])
        layer_norm(t1, xt, 0, st_)
        DMA("sync", x1s[ob * 128:(ob + 1) * 128, :], xt[:], "x1w%d" % (ob % 2), rd=[xt])

        def trg(xt=xt, ob=ob):
            for half in range(2):
                bt_ = nb()
                for jj in range(4):
                    k = half * 4 + jj
                    TR(bt_, bt_[:, jj * 128:(jj + 1) * 128], xt[:, k * 128:(k + 1) * 128], identF[:], [xt, identF])
                for jj in range(4):
                    k = half * 4 + jj
                    if jj % 2 == 0:
                        ACT(h2Tb[:, k, ob * 128:(ob + 1) * 128], bt_[:, jj * 128:(jj + 1) * 128], AF.Identity, [bt_, modp2],
                            [h2Tb], bias=modp2[:, k:k + 1], scale=modp2[:, 8 + k:9 + k])
                    else:
                        TS("dve", h2Tb[:, k, ob * 128:(ob + 1) * 128], bt_[:, jj * 128:(jj + 1) * 128], modp2[:, 8 + k:9 + k],
                           modp2[:, k:k + 1], ALU.mult, ALU.add, [bt_, modp2], [h2Tb])
        pend_g.append(trg)
    while pend_g:
        pend_g.pop(0)()
    for i in range(2):
        x1toks[i].lw = ("s", "x1w%d" % i, P.semvals["x1w%d" % i])

    W2 = alias("W2", YM0, YM0 + 16384, BF16, "p (f n) -> p f n", f=32)
    for f0 in range(0, 32, 8):
        DMA("sync", W2[:, f0:f0 + 8, :], w2b[f0 * 128:(f0 + 8) * 128, :].rearrange("(f p) n -> p f n", p=128), "W2",
            rd=[pctok], wr=[W2])
    A.top = markGH
    a1T = sbw("a1T", 8192, BF16, "p (f t) -> p f t", f=32)
    W1t = [sbw("W1t%d" % i, 2048, BF16, "p (k n) -> p k n", k=8) for i in range(2)]
    rl = [sbw("rl%d" % i, 512) for i in range(2)]
    wcnt = [0]
    for G in range(4):
        for c4 in range(8):
            w1 = W1t[wcnt[0] % len(W1t)]
            DMA("sync", w1[:, :, :], w1b[:, c4 * 512:(c4 + 1) * 512].rearrange("(k p) n -> p k n", p=128),
                "W1t%d" % (wcnt[0] % len(W1t)), rd=[pctok], wr=[w1])
            wcnt[0] += 1
            for f4 in range(4):
                f = c4 * 4 + f4
                bf_ = nb()
                for k in range(8):
                    MM(bf_, bf_[:, 0:512], w1[:, k, f4 * 128:(f4 + 1) * 128], h2Tb[:, k, G * 512:(G + 1) * 512], [w1, h2Tb],
                       k == 0, k == 7)
                r_ = rl[f % 2]
                ACT(r_[:], bf_[:, 0:512], AF.Relu, [bf_], [r_])
                TT_("dve", a1T[:, f, :], r_[:], r_[:], ALU.mult, [r_], [a1T])
        for tb in range(4):
            ob = G * 4 + tb
            xt = xr_[ob % 2]
            t1 = t1s[ob % 2]
            st_ = sts[ob % 2]
            DMA("sync", xt[:], x1s[ob * 128:(ob + 1) * 128, :], "xr%d" % (ob % 2), rd=x1toks, wr=[xt])
            for half in range(2):
                bo = nb()
                for f in range(32):
                    MM(bo, bo[:, 0:512], a1T[:, f, tb * 128:(tb + 1) * 128], W2[:, f, half * 512:(half + 1) * 512], [a1T, W2],
                       f == 0, f == 31)
                hs = slice(half * 512, (half + 1) * 512)
                TT_("dve", t1[:, hs], bo[:, 0:512], g2b[:, hs], ALU.mult, [bo, g2b], [t1])
            STT(t1[:], xt[:], ALPHA, t1[:], ALU.mult, ALU.add, [xt, t1], [t1])
            layer_norm(t1, xt, 2, st_)
            DMA("sync", out_d[ob * 128:(ob + 1) * 128, :], xt[:], "ow%d" % (ob % 2), rd=[xt])

    print("[kernel] ops:", {k: len(v) for k, v in P.ops.items()}, "cnt:", P.cnt, "nsem:", len(P.semkeys), flush=True)
    P.emit()
    st.close()
    return nc


_NC = [None]


def kernel(**inputs):
    x = np.ascontiguousarray(np.asarray(inputs["x"], dtype=np.float32))
    c = np.asarray(inputs["c"], dtype=np.float32)
    if _NC[0] is None:
        _NC[0] = build()
    nc = _NC[0]
    wmap = {}
    for n in WEIGHT_NAMES:
        wmap[n] = np.ascontiguousarray(np.asarray(inputs[n], dtype=np.float32).reshape(WEIGHT_SHAPES[n]))
    in_maps = []
    for core in range(8):
        b, h = core // 2, core % 2
        m = dict(wmap)
        m["xo"] = np.ascontiguousarray(x[b, h * TO:(h + 1) * TO])
        m["xp"] = np.ascontiguousarray(x[b, 0:TP])
        m["c"] = np.ascontiguousarray(c[b:b + 1])
        fl = np.zeros((128, 2), np.float32)
        fl[:, 0] = float(h)
        m["flags"] = fl
        in_maps.append(m)
    res = run_bass_kernel_spmd(nc, in_maps, core_ids=list(range(8)))
    out = np.empty((4, 4096, DM), np.float32)
    for core in range(8):
        b, h = core // 2, core % 2
        out[b, h * TO:(h + 1) * TO] = res.results[core]["out"]
    return out
```

```python
import contextlib
import numpy as np
import concourse.bass as bass
import concourse.mybir as mybir
from concourse.bass_utils import run_bass_kernel_spmd

F32 = mybir.dt.float32
BF16 = mybir.dt.bfloat16
AF = mybir.ActivationFunctionType
ALU = mybir.AluOpType

COMPUTE = ("pe", "act", "dve", "pool")
SELF_SYNC = ("act", "dve", "pool")
QUEUES = COMPUTE + ("sync",)


class Buf:
    __slots__ = ("name", "t", "tb", "lw", "rd")

    def __init__(self, name, t, tb=None):
        self.name = name
        self.t = t
        self.tb = tb
        self.lw = None
        self.rd = []

    def __getitem__(self, k):
        return self.t[k]


class Prog:
    def __init__(self, nc):
        self.nc = nc
        self.ops = {e: [] for e in QUEUES}
        self.cnt = {e: 0 for e in COMPUTE}
        self.waited = {e: {} for e in QUEUES}
        self.semvals = {}
        self.semkeys = []

    def _need(self, eng, waits, tok):
        if tok is None:
            return
        if tok[0] == "e":
            _, e2, idx = tok
            if e2 == eng and eng not in SELF_SYNC:
                return
            key = ("e", e2)
            val = idx
        else:
            _, sk, val = tok
            key = ("s", sk)
        if self.waited[eng].get(key, 0) >= val:
            return
        if waits.get(key, 0) < val:
            waits[key] = val

    def _collect(self, eng, reads, writes):
        waits = {}
        for b in reads:
            self._need(eng, waits, b.lw)
        for b in writes:
            self._need(eng, waits, b.lw)
            for tok in b.rd:
                self._need(eng, waits, tok)
        for k, v in waits.items():
            self.waited[eng][k] = v
        return waits

    def _mark(self, tok, reads, writes):
        for b in reads:
            b.rd.append(tok)
            if len(b.rd) > 12:
                best = {}
                for t in b.rd:
                    k = (t[0], t[1])
                    if k not in best or best[k][2] < t[2]:
                        best[k] = t
                b.rd = list(best.values())
        for b in writes:
            b.lw = tok
            b.rd = []

    def op(self, eng, fn, reads=(), writes=()):
        waits = self._collect(eng, reads, writes)
        self.cnt[eng] += 1
        tok = ("e", eng, self.cnt[eng])
        self._mark(tok, reads, writes)
        self.ops[eng].append((waits, fn, None))
        return tok

    def dma(self, q, fn, semkey, reads=(), writes=()):
        waits = self._collect(q, reads, writes)
        if semkey not in self.semvals:
            self.semvals[semkey] = 0
            self.semkeys.append(semkey)
        self.semvals[semkey] += 16
        tok = ("s", semkey, self.semvals[semkey])
        self._mark(tok, reads, writes)
        self.ops[q].append((waits, fn, semkey))
        return tok

    def emit(self):
        nc = self.nc
        with contextlib.ExitStack() as st:
            esem = {e: st.enter_context(nc.semaphore("se_" + e)) for e in COMPUTE}
            ssem = {k: st.enter_context(nc.semaphore("sd_%d" % i)) for i, k in enumerate(self.semkeys)}
            block = st.enter_context(nc.Block())

            def semof(key):
                return esem[key[1]] if key[0] == "e" else ssem[key[1]]

            def replay(engobj, name, extra=None):
                for waits, fn, semkey in self.ops[name]:
                    for key, val in waits.items():
                        engobj.wait_ge(semof(key), val)
                    ins = fn(engobj)
                    if semkey is not None:
                        ins.then_inc(ssem[semkey], 16)
                    else:
                        ins.then_inc(esem[name], 1)
                if extra:
                    extra(engobj)

            def final(engobj):
                for k in self.semkeys:
                    engobj.wait_ge(ssem[k], self.semvals[k])
                for e in COMPUTE:
                    if self.cnt[e]:
                        engobj.wait_ge(esem[e], self.cnt[e])

            @block.tensor
            def _(e):
                replay(e, "pe")

            @block.vector
            def _(e):
                replay(e, "dve")

            @block.gpsimd
            def _(e):
                replay(e, "pool")

            @block.scalar
            def _(e):
                replay(e, "act")

            @block.sync
            def _(e):
                replay(e, "sync", extra=final)


DM = 1024
TO = 2048
TP = 2048
TT = TO + TP
NH = 16
CZ, CX, CB, CC, CDT, CQ, CK, CV, CF = 0, 1024, 2048, 2304, 2560, 2576, 3600, 4624, 5648
ALPHA = 2.0 ** 0.25
EPS = 1e-5
NW = 52600
NEG = -30000.0

WEIGHT_NAMES = ["w_ada", "b_ada", "w_in", "conv_w", "conv_b", "dt_bias", "a_log", "d_skip", "ssm_norm_w",
                "f_bias", "attn_norm_w", "w_out", "ln1_g", "ln1_b", "w_ff_in", "w_ff_out", "ln2_g", "ln2_b"]
WEIGHT_SHAPES = {"w_ada": [1024, 6144], "b_ada": [1, 6144], "w_in": [1024, 5664], "conv_w": [4, 1536],
                 "conv_b": [1, 1536], "dt_bias": [1, 16], "a_log": [1, 16], "d_skip": [1, 16],
                 "ssm_norm_w": [1, 1024], "f_bias": [1, 16], "attn_norm_w": [1, 1024], "w_out": [2048, 1024],
                 "ln1_g": [1, 1024], "ln1_b": [1, 1024], "w_ff_in": [1024, 4096], "w_ff_out": [4096, 1024],
                 "ln2_g": [1, 1024], "ln2_b": [1, 1024]}


def build(stop_after=99):
    nc = bass.Bass("TRN2", target_bir_lowering=False)
    D = {}
    for n, shp in [("xo", [TO, DM]), ("xp", [TP, DM]), ("c", [1, DM]), ("flags", [128, 2])]:
        D[n] = nc.dram_tensor(n, shp, F32, kind="ExternalInput").ap()
    for n in WEIGHT_NAMES:
        D[n] = nc.dram_tensor(n, WEIGHT_SHAPES[n], F32, kind="ExternalInput").ap()
    out_d = nc.dram_tensor("out", [TO, DM], F32, kind="ExternalOutput").ap()
    x1s = nc.dram_tensor("x1s", [TO, DM], F32, kind="Internal").ap()
    c3d = nc.dram_tensor("c3d", [16, 3 * TT], BF16, kind="Internal").ap()
    ymsd = nc.dram_tensor("ymsd", [128, 8 * 2048], BF16, kind="Internal").ap()
    w1b = nc.dram_tensor("w1b", [1024, 4096], BF16, kind="Internal").ap()
    w2b = nc.dram_tensor("w2b", [4096, 1024], BF16, kind="Internal").ap()
    wob = nc.dram_tensor("wob", [2048, 1024], BF16, kind="Internal").ap()
    w_in = D["w_in"]

    st = contextlib.ExitStack()
    S = st.enter_context(nc.sbuf_tensor("S", [128, NW], F32))
    PS = st.enter_context(nc.psum_tensor("PS", [128, 4096], F32))
    P = Prog(nc)

    class A:
        top = 0

    regs = []

    def register(buf, lo, hi):
        for (l2, h2, b2) in regs:
            if l2 < hi and lo < h2 and b2 is not buf:
                if b2.lw is not None:
                    buf.rd.append(b2.lw)
                buf.rd.extend(b2.rd)
        regs.append((lo, hi, buf))
        return buf

    def alias(name, lo, hi, dt=F32, pat=None, **kw):
        assert hi <= NW, (name, hi)
        v = S[:, lo:hi]
        if dt == BF16:
            v = v.bitcast(BF16)
        if pat:
            v = v.rearrange(pat, **kw)
        return register(Buf(name, v), lo, hi)

    def sbw(name, words, dt=F32, pat=None, **kw):
        off = A.top
        A.top += words
        return alias(name, off, off + words, dt, pat, **kw)

    BK = [Buf("bk%d" % i, PS[:, 512 * i:512 * (i + 1)], PS[:, 512 * i:512 * (i + 1)].bitcast(BF16)) for i in range(8)]
    bkc = [0]

    ROT = [0, 1, 2, 3, 4, 7]

    def nb():
        b = BK[ROT[bkc[0] % len(ROT)]]
        bkc[0] += 1
        return b

    def MM(bank, out, lhsT, rhs, rd, start, stop=True):
        P.op("pe", lambda e: e.matmul(out, lhsT=lhsT, rhs=rhs, start=start, stop=stop, skip_group_check=True),
             reads=rd, writes=[bank])

    def TR(bank, out, in_, ident, rd):
        P.op("pe", lambda e: e.transpose(out=out, in_=in_, identity=ident), reads=rd, writes=[bank])

    def ACT(out, in_, func, rd, wr, bias=0.0, scale=1.0, accum=None):
        if accum is None:
            P.op("act", lambda e: e.activation(out=out, in_=in_, func=func, bias=bias, scale=scale), reads=rd, writes=wr)
        else:
            P.op("act", lambda e: e.activation(out=out, in_=in_, func=func, bias=bias, scale=scale, accum_out=accum),
                 reads=rd, writes=wr)

    def TT_(eng, out, in0, in1, op, rd, wr):
        P.op(eng, lambda e: e.tensor_tensor(out=out, in0=in0, in1=in1, op=op), reads=rd, writes=wr)

    def TS(eng, out, in0, s1, s2, op0, op1, rd, wr):
        if s2 is None:
            P.op(eng, lambda e: e.tensor_scalar(out=out, in0=in0, scalar1=s1, scalar2=None, op0=op0), reads=rd, writes=wr)
        else:
            P.op(eng, lambda e: e.tensor_scalar(out=out, in0=in0, scalar1=s1, scalar2=s2, op0=op0, op1=op1),
                 reads=rd, writes=wr)

    def STT(out, in0, scalar, in1, op0, op1, rd, wr):
        P.op("dve", lambda e: e.scalar_tensor_tensor(out=out, in0=in0, scalar=scalar, in1=in1, op0=op0, op1=op1),
             reads=rd, writes=wr)

    def CP(eng, out, in_, rd, wr):
        if eng == "act":
            P.op("act", lambda e: e.copy(out=out, in_=in_), reads=rd, writes=wr)
        else:
            P.op(eng, lambda e: e.tensor_copy(out=out, in_=in_), reads=rd, writes=wr)

    def MEMSET(eng, out, val, wr):
        P.op(eng, lambda e: e.memset(out, val), writes=wr)

    def DMA(q, out, in_, key, rd=(), wr=(), slow=False):
        if slow:
            P.dma(q, lambda e: e.dma_start(out=out, in_=in_, allow_slow_non_contiguous=True), key, reads=rd, writes=wr)
        else:
            P.dma(q, lambda e: e.dma_start(out=out, in_=in_), key, reads=rd, writes=wr)

    def wload(dst_buf, dst_ap_fn, rows0, c0, ncols, nk, key, src=None):
        src = w_in if src is None else src
        step = 8
        for k0 in range(0, nk, step):
            DMA("pool", dst_ap_fn(slice(k0, k0 + step)),
                src[rows0 + k0 * 128: rows0 + (k0 + step) * 128, c0:c0 + ncols].rearrange("(k p) n -> p k n", p=128),
                key, wr=[dst_buf])

    hT_all = sbw("hT", 16384, BF16, "p (k t) -> p k t", k=8)
    hTg = [register(Buf("hT%d" % g, hT_all.t[:, :, g * 512:(g + 1) * 512]), 0, 16384) for g in range(8)]
    YM0 = A.top
    ymT_all = sbw("ymT", 16384, BF16, "p (k t) -> p k t", k=16)
    ymS = [register(Buf("ymS%d" % b, ymT_all.t[:, 0:8, b * 128:(b + 1) * 128]), YM0, YM0 + 8192) for b in range(16)]
    ymA = [register(Buf("ymA%d" % b, ymT_all.t[:, 8:16, b * 128:(b + 1) * 128]), YM0 + 8192, YM0 + 16384) for b in range(16)]
    AT0 = YM0 + 8192

    identF = sbw("identF", 128)
    tri = sbw("tri", 128)
    onesF = sbw("onesF", 128)
    maskneg = sbw("maskneg", 128)
    identB = sbw("identB", 64, BF16)
    maskB = sbw("maskB", 64, BF16)
    convw = sbw("convw", 48, F32, "p (c k) -> p c k", k=4)
    convb = sbw("convb", 12)
    dtb = sbw("dtb", 16)
    Abc = sbw("Abc", 16)
    dsk = sbw("dsk", 16)
    fbb = sbw("fbb", 16)
    ssw = sbw("ssw", 8)
    atw = sbw("atw", 8)
    flg = sbw("flg", 2)
    ccol = sbw("ccol", 8)
    csil = sbw("csil", 4, BF16)
    modp1 = sbw("modp1", 16)
    modp2 = sbw("modp2", 16)
    modp3 = sbw("modp3", 16)
    atwh = sbw("atwh", 16)
    sscol = sbw("sscol", 16)
    valid = flg[:, 0:1]

    MEMSET("pool", identF[:], 0.0, [identF])
    P.op("pool", lambda e: e.affine_select(out=identF[:], in_=identF[:], pattern=[[-1, 128]], compare_op=ALU.not_equal,
                                           fill=1.0, base=0, channel_multiplier=1), reads=[identF], writes=[identF])
    MEMSET("pool", onesF[:], 1.0, [onesF])
    MEMSET("pool", tri[:], 1.0, [tri])
    P.op("pool", lambda e: e.affine_select(out=tri[:], in_=tri[:], pattern=[[1, 128]], compare_op=ALU.is_ge,
                                           fill=0.0, base=0, channel_multiplier=-1), reads=[tri], writes=[tri])
    MEMSET("pool", maskneg[:], 0.0, [maskneg])
    P.op("pool", lambda e: e.affine_select(out=maskneg[:], in_=maskneg[:], pattern=[[1, 128]], compare_op=ALU.is_ge,
                                           fill=NEG, base=0, channel_multiplier=-1), reads=[maskneg], writes=[maskneg])
    CP("pool", identB[:], identF[:], [identF], [identB])
    CP("pool", maskB[:], maskneg[:], [maskneg], [maskB])

    for k in range(4):
        DMA("sync", convw[:, :, k], D["conv_w"][k].rearrange("(c p) -> p c", p=128), "cw%d" % k, wr=[convw], slow=True)
    DMA("sync", convb[:], D["conv_b"][0].rearrange("(c p) -> p c", p=128), "cb", wr=[convb], slow=True)
    DMA("sync", ssw[:], D["ssm_norm_w"][0].rearrange("(c p) -> p c", p=128), "ssw", wr=[ssw], slow=True)
    DMA("sync", atw[:], D["attn_norm_w"][0].rearrange("(c p) -> p c", p=128), "atw", wr=[atw], slow=True)
    DMA("sync", atwh[0:64, :], D["attn_norm_w"][0].rearrange("(h d) -> d h", d=64), "atwh", wr=[atwh], slow=True)
    DMA("sync", ccol[:], D["c"][0].rearrange("(c p) -> p c", p=128), "ccol", wr=[ccol], slow=True)
    DMA("sync", dtb[:], D["dt_bias"][0:1, :].partition_broadcast(128), "dtb", wr=[dtb])
    DMA("sync", Abc[:], D["a_log"][0:1, :].partition_broadcast(128), "alog", wr=[Abc])
    DMA("sync", dsk[:], D["d_skip"][0:1, :].partition_broadcast(128), "dsk", wr=[dsk])
    DMA("sync", fbb[:], D["f_bias"][0:1, :].partition_broadcast(128), "fbb", wr=[fbb])
    DMA("sync", flg[:], D["flags"], "flg", wr=[flg])
    ACT(Abc[:], Abc[:], AF.Exp, [Abc], [Abc])
    TS("dve", Abc[:], Abc[:], -1.0, None, ALU.mult, None, [Abc], [Abc])
    ACT(csil[:], ccol[:], AF.Silu, [ccol], [csil])

    mark0 = A.top

    def ada_cols(c0, ncols, row, wt, brow):
        DMA("sync", brow[0:1, 0:ncols], D["b_ada"][0:1, c0:c0 + ncols], "brow", wr=[brow])
        for j in range(ncols // 512):
            DMA("pool", wt[:, :, :], D["w_ada"][:, c0 + j * 512: c0 + (j + 1) * 512].rearrange("(k p) n -> p k n", p=128),
                "wada", wr=[wt])
            bank = nb()
            for k in range(8):
                MM(bank, bank[0:1, 0:512], csil[:, k:k + 1], wt[:, k, :], [csil, wt], k == 0, k == 7)
            TT_("dve", row[0:1, j * 512:(j + 1) * 512], bank[0:1, 0:512], brow[0:1, j * 512:(j + 1) * 512], ALU.add,
                [bank, brow], [row])

    def row_to_cols(row, seg0, nseg, dst, dcol0, add1_from=None):
        bank = nb()
        for s in range(nseg):
            MM(bank, bank[:, s:s + 1], row[0:1, (seg0 + s) * 128:(seg0 + s + 1) * 128], onesF[0:1, 0:1], [row, onesF], s == 0)
        CP("dve", dst[:, dcol0:dcol0 + nseg], bank[:, 0:nseg], [bank], [dst])
        if add1_from is not None:
            TS("dve", dst[:, dcol0 + add1_from:dcol0 + nseg], dst[:, dcol0 + add1_from:dcol0 + nseg], 1.0, None, ALU.add, None,
               [dst], [dst])

    A.top = mark0
    Wz = sbw("Wz", 4096, BF16, "p (k n) -> p k n", k=8)
    markE = A.top
    row = sbw("row", 2048)
    brow = sbw("brow", 2048)
    wt_ada = sbw("wt_ada", 2048, BF16, "p (k n) -> p k n", k=8)
    ada_cols(0, 2048, row, wt_ada, brow)
    row_to_cols(row, 0, 16, modp1, 0, add1_from=8)
    Wc = alias("Wc", AT0 + 0, AT0 + 6144, BF16, "p (k n) -> p k n", k=8)
    Wdt = alias("Wdt", AT0 + 6144, AT0 + 6208, BF16, "p (k n) -> p k n", k=8)
    wload(Wc, lambda k: Wc[:, k, :], 0, CX, 1536, 8, "Wc")
    wload(Wdt, lambda k: Wdt[:, k, :], 0, CDT, 16, 8, "Wdt")
    wload(Wz, lambda k: Wz[:, k, :], 0, CZ, 1024, 8, "Wz")

    xin = [sbw("xin%d" % i, 1024) for i in range(2)]
    for blk in range(32):
        src = D["xp"][blk * 128:(blk + 1) * 128, :] if blk < 16 else D["xo"][(blk - 16) * 128:(blk - 15) * 128, :]
        xt = xin[blk % 2]
        DMA("sync", xt[:], src, "xin%d" % (blk % 2), wr=[xt])
        g = blk // 4
        t0 = blk * 128
        for half in range(2):
            bank = nb()
            for j in range(4):
                k = half * 4 + j
                TR(bank, bank[:, j * 128:(j + 1) * 128], xt[:, k * 128:(k + 1) * 128], identF[:], [xt, identF])
            for j in range(4):
                k = half * 4 + j
                if j % 2 == 0:
                    ACT(hT_all.t[:, k, t0:t0 + 128], bank[:, j * 128:(j + 1) * 128], AF.Identity, [bank, modp1], [hTg[g]],
                        bias=modp1[:, k:k + 1], scale=modp1[:, 8 + k:9 + k])
                else:
                    TS("dve", hT_all.t[:, k, t0:t0 + 128], bank[:, j * 128:(j + 1) * 128], modp1[:, 8 + k:9 + k],
                       modp1[:, k:k + 1], ALU.mult, ALU.add, [bank, modp1], [hTg[g]])

    rowB = alias("rowB", YM0, YM0 + 1024)
    browB = alias("browB", YM0 + 1024, YM0 + 2048)
    wtB = [alias("wtB%d" % i, YM0 + 2048 + i * 2048, YM0 + 4096 + i * 2048, BF16, "p (k n) -> p k n", k=8) for i in range(2)]

    def ada_part_load(part):
        c0 = 2048 + part * 1024
        DMA("sync", browB[0:1, 0:1024], D["b_ada"][0:1, c0:c0 + 1024], "browB", wr=[browB])
        for j in range(2):
            DMA("pool", wtB[j][:, :, :], D["w_ada"][:, c0 + j * 512: c0 + (j + 1) * 512].rearrange("(k p) n -> p k n", p=128),
                "wadaB%d" % j, wr=[wtB[j]])

    def ada_part_compute(part):
        for j in range(2):
            bank = nb()
            for k in range(8):
                MM(bank, bank[0:1, 0:512], csil[:, k:k + 1], wtB[j][:, k, :], [csil, wtB[j]], k == 0, k == 7)
            TT_("dve", rowB[0:1, j * 512:(j + 1) * 512], bank[0:1, 0:512], browB[0:1, j * 512:(j + 1) * 512], ALU.add,
                [bank, browB], [rowB])
        dst, dcol, add1 = [(modp3, 0, 0), (modp2, 0, None), (modp2, 8, 0), (modp3, 8, 0)][part]
        row_to_cols(rowB, 0, 8, dst, dcol, add1_from=add1)

    A.top = markE
    BT = alias("BT", AT0 + 6208, AT0 + 6720, BF16, "p (g t) -> p g t", g=2)
    CT = alias("CT", AT0 + 6720, AT0 + 7232, BF16, "p (g t) -> p g t", g=2)
    Btok = alias("Btok", AT0 + 7232, AT0 + 7744, BF16, "p (j n) -> p j n", j=4)
    xs_tok = sbw("xs_tok", 4096, F32, "p (j c) -> p j c", j=4)
    ubuf = [sbw("ubuf%d" % i, 516) for i in range(2)]
    halo = sbw("halo", 36, F32, "p (c k) -> p c k", k=3)
    acc0_off = A.top
    acc = [sbw("acc%d" % i, 512) for i in range(3)]
    xsT0_off = A.top
    xsT = [sbw("xsT%d" % i, 512) for i in range(2)]
    TAv = [S[:, acc0_off:acc0_off + 1024], S[:, xsT0_off:xsT0_off + 1024]]
    TAb = [[acc[0], acc[1]], [xsT[0], xsT[1]]]
    assert xsT0_off == acc0_off + 1536
    xc = sbw("xc", 512, BF16)
    xcd = sbw("xcd", 512, BF16)
    Sst = sbw("Sst", 1024)
    Sbf = sbw("Sbf", 512, BF16)
    zs = sbw("zs", 1024)
    y1 = sbw("y1", 1024)
    y2 = sbw("y2", 1024)
    MTb = sbw("MTb", 512, BF16, "p (h l) -> p h l", h=8)
    cbm = sbw("cbm", 256)
    Ust = sbw("Ust", 128)
    sms = [sbw("sm%d" % i, 64 * 8) for i in range(1)]
    sq = sbw("sq", 8)
    print("[kernel] SSD words free:", NW - A.top, flush=True)
    MEMSET("pool", Ust[:], 1.0, [Ust])
    P.op("pool", lambda e: e.affine_select(out=Ust[:], in_=Ust[:], pattern=[[-1, 128]], compare_op=ALU.is_gt,
                                           fill=0.0, base=0, channel_multiplier=1), reads=[Ust], writes=[Ust])

    MEMSET("pool", halo[:], 0.0, [halo])
    MEMSET("pool", Sst[:], 0.0, [Sst])

    cnt = [0]
    pend_fin = []
    SPAIRS = [(BK[0], BK[1], PS[:, 0:1024]), (BK[2], BK[3], PS[:, 1024:2048])]
    for G in range(8):
        own = G >= 4
        hg = hTg[G]
        tcol = slice(G * 512, (G + 1) * 512)
        if G == 4:
            hv = halo.t.rearrange("p c k -> p (c k)")
            TS("dve", hv, hv, valid, None, ALU.mult, None, [halo, flg], [halo])
            TS("dve", Sst[:], Sst[:], valid, None, ALU.mult, None, [Sst, flg], [Sst])
        cts = list(range(12)) if (own or G == 3) else list(range(10))
        if G < 4:
            ada_part_load(G)
        pend_c = []
        pend_d = []
        for ct in cts:
            i2 = cnt[0] % 2
            i3 = cnt[0] % 3
            cnt[0] += 1
            bank = nb()
            for k in range(8):
                MM(bank, bank[:, 0:512], Wc[:, k, ct * 128:(ct + 1) * 128], hT_all.t[:, k, tcol], [Wc, hg], k == 0, k == 7)
            while len(pend_d) > 2:
                pend_d.pop(0)()
            ub = ubuf[i2]
            CP("pool", ub[:, 0:3], halo[:, ct, :], [halo], [ub])
            CP("act", ub[:, 3:515], bank[:, 0:512], [bank], [ub])
            CP("pool", halo[:, ct, :], ub[:, 512:515], [ub], [halo])
            ac = acc[i3]
            ACT(ac[:], ub[:, 0:512], AF.Identity, [ub, convw, convb], [ac], bias=convb[:, ct:ct + 1], scale=convw[:, ct, 0:1])
            while len(pend_c) > 1:
                pend_c.pop(0)()
            for kk in range(1, 4):
                STT(ac[:], ub[:, kk:kk + 512], convw[:, ct, kk:kk + 1], ac[:], ALU.mult, ALU.add, [ub, convw, ac], [ac])
            if ct < 8:
                xt_ = xsT[i2]

                def cx(xt_=xt_, ac=ac):
                    ACT(xt_[:], ac[:], AF.Silu, [ac], [xt_])

                def dx(xt_=xt_, ct=ct):
                    b2 = nb()
                    for j in range(4):
                        TR(b2, b2[:, j * 128:(j + 1) * 128], xt_[:, j * 128:(j + 1) * 128], identF[:], [xt_, identF])
                    CP("dve" if ct % 2 else "act", xs_tok[:, :, ct * 128:(ct + 1) * 128],
                       b2[:, 0:512].rearrange("p (j c) -> p j c", j=4), [b2], [xs_tok])
                pend_c.append(cx)
                pend_d.append(dx)
            elif ct < 10:
                g = ct - 8

                def cb_(g=g, ac=ac):
                    ACT(BT[:, g, :], ac[:], AF.Silu, [ac], [BT])

                def db_(g=g):
                    b2 = nb()
                    for j in range(4):
                        TR(b2, b2.tb[:, j * 128:(j + 1) * 128], BT[:, g, j * 128:(j + 1) * 128], identB[:], [BT, identB])
                    CP("dve", Btok[:, :, g * 128:(g + 1) * 128], b2.tb[:, 0:512].rearrange("p (j c) -> p j c", j=4), [b2], [Btok])
                pend_c.append(cb_)
                pend_d.append(db_)
            else:
                g = ct - 10

                def cc_(g=g, ac=ac):
                    ACT(CT[:, g, :], ac[:], AF.Silu, [ac], [CT])
                pend_c.append(cc_)

        def flush_conv():
            while pend_c:
                pend_c.pop(0)()
            while pend_d:
                pend_d.pop(0)()
        if G < 4:
            ada_part_compute(G)
        bdt = nb()
        for j in range(4):
            ctok = slice((G * 4 + j) * 128, (G * 4 + j + 1) * 128)
            for k in range(8):
                MM(bdt, bdt[:, j * 16:(j + 1) * 16], hT_all.t[:, k, ctok], Wdt[:, k, :], [hg, Wdt], (j == 0 and k == 0), k == 7)
        flush_conv()
        sm = sms[0]
        xr4, dtt4, aa4, acs4, dout4, dte4, dch4, wv4 = [sm[:, i * 64:(i + 1) * 64] for i in range(8)]

        def v3(ap):
            return ap.rearrange("p (j h) -> p j h", j=4)
        TT_("dve", v3(xr4), v3(bdt[:, 0:64]), dtb[:].unsqueeze(1).to_broadcast([128, 4, 16]), ALU.add, [bdt, dtb], [sm])
        ACT(xr4, xr4, AF.Exp, [sm], [sm])
        ACT(dtt4, xr4, AF.Ln, [sm], [sm], bias=1.0)
        TT_("dve", v3(aa4), v3(dtt4), Abc[:].unsqueeze(1).to_broadcast([128, 4, 16]), ALU.mult, [sm, Abc], [sm])
        bcs = nb()
        MM(bcs, bcs[:, 0:64], tri[:], aa4, [tri, sm], True)
        MM(bcs, bcs[:, 64:128], onesF[:], aa4, [onesF, sm], False)
        CP("dve", acs4, bcs[:, 0:64], [bcs], [sm])
        ACT(dout4, bcs[:, 0:64], AF.Exp, [bcs], [sm])
        ACT(dch4, bcs[:, 64:128], AF.Exp, [bcs], [sm])
        TT_("dve", dte4, bcs[:, 64:128], acs4, ALU.subtract, [bcs, sm], [sm])
        ACT(dte4, dte4, AF.Exp, [sm], [sm])
        TT_("dve", wv4, dtt4, dte4, ALU.mult, [sm], [sm])
        for j in range(4):
            ch = G * 4 + j
            ctok = slice(ch * 128, (ch + 1) * 128)
            xr, dtt, aa, acs, dout, dte, dch, wv = [sm[:, i * 64 + j * 16:i * 64 + (j + 1) * 16] for i in range(8)]
            xsj = xs_tok[:, j, :].rearrange("p (h d) -> p h d", h=16)
            if not own:
                TT_("pool", xcd[:].rearrange("p (h d) -> p h d", h=16), xsj, wv.unsqueeze(2).to_broadcast([128, 16, 64]),
                    ALU.mult, [xs_tok, sm], [xcd])
            if own:
                CP("act", Sbf[:], Sst[:], [Sst], [Sbf])
                prs = []
                for g in range(2):
                    TT_("dve", TAv[g].rearrange("p (h l) -> p h l", h=8), tri[:].unsqueeze(1).to_broadcast([128, 8, 128]),
                        aa[:, g * 8:(g + 1) * 8].unsqueeze(2).to_broadcast([128, 8, 128]), ALU.mult, [tri, sm], TAb[g])
                TT_("pool", xc[:].rearrange("p (h d) -> p h d", h=16), xsj, dtt.unsqueeze(2).to_broadcast([128, 16, 64]),
                    ALU.mult, [xs_tok, sm] + TAb[1], [xc])
                TT_("pool", y2[:].rearrange("p (h d) -> p h d", h=16), xsj, dsk[:].unsqueeze(2).to_broadcast([128, 16, 64]),
                    ALU.mult, [xs_tok, dsk], [y2])
                for g in range(2):
                    b0, b1, pview = SPAIRS[g]
                    MM(b0, b0[:, 0:512], Ust[:], TAv[g][:, 0:512], [Ust] + TAb[g], True)
                    MM(b1, b1[:, 0:512], Ust[:], TAv[g][:, 512:1024], [Ust] + TAb[g], True)
                    ACT(pview, pview, AF.Exp, [b0, b1], [b0, b1])
                    prs.append((b0, b1, pview))
                bcb = BK[5]
                for g in range(2):
                    MM(bcb, bcb[:, g * 128:(g + 1) * 128], BT[:, g, j * 128:(j + 1) * 128], CT[:, g, j * 128:(j + 1) * 128],
                       [BT, CT], g == 0)
                TT_("dve", cbm[:].rearrange("p (g l) -> p g l", g=2), bcb[:, 0:256].rearrange("p (g l) -> p g l", g=2),
                    tri[:].unsqueeze(1).to_broadcast([128, 2, 128]), ALU.mult, [bcb, tri], [cbm])
                for half in range(2):
                    bz = BK[4 + 3 * half]
                    for k in range(8):
                        MM(bz, bz[:, 0:512], hT_all.t[:, k, ctok], Wz[:, k, half * 512:(half + 1) * 512], [hg, Wz], k == 0, k == 7)
                    ACT(zs[:, half * 512:(half + 1) * 512], bz[:, 0:512], AF.Silu, [bz], [zs])
                for g in range(2):
                    b0, b1, pview = prs[g]
                    TT_("dve", MTb[:], pview.rearrange("p (h l) -> p h l", h=8),
                        cbm[:, g * 128:(g + 1) * 128].unsqueeze(1).to_broadcast([128, 8, 128]), ALU.mult, [b0, b1, cbm], [MTb])
                    if g == 1:
                        TT_("pool", xcd[:].rearrange("p (h d) -> p h d", h=16), xsj, wv.unsqueeze(2).to_broadcast([128, 16, 64]),
                            ALU.mult, [xs_tok, sm, MTb], [xcd])
                    byd = BK[6]
                    for hh in range(8):
                        h = g * 8 + hh
                        MM(byd, byd[:, hh * 64:(hh + 1) * 64], MTb[:, hh, :], xc[:, h * 64:(h + 1) * 64], [MTb, xc], hh == 0)
                    if g == 0:
                        while pend_fin:
                            pend_fin.pop(0)()
                    boff = BK[4 + 3 * g]
                    MM(boff, boff[:, 0:512], CT[:, g, j * 128:(j + 1) * 128], Sbf[:, g * 512:(g + 1) * 512], [CT, Sbf], True)
                    ysl = slice(g * 512, (g + 1) * 512)
                    TT_("dve", y1[:, ysl].rearrange("p (h d) -> p h d", h=8), boff[:, 0:512].rearrange("p (h d) -> p h d", h=8),
                        dout[:, g * 8:(g + 1) * 8].unsqueeze(2).to_broadcast([128, 8, 64]), ALU.mult, [boff, sm], [y1])
                    TT_("dve", y1[:, ysl], byd[:, 0:512], y1[:, ysl], ALU.add, [byd, y1], [y1])
                TT_("dve", y2[:], y2[:], y1[:], ALU.add, [y2, y1], [y2])
                TT_("dve", y2[:], y2[:], zs[:], ALU.mult, [y2, zs], [y2])
                MEMSET("pool", sq[:, 0:1], 0.0, [sq])
                ACT(y1[:], y2[:], AF.Square, [y2], [y1, sq], accum=sq[:, 0:1])
                ACT(sq[:, 1:2], sq[:, 0:1], AF.Sqrt, [sq], [sq], bias=EPS, scale=1.0 / 1024.0)
                P.op("dve", lambda e: e.reciprocal(out=sq[:, 2:3], in_=sq[:, 1:2]), reads=[sq], writes=[sq])
                TS("dve", y1[:], y2[:], sq[:, 2:3], None, ALU.mult, None, [y2, sq], [y1])

                def fin(ob=ch - 16):
                    for half in range(2):
                        bt_ = BK[4 + 3 * half]
                        for jj in range(4):
                            kt = half * 4 + jj
                            TR(bt_, bt_[:, jj * 128:(jj + 1) * 128], y1[:, kt * 128:(kt + 1) * 128], identF[:], [y1, identF])
                        for jj in range(4):
                            kt = half * 4 + jj
                            ACT(ymT_all.t[:, kt, ob * 128:(ob + 1) * 128], bt_[:, jj * 128:(jj + 1) * 128], AF.Identity,
                                [bt_, ssw], [ymS[ob]], scale=ssw[:, kt:kt + 1])
                pend_fin.append(fin)
            for g in range(2):
                bs = nb()
                MM(bs, bs[:, 0:512], Btok[:, j, g * 128:(g + 1) * 128], xcd[:, g * 512:(g + 1) * 512], [Btok, xcd], True)
                ssl = slice(g * 512, (g + 1) * 512)
                TT_("pool", Sst[:, ssl].rearrange("p (h d) -> p h d", h=8), Sst[:, ssl].rearrange("p (h d) -> p h d", h=8),
                    dch[:, g * 8:(g + 1) * 8].unsqueeze(2).to_broadcast([128, 8, 64]), ALU.mult, [Sst, sm], [Sst])
                TT_("dve", Sst[:, ssl], bs[:, 0:512], Sst[:, ssl], ALU.add, [bs, Sst], [Sst])
    while pend_fin:
        pend_fin.pop(0)()

    A.top = mark0
    markF = A.top
    DMA("sync", ymsd.rearrange("p (k t) -> p k t", k=8), ymT_all.t[:, 0:8, :], "ymsp", rd=ymS)
    KaS = [[sbw("Ka0%d" % i, 2048, BF16) for i in range(2)],
           [alias("Ka1%d" % i, YM0 + i * 2048, YM0 + (i + 1) * 2048, BF16) for i in range(2)]]
    VaS = [sbw("Va0", 4096, BF16, "p (b h d) -> p b h d", b=32, h=2),
           alias("Va1", YM0 + 4096, YM0 + 8192, BF16, "p (b h d) -> p b h d", b=32, h=2)]
    QaS = [[sbw("Qa0%d" % i, 1024, BF16) for i in range(2)], None]
    Wqkv = sbw("Wqkv", 1536, BF16, "p (k n) -> p k n", k=8)
    pT = [sbw("pT%d" % i, 512, BF16) for i in range(3)]
    vts = [sbw("vts%d" % i, 256, BF16) for i in range(2)]
    print("[kernel] attention words free:", NW - A.top, flush=True)
    markF2 = A.top
    Wf = sbw("Wf", 64, BF16, "p (k n) -> p k n", k=8)
    lfa = sbw("lfa", 512)
    cumT = sbw("cumT", 512)
    r1 = sbw("r1", 512)
    c3s = sbw("c3s", 768, BF16, "p (r t) -> p r t", r=3)
    carry = sbw("carry", 1)

    wload(Wf, lambda k: Wf[:, k, :], 0, CF, 16, 8, "Wf")
    MEMSET("dve", carry[:], 0.0, [carry])
    MEMSET("pool", sscol[:], 0.0, [sscol])
    bf = nb()
    for blk in range(32):
        for k in range(8):
            MM(bf, bf[:, blk * 16:(blk + 1) * 16], hT_all.t[:, k, blk * 128:(blk + 1) * 128], Wf[:, k, :], [hTg[blk // 4], Wf],
               (blk == 0 and k == 0), k == 7)
    TT_("dve", lfa[:].rearrange("p (b h) -> p b h", b=32), bf[:, 0:512].rearrange("p (b h) -> p b h", b=32),
        fbb[:].unsqueeze(1).to_broadcast([128, 32, 16]), ALU.add, [bf, fbb], [lfa])
    ACT(lfa[:], lfa[:], AF.Exp, [lfa], [lfa], scale=-1.0)
    ACT(lfa[:], lfa[:], AF.Ln, [lfa], [lfa], bias=1.0)
    TS("dve", lfa[:], lfa[:], -1.0, None, ALU.mult, None, [lfa], [lfa])
    for q4 in range(8):
        bc = nb()
        for bq in range(4):
            blk = q4 * 4 + bq
            MM(bc, bc[0:16, bq * 128:(bq + 1) * 128], lfa[:, blk * 16:(blk + 1) * 16], tri[:], [lfa, tri], bq == 0)
        for bq in range(4):
            TS("dve", cumT[0:16, bq * 128:(bq + 1) * 128], bc[0:16, bq * 128:(bq + 1) * 128], carry[0:16, 0:1], None, ALU.add, None,
               [bc, carry], [cumT])
            CP("dve", carry[0:16, 0:1], cumT[0:16, bq * 128 + 127:bq * 128 + 128], [cumT], [carry])
        CP("dve", c3s[0:16, 0, :], cumT[0:16, :], [cumT], [c3s])
        TT_("dve", r1[0:16, :], cumT[0:16, :], c3s[0:16, 0, :], ALU.subtract, [cumT, c3s], [r1])
        CP("dve", c3s[0:16, 1, :], r1[0:16, :], [r1], [c3s])
        TT_("dve", r1[0:16, :], r1[0:16, :], c3s[0:16, 1, :], ALU.subtract, [r1, c3s], [r1])
        CP("dve", c3s[0:16, 2, :], r1[0:16, :], [r1], [c3s])
        DMA("sync", c3d.rearrange("h (r t) -> h r t", r=3)[:, :, q4 * 512:(q4 + 1) * 512], c3s[0:16, :, :], "c3w", rd=[c3s])
    c3tok = Buf("c3dram", None)
    c3tok.lw = ("s", "c3w", P.semvals["c3w"])
    c3v = c3d.rearrange("h (r t) -> h r t", r=3)
    A.top = markF2
    rbc = sbw("rbc", 512)
    yv = [sbw("yv%d" % i, 512) for i in range(2)]
    ysq = sbw("ysq", 512)
    QaS[1] = [sbw("Qa1%d" % i, 1024, BF16) for i in range(2)]

    for Va in VaS:
        MEMSET("pool", Va[:, :, :, 64:128], 1.0, [Va])
        TS("pool", Va[:, 0:16, :, 64:128], Va[:, 0:16, :, 64:128], valid, None, ALU.mult, None, [Va, flg], [Va])
    for sl in range(2):
        for i in range(2):
            MEMSET("pool", KaS[sl][i][64:128, :], 0.0, [KaS[sl][i]])
            MEMSET("pool", KaS[sl][i][64:70, :], 1.0, [KaS[sl][i]])
            MEMSET("pool", QaS[sl][i][64:128, :], 0.0, [QaS[sl][i]])
            MEMSET("pool", QaS[sl][i][64:70, :], -1.0, [QaS[sl][i]])

    pcnt = [0]
    ecnt = [0]
    fcnt = [0]
    deferred = []
    filler = []
    tcnt = [0]

    def tick():
        for d_ in deferred:
            d_[0] -= 1
        while deferred and deferred[0][0] <= 0:
            deferred.pop(0)[1]()
        tcnt[0] += 1
        if filler and tcnt[0] % 5 == 0:
            filler.pop(0)()

    def flush():
        while deferred:
            deferred.pop(0)[1]()

    def fbank():
        b = BK[4 + 3 * (fcnt[0] % 2)]
        fcnt[0] += 1
        return b

    def proj_units(hp):
        sl = hp % 2
        Ka, Qa, Va, wq = KaS[sl], QaS[sl], VaS[sl], Wqkv
        units = []

        def u_load():
            for i, c0 in enumerate((CQ, CK, CV)):
                wload(wq, (lambda i: (lambda k: wq[:, k, i * 128:(i + 1) * 128]))(i), 0, c0 + hp * 128, 128, 8, "Wq")
            if hp == 0:
                for r0 in range(0, 1024, 128):
                    DMA("pool", w1b[r0:r0 + 128, :], D["w_ff_in"][r0:r0 + 128, :], "precast")
                for r0 in range(0, 2048, 128):
                    DMA("pool", wob[r0:r0 + 128, :], D["w_out"][r0:r0 + 128, :], "precast")
                for r0 in range(0, 4096, 128):
                    DMA("pool", w2b[r0:r0 + 128, :], D["w_ff_out"][r0:r0 + 128, :], "precast")
            for hh in range(2):
                h = hp * 2 + hh
                DMA("sync", Ka[hh][67:70, :], c3v[h, :, :], "ka%d%d" % (sl, hh), rd=[c3tok], wr=[Ka[hh]])
                DMA("sync", Qa[hh][64:67, :], c3v[h, :, TP:TT], "qa%d%d" % (sl, hh), rd=[c3tok], wr=[Qa[hh]])
        units.append(u_load)

        def u_k(G):
            bk_ = fbank()
            for k in range(8):
                MM(bk_, bk_[:, 0:512], wq[:, k, 128:256], hT_all.t[:, k, G * 512:(G + 1) * 512], [wq, hTg[G]], k == 0, k == 7)
            CP("dve", Ka[0][0:64, G * 512:(G + 1) * 512], bk_[0:64, 0:512], [bk_], [Ka[0]])
            CP("dve", Ka[1][0:64, G * 512:(G + 1) * 512], bk_[64:128, 0:512], [bk_], [Ka[1]])

        def u_q(G):
            bq_ = fbank()
            for k in range(8):
                MM(bq_, bq_[:, 0:512], wq[:, k, 0:128], hT_all.t[:, k, TP + G * 512:TP + (G + 1) * 512], [wq, hTg[4 + G]],
                   k == 0, k == 7)
            TS("dve", Qa[0][0:64, G * 512:(G + 1) * 512], bq_[0:64, 0:512], 0.125, None, ALU.mult, None, [bq_], [Qa[0]])
            TS("dve", Qa[1][0:64, G * 512:(G + 1) * 512], bq_[64:128, 0:512], 0.125, None, ALU.mult, None, [bq_], [Qa[1]])

        def u_v(G):
            bv = fbank()
            for k in range(8):
                MM(bv, bv[:, 0:512], wq[:, k, 256:384], hT_all.t[:, k, G * 512:(G + 1) * 512], [wq, hTg[G]], k == 0, k == 7)
            vt = vts[G % 2]
            CP("dve", vt[:, 0:512], bv[:, 0:512], [bv], [vt])
            b2 = fbank()
            for j in range(4):
                TR(b2, b2.tb[:, j * 128:(j + 1) * 128], vt[:, j * 128:(j + 1) * 128], identB[:], [vt, identB])
            src = b2.tb[:, 0:512].rearrange("p (b h d) -> p b h d", b=4, h=2)
            if G < 4:
                TS("dve", Va[:, G * 4:(G + 1) * 4, :, 0:64], src, valid, None, ALU.mult, None, [b2, flg], [Va])
            else:
                CP("dve", Va[:, G * 4:(G + 1) * 4, :, 0:64], src, [b2], [Va])
        for G in range(8):
            units.append((lambda G: (lambda: u_k(G)))(G))
        for G in range(4):
            units.append((lambda G: (lambda: u_q(G)))(G))
        for G in range(8):
            units.append((lambda G: (lambda: u_v(G)))(G))
        return units

    PAIRS = [(BK[0], BK[1], PS[:, 0:1024]), (BK[2], BK[3], PS[:, 1024:2048])]
    for u in proj_units(0):
        u()
    for hp in range(NH // 2):
        sl = hp % 2
        Va = VaS[sl]
        if hp + 1 < NH // 2:
            filler.extend(proj_units(hp + 1))
        for hh in range(2):
            h = hp * 2 + hh
            ka, qa = KaS[sl][hh], QaS[sl][hh]
            for G in range(4):
                po = BK[5 + ecnt[0] % 2]
                npre = 16 + 4 * G
                nkb = npre + 4
                steps = [([kb, kb + 1], 0, 512) for kb in range(0, npre, 2)] + \
                        [([npre + r], r, (4 - r) * 128) for r in range(4)]
                pend = None
                for si in range(len(steps) + 1):
                    cur = None
                    if si < len(steps):
                        kbs, q0, ncol = steps[si]
                        pt = pT[pcnt[0] % 3]
                        if len(kbs) == 2:
                            b0, b1, pview = PAIRS[pcnt[0] % 2]
                            for bi, kb in zip((b0, b1), kbs):
                                MM(bi, bi[:, 0:512], ka[:, kb * 128:(kb + 1) * 128], qa[:, G * 512:(G + 1) * 512],
                                   [ka, qa], True, True)
                            ACT(pt[:, 0:1024], pview, AF.Exp, [b0, b1], [pt])
                        else:
                            kb = kbs[0]
                            bs_ = BK[4 + 3 * (q0 % 2)]
                            MM(bs_, bs_[:, 0:ncol], ka[:, kb * 128:(kb + 1) * 128],
                               qa[:, G * 512 + q0 * 128:(G + 1) * 512], [ka, qa], True, False)
                            MM(bs_, bs_[:, 0:128], identB[:], maskB[:], [identB, maskB], False, True)
                            ACT(pt[:, 0:ncol], bs_[:, 0:ncol], AF.Exp, [bs_], [pt])
                        pcnt[0] += 1
                        cur = (kbs, q0, ncol, pt)
                    if pend is not None:
                        kbs_, q0_, ncol_, pt_ = pend
                        for i_, kb_ in enumerate(kbs_):
                            MM(po, po[:, q0_ * 128:512], Va[:, kb_, hh, :], pt_[:, i_ * 512:i_ * 512 + ncol_], [Va, pt_],
                               kb_ == 0, kb_ == nkb - 1)
                    pend = cur
                    tick()
                P.op("dve", (lambda po_: (lambda e: e.reciprocal(out=rbc[0:64, :], in_=po_[64:128, 0:512])))(po),
                     reads=[po], writes=[rbc])
                yv_ = yv[ecnt[0] % 2]
                TT_("dve", yv_[0:64, :], po[0:64, 0:512], rbc[0:64, :], ALU.mult, [po, rbc], [yv_])
                pr = (h % 2) * 64
                TT_("pool", ysq[0:64, :], yv_[0:64, :], yv_[0:64, :], ALU.mult, [yv_], [ysq])
                TS("pool", ymT_all.t[pr:pr + 64, 8 + h // 2, G * 512:(G + 1) * 512], yv_[0:64, :], atwh[0:64, h:h + 1], None,
                   ALU.mult, None, [yv_, atwh], [ymA[G * 4 + j] for j in range(4)])

                def ep2(G=G, e_=ecnt[0]):
                    bss = BK[4 + 3 * (e_ % 2)]
                    for j4 in range(4):
                        MM(bss, bss[:, j4:j4 + 1], ysq[0:64, j4 * 128:(j4 + 1) * 128], onesF[0:64, 0:1], [onesF, ysq], j4 == 0)
                    TT_("dve", sscol[:, G * 4:(G + 1) * 4], bss[:, 0:4], sscol[:, G * 4:(G + 1) * 4], ALU.add,
                        [bss, sscol], [sscol])

                deferred.append([8, ep2])
                ecnt[0] += 1
        while filler:
            filler.pop(0)()
    flush()

    A.top = markF
    g1b = alias("g1b", 0, 1024)
    g2b = alias("g2b", 1024, 2048)
    lnb = alias("lnb", 2048, 6144, F32, "p (v n) -> p v n", v=4)
    h2Tb = alias("h2Tb", 6144, 14336, BF16, "p (k t) -> p k t", k=8)
    xr_ = [sbw("xr%d" % i, 1024) for i in range(2)]
    t1s = [sbw("t1_%d" % i, 1024) for i in range(2)]
    t2 = sbw("t2", 1024)
    sts = [sbw("st%d" % i, 16) for i in range(2)]
    t1 = t1s[0]
    st_ = sts[0]
    markGH = A.top
    Wo = sbw("Wo", 8192, BF16, "p (k n) -> p k n", k=16)
    dg = [sbw("dg%d" % i, 128) for i in range(2)]

    ymS = [register(Buf("ymS2_%d" % b, ymT_all.t[:, 0:8, b * 128:(b + 1) * 128]), YM0, YM0 + 8192) for b in range(16)]
    spill = Buf("ymsd", None)
    spill.lw = ("s", "ymsp", P.semvals["ymsp"])
    DMA("sync", ymT_all.t[:, 0:8, :], ymsd.rearrange("p (k t) -> p k t", k=8), "ymrl", rd=[spill], wr=ymS)
    pctok = Buf("precast", None)
    pctok.lw = ("s", "precast", P.semvals["precast"])
    for v, nme in enumerate(["ln1_g", "ln1_b", "ln2_g", "ln2_b"]):
        DMA("sync", lnb[:, v, :], D[nme][0:1, :].partition_broadcast(128), "lnb%d" % v, wr=[lnb])
    for k0 in (0, 8):
        DMA("sync", Wo[:, k0:k0 + 8, :], wob[k0 * 128:(k0 + 8) * 128, :].rearrange("(k p) n -> p k n", p=128), "Wo",
            rd=[pctok], wr=[Wo])

    def col_bcast(c0, dst):
        for half in range(2):
            bank = nb()
            for j in range(4):
                k = half * 4 + j
                d_ = dg[k % 2]
                TS("dve", d_[:], identF[:], modp3[:, c0 + k:c0 + k + 1], None, ALU.mult, None, [identF, modp3], [d_])
                MM(bank, bank[:, j * 128:(j + 1) * 128], onesF[:], d_[:], [onesF, d_], j == 0)
            CP("act", dst[:, half * 512:(half + 1) * 512], bank[:, 0:512], [bank], [dst])

    col_bcast(0, g1b)
    col_bcast(8, g2b)

    def layer_norm(src, dst, gi, st_):
        MEMSET("dve", st_[:, 0:2], 0.0, [st_])
        ACT(t2[:], src[:], AF.Identity, [src], [st_], accum=st_[:, 0:1])
        ACT(t2[:], src[:], AF.Square, [src], [st_], accum=st_[:, 1:2])
        TS("dve", st_[:, 2:3], st_[:, 0:1], 1.0 / 1024.0, None, ALU.mult, None, [st_], [st_])
        TT_("dve", st_[:, 3:4], st_[:, 2:3], st_[:, 2:3], ALU.mult, [st_], [st_])
        STT(st_[:, 4:5], st_[:, 1:2], 1.0 / 1024.0, st_[:, 3:4], ALU.mult, ALU.subtract, [st_], [st_])
        ACT(st_[:, 5:6], st_[:, 4:5], AF.Sqrt, [st_], [st_], bias=EPS)
        P.op("dve", lambda e: e.reciprocal(out=st_[:, 6:7], in_=st_[:, 5:6]), reads=[st_], writes=[st_])
        TS("dve", dst[:], src[:], st_[:, 2:3], st_[:, 6:7], ALU.subtract, ALU.mult, [src, st_], [dst])
        TT_("dve", dst[:], dst[:], lnb[:, gi, :], ALU.mult, [dst, lnb], [dst])
        TT_("dve", dst[:], dst[:], lnb[:, gi + 1, :], ALU.add, [dst, lnb], [dst])

    x1toks = [Buf("x1dram%d" % i, None) for i in range(2)]
    pend_g = []
    for ob in range(16):
        xt = xr_[ob % 2]
        t1 = t1s[ob % 2]
        st_ = sts[ob % 2]
        DMA("sync", xt[:], D["xo"][ob * 128:(ob + 1) * 128, :], "xr%d" % (ob % 2), wr=[xt])
        ACT(st_[:, 9:10], sscol[:, ob:ob + 1], AF.Sqrt, [sscol], [st_], bias=EPS, scale=1.0 / 1024.0)
        P.op("dve", lambda e, st_=st_: e.reciprocal(out=st_[:, 10:11], in_=st_[:, 9:10]), reads=[st_], writes=[st_])
        for half in range(2):
            b1 = nb()
            for kt in range(8):
                MM(b1, b1[:, 0:512], ymT_all.t[:, kt, ob * 128:(ob + 1) * 128], Wo[:, kt, half * 512:(half + 1) * 512],
                   [ymS[ob], Wo], kt == 0, kt == 7)
            b2 = nb()
            for kt in range(8, 16):
                MM(b2, b2[:, 0:512], ymT_all.t[:, kt, ob * 128:(ob + 1) * 128], Wo[:, kt, half * 512:(half + 1) * 512],
                   [ymA[ob], Wo], kt == 8, kt == 15)
            hs = slice(half * 512, (half + 1) * 512)
            CP("act", t1[:, hs], b1[:, 0:512], [b1], [t1])
            STT(t1[:, hs], b2[:, 0:512], st_[:, 10:11], t1[:, hs], ALU.mult, ALU.add, [b2, st_, t1], [t1])
        while pend_g:
            pend_g.pop(0)()
        TT_("dve", t1[:], t1[:], g1b[:], ALU.mult, [t1, g1b], [t1])
        STT(t1[:], xt[:], ALPHA, t1[:], ALU.mult, ALU.add, [xt, t1], [t1])
        layer_norm(t1, xt, 0, st_)
        DMA("sync", x1s[ob * 128:(ob + 1) * 128, :], xt[:], "x1w%d" % (ob % 2), rd=[xt])

        def trg(xt=xt, ob=ob):
            for half in range(2):
                bt_ = nb()
                for jj in range(4):
                    k = half * 4 + jj
                    TR(bt_, bt_[:, jj * 128:(jj + 1) * 128], xt[:, k * 128:(k + 1) * 128], identF[:], [xt, identF])
                for jj in range(4):
                    k = half * 4 + jj
                    if jj % 2 == 0:
                        ACT(h2Tb[:, k, ob * 128:(ob + 1) * 128], bt_[:, jj * 128:(jj + 1) * 128], AF.Identity, [bt_, modp2],
                            [h2Tb], bias=modp2[:, k:k + 1], scale=modp2[:, 8 + k:9 + k])
                    else:
                        TS("dve", h2Tb[:, k, ob * 128:(ob + 1) * 128], bt_[:, jj * 128:(jj + 1) * 128], modp2[:, 8 + k:9 + k],
                           modp2[:, k:k + 1], ALU.mult, ALU.add, [bt_, modp2], [h2Tb])
        pend_g.append(trg)
    while pend_g:
        pend_g.pop(0)()
    for i in range(2):
        x1toks[i].lw = ("s", "x1w%d" % i, P.semvals["x1w%d" % i])

    W2 = alias("W2", YM0, YM0 + 16384, BF16, "p (f n) -> p f n", f=32)
    for f0 in range(0, 32, 8):
        DMA("sync", W2[:, f0:f0 + 8, :], w2b[f0 * 128:(f0 + 8) * 128, :].rearrange("(f p) n -> p f n", p=128), "W2",
            rd=[pctok], wr=[W2])
    A.top = markGH
    a1T = sbw("a1T", 8192, BF16, "p (f t) -> p f t", f=32)
    W1t = [sbw("W1t%d" % i, 2048, BF16, "p (k n) -> p k n", k=8) for i in range(2)]
    rl = [sbw("rl%d" % i, 512) for i in range(2)]
    wcnt = [0]
    for G in range(4):
        for c4 in range(8):
            w1 = W1t[wcnt[0] % len(W1t)]
            DMA("sync", w1[:, :, :], w1b[:, c4 * 512:(c4 + 1) * 512].rearrange("(k p) n -> p k n", p=128),
                "W1t%d" % (wcnt[0] % len(W1t)), rd=[pctok], wr=[w1])
            wcnt[0] += 1
            for f4 in range(4):
                f = c4 * 4 + f4
                bf_ = nb()
                for k in range(8):
                    MM(bf_, bf_[:, 0:512], w1[:, k, f4 * 128:(f4 + 1) * 128], h2Tb[:, k, G * 512:(G + 1) * 512], [w1, h2Tb],
                       k == 0, k == 7)
                r_ = rl[f % 2]
                ACT(r_[:], bf_[:, 0:512], AF.Relu, [bf_], [r_])
                TT_("dve", a1T[:, f, :], r_[:], r_[:], ALU.mult, [r_], [a1T])
        for tb in range(4):
            ob = G * 4 + tb
            xt = xr_[ob % 2]
            t1 = t1s[ob % 2]
            st_ = sts[ob % 2]
            DMA("sync", xt[:], x1s[ob * 128:(ob + 1) * 128, :], "xr%d" % (ob % 2), rd=x1toks, wr=[xt])
            for half in range(2):
                bo = nb()
                for f in range(32):
                    MM(bo, bo[:, 0:512], a1T[:, f, tb * 128:(tb + 1) * 128], W2[:, f, half * 512:(half + 1) * 512], [a1T, W2],
                       f == 0, f == 31)
                hs = slice(half * 512, (half + 1) * 512)
                TT_("dve", t1[:, hs], bo[:, 0:512], g2b[:, hs], ALU.mult, [bo, g2b], [t1])
            STT(t1[:], xt[:], ALPHA, t1[:], ALU.mult, ALU.add, [xt, t1], [t1])
            layer_norm(t1, xt, 2, st_)
            DMA("sync", out_d[ob * 128:(ob + 1) * 128, :], xt[:], "ow%d" % (ob % 2), rd=[xt])

    print("[kernel] ops:", {k: len(v) for k, v in P.ops.items()}, "cnt:", P.cnt, "nsem:", len(P.semkeys), flush=True)
    P.emit()
    st.close()
    return nc


_NC = [None]


def kernel(**inputs):
    x = np.ascontiguousarray(np.asarray(inputs["x"], dtype=np.float32))
    c = np.asarray(inputs["c"], dtype=np.float32)
    if _NC[0] is None:
        _NC[0] = build()
    nc = _NC[0]
    wmap = {}
    for n in WEIGHT_NAMES:
        wmap[n] = np.ascontiguousarray(np.asarray(inputs[n], dtype=np.float32).reshape(WEIGHT_SHAPES[n]))
    in_maps = []
    for core in range(8):
        b, h = core // 2, core % 2
        m = dict(wmap)
        m["xo"] = np.ascontiguousarray(x[b, h * TO:(h + 1) * TO])
        m["xp"] = np.ascontiguousarray(x[b, 0:TP])
        m["c"] = np.ascontiguousarray(c[b:b + 1])
        fl = np.zeros((128, 2), np.float32)
        fl[:, 0] = float(h)
        m["flags"] = fl
        in_maps.append(m)
    res = run_bass_kernel_spmd(nc, in_maps, core_ids=list(range(8)))
    out = np.empty((4, 4096, DM), np.float32)
    for core in range(8):
        b, h = core // 2, core % 2
        out[b, h * TO:(h + 1) * TO] = res.results[core]["out"]
    return out
```

```python
import contextlib
import numpy as np
import concourse.bass as bass
import concourse.mybir as mybir
from concourse.bass_utils import run_bass_kernel_spmd

F32 = mybir.dt.float32
BF16 = mybir.dt.bfloat16
AF = mybir.ActivationFunctionType
ALU = mybir.AluOpType

COMPUTE = ("pe", "act", "dve", "pool")
SELF_SYNC = ("act", "dve", "pool")
QUEUES = COMPUTE + ("sync",)


class Buf:
    __slots__ = ("name", "t", "tb", "lw", "rd")

    def __init__(self, name, t, tb=None):
        self.name = name
        self.t = t
        self.tb = tb
        self.lw = None
        self.rd = []

    def __getitem__(self, k):
        return self.t[k]


class Prog:
    def __init__(self, nc):
        self.nc = nc
        self.ops = {e: [] for e in QUEUES}
        self.cnt = {e: 0 for e in COMPUTE}
        self.waited = {e: {} for e in QUEUES}
        self.semvals = {}
        self.semkeys = []

    def _need(self, eng, waits, tok):
        if tok is None:
            return
        if tok[0] == "e":
            _, e2, idx = tok
            if e2 == eng and eng not in SELF_SYNC:
                return
            key = ("e", e2)
            val = idx
        else:
            _, sk, val = tok
            key = ("s", sk)
        if self.waited[eng].get(key, 0) >= val:
            return
        if waits.get(key, 0) < val:
            waits[key] = val

    def _collect(self, eng, reads, writes):
        waits = {}
        for b in reads:
            self._need(eng, waits, b.lw)
        for b in writes:
            self._need(eng, waits, b.lw)
            for tok in b.rd:
                self._need(eng, waits, tok)
        for k, v in waits.items():
            self.waited[eng][k] = v
        return waits

    def _mark(self, tok, reads, writes):
        for b in reads:
            b.rd.append(tok)
            if len(b.rd) > 12:
                best = {}
                for t in b.rd:
                    k = (t[0], t[1])
                    if k not in best or best[k][2] < t[2]:
                        best[k] = t
                b.rd = list(best.values())
        for b in writes:
            b.lw = tok
            b.rd = []

    def op(self, eng, fn, reads=(), writes=()):
        waits = self._collect(eng, reads, writes)
        self.cnt[eng] += 1
        tok = ("e", eng, self.cnt[eng])
        self._mark(tok, reads, writes)
        self.ops[eng].append((waits, fn, None))
        return tok

    def dma(self, q, fn, semkey, reads=(), writes=()):
        waits = self._collect(q, reads, writes)
        if semkey not in self.semvals:
            self.semvals[semkey] = 0
            self.semkeys.append(semkey)
        self.semvals[semkey] += 16
        tok = ("s", semkey, self.semvals[semkey])
        self._mark(tok, reads, writes)
        self.ops[q].append((waits, fn, semkey))
        return tok

    def emit(self):
        nc = self.nc
        with contextlib.ExitStack() as st:
            esem = {e: st.enter_context(nc.semaphore("se_" + e)) for e in COMPUTE}
            ssem = {k: st.enter_context(nc.semaphore("sd_%d" % i)) for i, k in enumerate(self.semkeys)}
            block = st.enter_context(nc.Block())

            def semof(key):
                return esem[key[1]] if key[0] == "e" else ssem[key[1]]

            def replay(engobj, name, extra=None):
                for waits, fn, semkey in self.ops[name]:
                    for key, val in waits.items():
                        engobj.wait_ge(semof(key), val)
                    ins = fn(engobj)
                    if semkey is not None:
                        ins.then_inc(ssem[semkey], 16)
                    else:
                        ins.then_inc(esem[name], 1)
                if extra:
                    extra(engobj)

            def final(engobj):
                for k in self.semkeys:
                    engobj.wait_ge(ssem[k], self.semvals[k])
                for e in COMPUTE:
                    if self.cnt[e]:
                        engobj.wait_ge(esem[e], self.cnt[e])

            @block.tensor
            def _(e):
                replay(e, "pe")

            @block.vector
            def _(e):
                replay(e, "dve")

            @block.gpsimd
            def _(e):
                replay(e, "pool")

            @block.scalar
            def _(e):
                replay(e, "act")

            @block.sync
            def _(e):
                replay(e, "sync", extra=final)


DM = 1024
TO = 2048
TP = 2048
TT = TO + TP
NH = 16
CZ, CX, CB, CC, CDT, CQ, CK, CV, CF = 0, 1024, 2048, 2304, 2560, 2576, 3600, 4624, 5648
ALPHA = 2.0 ** 0.25
EPS = 1e-5
NW = 52600
NEG = -30000.0

WEIGHT_NAMES = ["w_ada", "b_ada", "w_in", "conv_w", "conv_b", "dt_bias", "a_log", "d_skip", "ssm_norm_w",
                "f_bias", "attn_norm_w", "w_out", "ln1_g", "ln1_b", "w_ff_in", "w_ff_out", "ln2_g", "ln2_b"]
WEIGHT_SHAPES = {"w_ada": [1024, 6144], "b_ada": [1, 6144], "w_in": [1024, 5664], "conv_w": [4, 1536],
                 "conv_b": [1, 1536], "dt_bias": [1, 16], "a_log": [1, 16], "d_skip": [1, 16],
                 "ssm_norm_w": [1, 1024], "f_bias": [1, 16], "attn_norm_w": [1, 1024], "w_out": [2048, 1024],
                 "ln1_g": [1, 1024], "ln1_b": [1, 1024], "w_ff_in": [1024, 4096], "w_ff_out": [4096, 1024],
                 "ln2_g": [1, 1024], "ln2_b": [1, 1024]}


def build(stop_after=99):
    nc = bass.Bass("TRN2", target_bir_lowering=False)
    D = {}
    for n, shp in [("xo", [TO, DM]), ("xp", [TP, DM]), ("c", [1, DM]), ("flags", [128, 2])]:
        D[n] = nc.dram_tensor(n, shp, F32, kind="ExternalInput").ap()
    for n in WEIGHT_NAMES:
        D[n] = nc.dram_tensor(n, WEIGHT_SHAPES[n], F32, kind="ExternalInput").ap()
    out_d = nc.dram_tensor("out", [TO, DM], F32, kind="ExternalOutput").ap()
    x1s = nc.dram_tensor("x1s", [TO, DM], F32, kind="Internal").ap()
    c3d = nc.dram_tensor("c3d", [16, 3 * TT], BF16, kind="Internal").ap()
    ymsd = nc.dram_tensor("ymsd", [128, 8 * 2048], BF16, kind="Internal").ap()
    w1b = nc.dram_tensor("w1b", [1024, 4096], BF16, kind="Internal").ap()
    w2b = nc.dram_tensor("w2b", [4096, 1024], BF16, kind="Internal").ap()
    wob = nc.dram_tensor("wob", [2048, 1024], BF16, kind="Internal").ap()
    w_in = D["w_in"]

    st = contextlib.ExitStack()
    S = st.enter_context(nc.sbuf_tensor("S", [128, NW], F32))
    PS = st.enter_context(nc.psum_tensor("PS", [128, 4096], F32))
    P = Prog(nc)

    class A:
        top = 0

    regs = []

    def register(buf, lo, hi):
        for (l2, h2, b2) in regs:
            if l2 < hi and lo < h2 and b2 is not buf:
                if b2.lw is not None:
                    buf.rd.append(b2.lw)
                buf.rd.extend(b2.rd)
        regs.append((lo, hi, buf))
        return buf

    def alias(name, lo, hi, dt=F32, pat=None, **kw):
        assert hi <= NW, (name, hi)
        v = S[:, lo:hi]
        if dt == BF16:
            v = v.bitcast(BF16)
        if pat:
            v = v.rearrange(pat, **kw)
        return register(Buf(name, v), lo, hi)

    def sbw(name, words, dt=F32, pat=None, **kw):
        off = A.top
        A.top += words
        return alias(name, off, off + words, dt, pat, **kw)

    BK = [Buf("bk%d" % i, PS[:, 512 * i:512 * (i + 1)], PS[:, 512 * i:512 * (i + 1)].bitcast(BF16)) for i in range(8)]
    bkc = [0]

    ROT = [0, 1, 2, 3, 4, 7]

    def nb():
        b = BK[ROT[bkc[0] % len(ROT)]]
        bkc[0] += 1
        return b

    def MM(bank, out, lhsT, rhs, rd, start, stop=True):
        P.op("pe", lambda e: e.matmul(out, lhsT=lhsT, rhs=rhs, start=start, stop=stop, skip_group_check=True),
             reads=rd, writes=[bank])

    def TR(bank, out, in_, ident, rd):
        P.op("pe", lambda e: e.transpose(out=out, in_=in_, identity=ident), reads=rd, writes=[bank])

    def ACT(out, in_, func, rd, wr, bias=0.0, scale=1.0, accum=None):
        if accum is None:
            P.op("act", lambda e: e.activation(out=out, in_=in_, func=func, bias=bias, scale=scale), reads=rd, writes=wr)
        else:
            P.op("act", lambda e: e.activation(out=out, in_=in_, func=func, bias=bias, scale=scale, accum_out=accum),
                 reads=rd, writes=wr)

    def TT_(eng, out, in0, in1, op, rd, wr):
        P.op(eng, lambda e: e.tensor_tensor(out=out, in0=in0, in1=in1, op=op), reads=rd, writes=wr)

    def TS(eng, out, in0, s1, s2, op0, op1, rd, wr):
        if s2 is None:
            P.op(eng, lambda e: e.tensor_scalar(out=out, in0=in0, scalar1=s1, scalar2=None, op0=op0), reads=rd, writes=wr)
        else:
            P.op(eng, lambda e: e.tensor_scalar(out=out, in0=in0, scalar1=s1, scalar2=s2, op0=op0, op1=op1),
                 reads=rd, writes=wr)

    def STT(out, in0, scalar, in1, op0, op1, rd, wr):
        P.op("dve", lambda e: e.scalar_tensor_tensor(out=out, in0=in0, scalar=scalar, in1=in1, op0=op0, op1=op1),
             reads=rd, writes=wr)

    def CP(eng, out, in_, rd, wr):
        if eng == "act":
            P.op("act", lambda e: e.copy(out=out, in_=in_), reads=rd, writes=wr)
        else:
            P.op(eng, lambda e: e.tensor_copy(out=out, in_=in_), reads=rd, writes=wr)

    def MEMSET(eng, out, val, wr):
        P.op(eng, lambda e: e.memset(out, val), writes=wr)

    def DMA(q, out, in_, key, rd=(), wr=(), slow=False):
        if slow:
            P.dma(q, lambda e: e.dma_start(out=out, in_=in_, allow_slow_non_contiguous=True), key, reads=rd, writes=wr)
        else:
            P.dma(q, lambda e: e.dma_start(out=out, in_=in_), key, reads=rd, writes=wr)

    def wload(dst_buf, dst_ap_fn, rows0, c0, ncols, nk, key, src=None):
        src = w_in if src is None else src
        step = 8
        for k0 in range(0, nk, step):
            DMA("pool", dst_ap_fn(slice(k0, k0 + step)),
                src[rows0 + k0 * 128: rows0 + (k0 + step) * 128, c0:c0 + ncols].rearrange("(k p) n -> p k n", p=128),
                key, wr=[dst_buf])

    hT_all = sbw("hT", 16384, BF16, "p (k t) -> p k t", k=8)
    hTg = [register(Buf("hT%d" % g, hT_all.t[:, :, g * 512:(g + 1) * 512]), 0, 16384) for g in range(8)]
    YM0 = A.top
    ymT_all = sbw("ymT", 16384, BF16, "p (k t) -> p k t", k=16)
    ymS = [register(Buf("ymS%d" % b, ymT_all.t[:, 0:8, b * 128:(b + 1) * 128]), YM0, YM0 + 8192) for b in range(16)]
    ymA = [register(Buf("ymA%d" % b, ymT_all.t[:, 8:16, b * 128:(b + 1) * 128]), YM0 + 8192, YM0 + 16384) for b in range(16)]
    AT0 = YM0 + 8192

    identF = sbw("identF", 128)
    tri = sbw("tri", 128)
    onesF = sbw("onesF", 128)
    maskneg = sbw("maskneg", 128)
    identB = sbw("identB", 64, BF16)
    maskB = sbw("maskB", 64, BF16)
    convw = sbw("convw", 48, F32, "p (c k) -> p c k", k=4)
    convb = sbw("convb", 12)
    dtb = sbw("dtb", 16)
    Abc = sbw("Abc", 16)
    dsk = sbw("dsk", 16)
    fbb = sbw("fbb", 16)
    ssw = sbw("ssw", 8)
    atw = sbw("atw", 8)
    flg = sbw("flg", 2)
    ccol = sbw("ccol", 8)
    csil = sbw("csil", 4, BF16)
    modp1 = sbw("modp1", 16)
    modp2 = sbw("modp2", 16)
    modp3 = sbw("modp3", 16)
    atwh = sbw("atwh", 16)
    sscol = sbw("sscol", 16)
    valid = flg[:, 0:1]

    MEMSET("pool", identF[:], 0.0, [identF])
    P.op("pool", lambda e: e.affine_select(out=identF[:], in_=identF[:], pattern=[[-1, 128]], compare_op=ALU.not_equal,
                                           fill=1.0, base=0, channel_multiplier=1), reads=[identF], writes=[identF])
    MEMSET("pool", onesF[:], 1.0, [onesF])
    MEMSET("pool", tri[:], 1.0, [tri])
    P.op("pool", lambda e: e.affine_select(out=tri[:], in_=tri[:], pattern=[[1, 128]], compare_op=ALU.is_ge,
                                           fill=0.0, base=0, channel_multiplier=-1), reads=[tri], writes=[tri])
    MEMSET("pool", maskneg[:], 0.0, [maskneg])
    P.op("pool", lambda e: e.affine_select(out=maskneg[:], in_=maskneg[:], pattern=[[1, 128]], compare_op=ALU.is_ge,
                                           fill=NEG, base=0, channel_multiplier=-1), reads=[maskneg], writes=[maskneg])
    CP("pool", identB[:], identF[:], [identF], [identB])
    CP("pool", maskB[:], maskneg[:], [maskneg], [maskB])

    for k in range(4):
        DMA("sync", convw[:, :, k], D["conv_w"][k].rearrange("(c p) -> p c", p=128), "cw%d" % k, wr=[convw], slow=True)
    DMA("sync", convb[:], D["conv_b"][0].rearrange("(c p) -> p c", p=128), "cb", wr=[convb], slow=True)
    DMA("sync", ssw[:], D["ssm_norm_w"][0].rearrange("(c p) -> p c", p=128), "ssw", wr=[ssw], slow=True)
    DMA("sync", atw[:], D["attn_norm_w"][0].rearrange("(c p) -> p c", p=128), "atw", wr=[atw], slow=True)
    DMA("sync", atwh[0:64, :], D["attn_norm_w"][0].rearrange("(h d) -> d h", d=64), "atwh", wr=[atwh], slow=True)
    DMA("sync", ccol[:], D["c"][0].rearrange("(c p) -> p c", p=128), "ccol", wr=[ccol], slow=True)
    DMA("sync", dtb[:], D["dt_bias"][0:1, :].partition_broadcast(128), "dtb", wr=[dtb])
    DMA("sync", Abc[:], D["a_log"][0:1, :].partition_broadcast(128), "alog", wr=[Abc])
    DMA("sync", dsk[:], D["d_skip"][0:1, :].partition_broadcast(128), "dsk", wr=[dsk])
    DMA("sync", fbb[:], D["f_bias"][0:1, :].partition_broadcast(128), "fbb", wr=[fbb])
    DMA("sync", flg[:], D["flags"], "flg", wr=[flg])
    ACT(Abc[:], Abc[:], AF.Exp, [Abc], [Abc])
    TS("dve", Abc[:], Abc[:], -1.0, None, ALU.mult, None, [Abc], [Abc])
    ACT(csil[:], ccol[:], AF.Silu, [ccol], [csil])

    mark0 = A.top

    def ada_cols(c0, ncols, row, wt, brow):
        DMA("sync", brow[0:1, 0:ncols], D["b_ada"][0:1, c0:c0 + ncols], "brow", wr=[brow])
        for j in range(ncols // 512):
            DMA("pool", wt[:, :, :], D["w_ada"][:, c0 + j * 512: c0 + (j + 1) * 512].rearrange("(k p) n -> p k n", p=128),
                "wada", wr=[wt])
            bank = nb()
            for k in range(8):
                MM(bank, bank[0:1, 0:512], csil[:, k:k + 1], wt[:, k, :], [csil, wt], k == 0, k == 7)
            TT_("dve", row[0:1, j * 512:(j + 1) * 512], bank[0:1, 0:512], brow[0:1, j * 512:(j + 1) * 512], ALU.add,
                [bank, brow], [row])

    def row_to_cols(row, seg0, nseg, dst, dcol0, add1_from=None):
        bank = nb()
        for s in range(nseg):
            MM(bank, bank[:, s:s + 1], row[0:1, (seg0 + s) * 128:(seg0 + s + 1) * 128], onesF[0:1, 0:1], [row, onesF], s == 0)
        CP("dve", dst[:, dcol0:dcol0 + nseg], bank[:, 0:nseg], [bank], [dst])
        if add1_from is not None:
            TS("dve", dst[:, dcol0 + add1_from:dcol0 + nseg], dst[:, dcol0 + add1_from:dcol0 + nseg], 1.0, None, ALU.add, None,
               [dst], [dst])

    A.top = mark0
    Wz = sbw("Wz", 4096, BF16, "p (k n) -> p k n", k=8)
    markE = A.top
    row = sbw("row", 2048)
    brow = sbw("brow", 2048)
    wt_ada = sbw("wt_ada", 2048, BF16, "p (k n) -> p k n", k=8)
    ada_cols(0, 2048, row, wt_ada, brow)
    row_to_cols(row, 0, 16, modp1, 0, add1_from=8)
    Wc = alias("Wc", AT0 + 0, AT0 + 6144, BF16, "p (k n) -> p k n", k=8)
    Wdt = alias("Wdt", AT0 + 6144, AT0 + 6208, BF16, "p (k n) -> p k n", k=8)
    wload(Wc, lambda k: Wc[:, k, :], 0, CX, 1536, 8, "Wc")
    wload(Wdt, lambda k: Wdt[:, k, :], 0, CDT, 16, 8, "Wdt")
    wload(Wz, lambda k: Wz[:, k, :], 0, CZ, 1024, 8, "Wz")

    xin = [sbw("xin%d" % i, 1024) for i in range(2)]
    for blk in range(32):
        src = D["xp"][blk * 128:(blk + 1) * 128, :] if blk < 16 else D["xo"][(blk - 16) * 128:(blk - 15) * 128, :]
        xt = xin[blk % 2]
        DMA("sync", xt[:], src, "xin%d" % (blk % 2), wr=[xt])
        g = blk // 4
        t0 = blk * 128
        for half in range(2):
            bank = nb()
            for j in range(4):
                k = half * 4 + j
                TR(bank, bank[:, j * 128:(j + 1) * 128], xt[:, k * 128:(k + 1) * 128], identF[:], [xt, identF])
            for j in range(4):
                k = half * 4 + j
                if j % 2 == 0:
                    ACT(hT_all.t[:, k, t0:t0 + 128], bank[:, j * 128:(j + 1) * 128], AF.Identity, [bank, modp1], [hTg[g]],
                        bias=modp1[:, k:k + 1], scale=modp1[:, 8 + k:9 + k])
                else:
                    TS("dve", hT_all.t[:, k, t0:t0 + 128], bank[:, j * 128:(j + 1) * 128], modp1[:, 8 + k:9 + k],
                       modp1[:, k:k + 1], ALU.mult, ALU.add, [bank, modp1], [hTg[g]])

    rowB = alias("rowB", YM0, YM0 + 1024)
    browB = alias("browB", YM0 + 1024, YM0 + 2048)
    wtB = [alias("wtB%d" % i, YM0 + 2048 + i * 2048, YM0 + 4096 + i * 2048, BF16, "p (k n) -> p k n", k=8) for i in range(2)]

    def ada_part_load(part):
        c0 = 2048 + part * 1024
        DMA("sync", browB[0:1, 0:1024], D["b_ada"][0:1, c0:c0 + 1024], "browB", wr=[browB])
        for j in range(2):
            DMA("pool", wtB[j][:, :, :], D["w_ada"][:, c0 + j * 512: c0 + (j + 1) * 512].rearrange("(k p) n -> p k n", p=128),
                "wadaB%d" % j, wr=[wtB[j]])

    def ada_part_compute(part):
        for j in range(2):
            bank = nb()
            for k in range(8):
                MM(bank, bank[0:1, 0:512], csil[:, k:k + 1], wtB[j][:, k, :], [csil, wtB[j]], k == 0, k == 7)
            TT_("dve", rowB[0:1, j * 512:(j + 1) * 512], bank[0:1, 0:512], browB[0:1, j * 512:(j + 1) * 512], ALU.add,
                [bank, browB], [rowB])
        dst, dcol, add1 = [(modp3, 0, 0), (modp2, 0, None), (modp2, 8, 0), (modp3, 8, 0)][part]
        row_to_cols(rowB, 0, 8, dst, dcol, add1_from=add1)

    A.top = markE
    BT = alias("BT", AT0 + 6208, AT0 + 6720, BF16, "p (g t) -> p g t", g=2)
    CT = alias("CT", AT0 + 6720, AT0 + 7232, BF16, "p (g t) -> p g t", g=2)
    Btok = alias("Btok", AT0 + 7232, AT0 + 7744, BF16, "p (j n) -> p j n", j=4)
    xs_tok = sbw("xs_tok", 4096, F32, "p (j c) -> p j c", j=4)
    ubuf = [sbw("ubuf%d" % i, 516) for i in range(2)]
    halo = sbw("halo", 36, F32, "p (c k) -> p c k", k=3)
    acc0_off = A.top
    acc = [sbw("acc%d" % i, 512) for i in range(3)]
    xsT0_off = A.top
    xsT = [sbw("xsT%d" % i, 512) for i in range(2)]
    TAv = [S[:, acc0_off:acc0_off + 1024], S[:, xsT0_off:xsT0_off + 1024]]
    TAb = [[acc[0], acc[1]], [xsT[0], xsT[1]]]
    assert xsT0_off == acc0_off + 1536
    xc = sbw("xc", 512, BF16)
    xcd = sbw("xcd", 512, BF16)
    Sst = sbw("Sst", 1024)
    Sbf = sbw("Sbf", 512, BF16)
    zs = sbw("zs", 1024)
    y1 = sbw("y1", 1024)
    y2 = sbw("y2", 1024)
    MTb = sbw("MTb", 512, BF16, "p (h l) -> p h l", h=8)
    cbm = sbw("cbm", 256)
    Ust = sbw("Ust", 128)
    sms = [sbw("sm%d" % i, 64 * 8) for i in range(1)]
    sq = sbw("sq", 8)
    print("[kernel] SSD words free:", NW - A.top, flush=True)
    MEMSET("pool", Ust[:], 1.0, [Ust])
    P.op("pool", lambda e: e.affine_select(out=Ust[:], in_=Ust[:], pattern=[[-1, 128]], compare_op=ALU.is_gt,
                                           fill=0.0, base=0, channel_multiplier=1), reads=[Ust], writes=[Ust])

    MEMSET("pool", halo[:], 0.0, [halo])
    MEMSET("pool", Sst[:], 0.0, [Sst])

    cnt = [0]
    pend_fin = []
    SPAIRS = [(BK[0], BK[1], PS[:, 0:1024]), (BK[2], BK[3], PS[:, 1024:2048])]
    for G in range(8):
        own = G >= 4
        hg = hTg[G]
        tcol = slice(G * 512, (G + 1) * 512)
        if G == 4:
            hv = halo.t.rearrange("p c k -> p (c k)")
            TS("dve", hv, hv, valid, None, ALU.mult, None, [halo, flg], [halo])
            TS("dve", Sst[:], Sst[:], valid, None, ALU.mult, None, [Sst, flg], [Sst])
        cts = list(range(12)) if (own or G == 3) else list(range(10))
        if G < 4:
            ada_part_load(G)
        pend_c = []
        pend_d = []
        for ct in cts:
            i2 = cnt[0] % 2
            i3 = cnt[0] % 3
            cnt[0] += 1
            bank = nb()
            for k in range(8):
                MM(bank, bank[:, 0:512], Wc[:, k, ct * 128:(ct + 1) * 128], hT_all.t[:, k, tcol], [Wc, hg], k == 0, k == 7)
            while len(pend_d) > 2:
                pend_d.pop(0)()
            ub = ubuf[i2]
            CP("pool", ub[:, 0:3], halo[:, ct, :], [halo], [ub])
            CP("act", ub[:, 3:515], bank[:, 0:512], [bank], [ub])
            CP("pool", halo[:, ct, :], ub[:, 512:515], [ub], [halo])
            ac = acc[i3]
            ACT(ac[:], ub[:, 0:512], AF.Identity, [ub, convw, convb], [ac], bias=convb[:, ct:ct + 1], scale=convw[:, ct, 0:1])
            while len(pend_c) > 1:
                pend_c.pop(0)()
            for kk in range(1, 4):
                STT(ac[:], ub[:, kk:kk + 512], convw[:, ct, kk:kk + 1], ac[:], ALU.mult, ALU.add, [ub, convw, ac], [ac])
            if ct < 8:
                xt_ = xsT[i2]

                def cx(xt_=xt_, ac=ac):
                    ACT(xt_[:], ac[:], AF.Silu, [ac], [xt_])

                def dx(xt_=xt_, ct=ct):
                    b2 = nb()
                    for j in range(4):
                        TR(b2, b2[:, j * 128:(j + 1) * 128], xt_[:, j * 128:(j + 1) * 128], identF[:], [xt_, identF])
                    CP("dve" if ct % 2 else "act", xs_tok[:, :, ct * 128:(ct + 1) * 128],
                       b2[:, 0:512].rearrange("p (j c) -> p j c", j=4), [b2], [xs_tok])
                pend_c.append(cx)
                pend_d.append(dx)
            elif ct < 10:
                g = ct - 8

                def cb_(g=g, ac=ac):
                    ACT(BT[:, g, :], ac[:], AF.Silu, [ac], [BT])

                def db_(g=g):
                    b2 = nb()
                    for j in range(4):
                        TR(b2, b2.tb[:, j * 128:(j + 1) * 128], BT[:, g, j * 128:(j + 1) * 128], identB[:], [BT, identB])
                    CP("dve", Btok[:, :, g * 128:(g + 1) * 128], b2.tb[:, 0:512].rearrange("p (j c) -> p j c", j=4), [b2], [Btok])
                pend_c.append(cb_)
                pend_d.append(db_)
            else:
                g = ct - 10

                def cc_(g=g, ac=ac):
                    ACT(CT[:, g, :], ac[:], AF.Silu, [ac], [CT])
                pend_c.append(cc_)

        def flush_conv():
            while pend_c:
                pend_c.pop(0)()
            while pend_d:
                pend_d.pop(0)()
        if G < 4:
            ada_part_compute(G)
        bdt = nb()
        for j in range(4):
            ctok = slice((G * 4 + j) * 128, (G * 4 + j + 1) * 128)
            for k in range(8):
                MM(bdt, bdt[:, j * 16:(j + 1) * 16], hT_all.t[:, k, ctok], Wdt[:, k, :], [hg, Wdt], (j == 0 and k == 0), k == 7)
        flush_conv()
        sm = sms[0]
        xr4, dtt4, aa4, acs4, dout4, dte4, dch4, wv4 = [sm[:, i * 64:(i + 1) * 64] for i in range(8)]

        def v3(ap):
            return ap.rearrange("p (j h) -> p j h", j=4)
        TT_("dve", v3(xr4), v3(bdt[:, 0:64]), dtb[:].unsqueeze(1).to_broadcast([128, 4, 16]), ALU.add, [bdt, dtb], [sm])
        ACT(xr4, xr4, AF.Exp, [sm], [sm])
        ACT(dtt4, xr4, AF.Ln, [sm], [sm], bias=1.0)
        TT_("dve", v3(aa4), v3(dtt4), Abc[:].unsqueeze(1).to_broadcast([128, 4, 16]), ALU.mult, [sm, Abc], [sm])
        bcs = nb()
        MM(bcs, bcs[:, 0:64], tri[:], aa4, [tri, sm], True)
        MM(bcs, bcs[:, 64:128], onesF[:], aa4, [onesF, sm], False)
        CP("dve", acs4, bcs[:, 0:64], [bcs], [sm])
        ACT(dout4, bcs[:, 0:64], AF.Exp, [bcs], [sm])
        ACT(dch4, bcs[:, 64:128], AF.Exp, [bcs], [sm])
        TT_("dve", dte4, bcs[:, 64:128], acs4, ALU.subtract, [bcs, sm], [sm])
        ACT(dte4, dte4, AF.Exp, [sm], [sm])
        TT_("dve", wv4, dtt4, dte4, ALU.mult, [sm], [sm])
        for j in range(4):
            ch = G * 4 + j
            ctok = slice(ch * 128, (ch + 1) * 128)
            xr, dtt, aa, acs, dout, dte, dch, wv = [sm[:, i * 64 + j * 16:i * 64 + (j + 1) * 16] for i in range(8)]
            xsj = xs_tok[:, j, :].rearrange("p (h d) -> p h d", h=16)
            if not own:
                TT_("pool", xcd[:].rearrange("p (h d) -> p h d", h=16), xsj, wv.unsqueeze(2).to_broadcast([128, 16, 64]),
                    ALU.mult, [xs_tok, sm], [xcd])
            if own:
                CP("act", Sbf[:], Sst[:], [Sst], [Sbf])
                prs = []
                for g in range(2):
                    TT_("dve", TAv[g].rearrange("p (h l) -> p h l", h=8), tri[:].unsqueeze(1).to_broadcast([128, 8, 128]),
                        aa[:, g * 8:(g + 1) * 8].unsqueeze(2).to_broadcast([128, 8, 128]), ALU.mult, [tri, sm], TAb[g])
                TT_("pool", xc[:].rearrange("p (h d) -> p h d", h=16), xsj, dtt.unsqueeze(2).to_broadcast([128, 16, 64]),
                    ALU.mult, [xs_tok, sm] + TAb[1], [xc])
                TT_("pool", y2[:].rearrange("p (h d) -> p h d", h=16), xsj, dsk[:].unsqueeze(2).to_broadcast([128, 16, 64]),
                    ALU.mult, [xs_tok, dsk], [y2])
                TT_("pool", xcd[:].rearrange("p (h d) -> p h d", h=16), xsj, wv.unsqueeze(2).to_broadcast([128, 16, 64]),
                    ALU.mult, [xs_tok, sm], [xcd])
                for g in range(2):
                    b0, b1, pview = SPAIRS[g]
                    MM(b0, b0[:, 0:512], Ust[:], TAv[g][:, 0:512], [Ust] + TAb[g], True)
                    MM(b1, b1[:, 0:512], Ust[:], TAv[g][:, 512:1024], [Ust] + TAb[g], True)
                    ACT(pview, pview, AF.Exp, [b0, b1], [b0, b1])
                    prs.append((b0, b1, pview))
                bcb = BK[5]
                for g in range(2):
                    MM(bcb, bcb[:, g * 128:(g + 1) * 128], BT[:, g, j * 128:(j + 1) * 128], CT[:, g, j * 128:(j + 1) * 128],
                       [BT, CT], g == 0)
                TT_("dve", cbm[:].rearrange("p (g l) -> p g l", g=2), bcb[:, 0:256].rearrange("p (g l) -> p g l", g=2),
                    tri[:].unsqueeze(1).to_broadcast([128, 2, 128]), ALU.mult, [bcb, tri], [cbm])
                for half in range(2):
                    bz = BK[4 + 3 * half]
                    for k in range(8):
                        MM(bz, bz[:, 0:512], hT_all.t[:, k, ctok], Wz[:, k, half * 512:(half + 1) * 512], [hg, Wz], k == 0, k == 7)
                    ACT(zs[:, half * 512:(half + 1) * 512], bz[:, 0:512], AF.Silu, [bz], [zs])
                for g in range(2):
                    b0, b1, pview = prs[g]
                    TT_("dve", MTb[:], pview.rearrange("p (h l) -> p h l", h=8),
                        cbm[:, g * 128:(g + 1) * 128].unsqueeze(1).to_broadcast([128, 8, 128]), ALU.mult, [b0, b1, cbm], [MTb])
                    byd = BK[6]
                    for hh in range(8):
                        h = g * 8 + hh
                        MM(byd, byd[:, hh * 64:(hh + 1) * 64], MTb[:, hh, :], xc[:, h * 64:(h + 1) * 64], [MTb, xc], hh == 0)
                    if g == 0:
                        while pend_fin:
                            pend_fin.pop(0)()
                    boff = BK[4 + 3 * g]
                    MM(boff, boff[:, 0:512], CT[:, g, j * 128:(j + 1) * 128], Sbf[:, g * 512:(g + 1) * 512], [CT, Sbf], True)
                    ysl = slice(g * 512, (g + 1) * 512)
                    TT_("dve", y1[:, ysl].rearrange("p (h d) -> p h d", h=8), boff[:, 0:512].rearrange("p (h d) -> p h d", h=8),
                        dout[:, g * 8:(g + 1) * 8].unsqueeze(2).to_broadcast([128, 8, 64]), ALU.mult, [boff, sm], [y1])
                    TT_("dve", y1[:, ysl], byd[:, 0:512], y1[:, ysl], ALU.add, [byd, y1], [y1])
                TT_("dve", y2[:], y2[:], y1[:], ALU.add, [y2, y1], [y2])
                TT_("dve", y2[:], y2[:], zs[:], ALU.mult, [y2, zs], [y2])
                MEMSET("pool", sq[:, 0:1], 0.0, [sq])
                ACT(y1[:], y2[:], AF.Square, [y2], [y1, sq], accum=sq[:, 0:1])
                ACT(sq[:, 1:2], sq[:, 0:1], AF.Sqrt, [sq], [sq], bias=EPS, scale=1.0 / 1024.0)
                P.op("dve", lambda e: e.reciprocal(out=sq[:, 2:3], in_=sq[:, 1:2]), reads=[sq], writes=[sq])
                TS("dve", y1[:], y2[:], sq[:, 2:3], None, ALU.mult, None, [y2, sq], [y1])

                def fin(ob=ch - 16):
                    for half in range(2):
                        bt_ = BK[4 + 3 * half]
                        for jj in range(4):
                            kt = half * 4 + jj
                            TR(bt_, bt_[:, jj * 128:(jj + 1) * 128], y1[:, kt * 128:(kt + 1) * 128], identF[:], [y1, identF])
                        for jj in range(4):
                            kt = half * 4 + jj
                            ACT(ymT_all.t[:, kt, ob * 128:(ob + 1) * 128], bt_[:, jj * 128:(jj + 1) * 128], AF.Identity,
                                [bt_, ssw], [ymS[ob]], scale=ssw[:, kt:kt + 1])
                pend_fin.append(fin)
            for g in range(2):
                bs = nb()
                MM(bs, bs[:, 0:512], Btok[:, j, g * 128:(g + 1) * 128], xcd[:, g * 512:(g + 1) * 512], [Btok, xcd], True)
                ssl = slice(g * 512, (g + 1) * 512)
                TT_("pool", Sst[:, ssl].rearrange("p (h d) -> p h d", h=8), Sst[:, ssl].rearrange("p (h d) -> p h d", h=8),
                    dch[:, g * 8:(g + 1) * 8].unsqueeze(2).to_broadcast([128, 8, 64]), ALU.mult, [Sst, sm], [Sst])
                TT_("dve", Sst[:, ssl], bs[:, 0:512], Sst[:, ssl], ALU.add, [bs, Sst], [Sst])
    while pend_fin:
        pend_fin.pop(0)()

    A.top = mark0
    markF = A.top
    DMA("sync", ymsd.rearrange("p (k t) -> p k t", k=8), ymT_all.t[:, 0:8, :], "ymsp", rd=ymS)
    KaS = [[sbw("Ka0%d" % i, 2048, BF16) for i in range(2)],
           [alias("Ka1%d" % i, YM0 + i * 2048, YM0 + (i + 1) * 2048, BF16) for i in range(2)]]
    VaS = [sbw("Va0", 4096, BF16, "p (b h d) -> p b h d", b=32, h=2),
           alias("Va1", YM0 + 4096, YM0 + 8192, BF16, "p (b h d) -> p b h d", b=32, h=2)]
    QaS = [[sbw("Qa0%d" % i, 1024, BF16) for i in range(2)], None]
    Wqkv = sbw("Wqkv", 1536, BF16, "p (k n) -> p k n", k=8)
    pT = [sbw("pT%d" % i, 512, BF16) for i in range(3)]
    vts = [sbw("vts%d" % i, 256, BF16) for i in range(2)]
    print("[kernel] attention words free:", NW - A.top, flush=True)
    markF2 = A.top
    Wf = sbw("Wf", 64, BF16, "p (k n) -> p k n", k=8)
    lfa = sbw("lfa", 512)
    cumT = sbw("cumT", 512)
    r1 = sbw("r1", 512)
    c3s = sbw("c3s", 768, BF16, "p (r t) -> p r t", r=3)
    carry = sbw("carry", 1)

    wload(Wf, lambda k: Wf[:, k, :], 0, CF, 16, 8, "Wf")
    MEMSET("dve", carry[:], 0.0, [carry])
    MEMSET("pool", sscol[:], 0.0, [sscol])
    bf = nb()
    for blk in range(32):
        for k in range(8):
            MM(bf, bf[:, blk * 16:(blk + 1) * 16], hT_all.t[:, k, blk * 128:(blk + 1) * 128], Wf[:, k, :], [hTg[blk // 4], Wf],
               (blk == 0 and k == 0), k == 7)
    TT_("dve", lfa[:].rearrange("p (b h) -> p b h", b=32), bf[:, 0:512].rearrange("p (b h) -> p b h", b=32),
        fbb[:].unsqueeze(1).to_broadcast([128, 32, 16]), ALU.add, [bf, fbb], [lfa])
    ACT(lfa[:], lfa[:], AF.Exp, [lfa], [lfa], scale=-1.0)
    ACT(lfa[:], lfa[:], AF.Ln, [lfa], [lfa], bias=1.0)
    TS("dve", lfa[:], lfa[:], -1.0, None, ALU.mult, None, [lfa], [lfa])
    for q4 in range(8):
        bc = nb()
        for bq in range(4):
            blk = q4 * 4 + bq
            MM(bc, bc[0:16, bq * 128:(bq + 1) * 128], lfa[:, blk * 16:(blk + 1) * 16], tri[:], [lfa, tri], bq == 0)
        for bq in range(4):
            TS("dve", cumT[0:16, bq * 128:(bq + 1) * 128], bc[0:16, bq * 128:(bq + 1) * 128], carry[0:16, 0:1], None, ALU.add, None,
               [bc, carry], [cumT])
            CP("dve", carry[0:16, 0:1], cumT[0:16, bq * 128 + 127:bq * 128 + 128], [cumT], [carry])
        CP("dve", c3s[0:16, 0, :], cumT[0:16, :], [cumT], [c3s])
        TT_("dve", r1[0:16, :], cumT[0:16, :], c3s[0:16, 0, :], ALU.subtract, [cumT, c3s], [r1])
        CP("dve", c3s[0:16, 1, :], r1[0:16, :], [r1], [c3s])
        TT_("dve", r1[0:16, :], r1[0:16, :], c3s[0:16, 1, :], ALU.subtract, [r1, c3s], [r1])
        CP("dve", c3s[0:16, 2, :], r1[0:16, :], [r1], [c3s])
        DMA("sync", c3d.rearrange("h (r t) -> h r t", r=3)[:, :, q4 * 512:(q4 + 1) * 512], c3s[0:16, :, :], "c3w", rd=[c3s])
    c3tok = Buf("c3dram", None)
    c3tok.lw = ("s", "c3w", P.semvals["c3w"])
    c3v = c3d.rearrange("h (r t) -> h r t", r=3)
    A.top = markF2
    rbc = sbw("rbc", 512)
    yv = [sbw("yv%d" % i, 512) for i in range(2)]
    ysq = sbw("ysq", 512)
    QaS[1] = [sbw("Qa1%d" % i, 1024, BF16) for i in range(2)]

    for Va in VaS:
        MEMSET("pool", Va[:, :, :, 64:128], 1.0, [Va])
        TS("pool", Va[:, 0:16, :, 64:128], Va[:, 0:16, :, 64:128], valid, None, ALU.mult, None, [Va, flg], [Va])
    for sl in range(2):
        for i in range(2):
            MEMSET("pool", KaS[sl][i][64:128, :], 0.0, [KaS[sl][i]])
            MEMSET("pool", KaS[sl][i][64:70, :], 1.0, [KaS[sl][i]])
            MEMSET("pool", QaS[sl][i][64:128, :], 0.0, [QaS[sl][i]])
            MEMSET("pool", QaS[sl][i][64:70, :], -1.0, [QaS[sl][i]])

    pcnt = [0]
    ecnt = [0]
    fcnt = [0]
    deferred = []
    filler = []
    tcnt = [0]

    def tick():
        for d_ in deferred:
            d_[0] -= 1
        while deferred and deferred[0][0] <= 0:
            deferred.pop(0)[1]()
        tcnt[0] += 1
        if filler and tcnt[0] % 5 == 0:
            filler.pop(0)()

    def flush():
        while deferred:
            deferred.pop(0)[1]()

    def fbank():
        b = BK[4 + 3 * (fcnt[0] % 2)]
        fcnt[0] += 1
        return b

    def proj_units(hp):
        sl = hp % 2
        Ka, Qa, Va, wq = KaS[sl], QaS[sl], VaS[sl], Wqkv
        units = []

        def u_load():
            for i, c0 in enumerate((CQ, CK, CV)):
                wload(wq, (lambda i: (lambda k: wq[:, k, i * 128:(i + 1) * 128]))(i), 0, c0 + hp * 128, 128, 8, "Wq")
            if hp == 0:
                for r0 in range(0, 1024, 128):
                    DMA("pool", w1b[r0:r0 + 128, :], D["w_ff_in"][r0:r0 + 128, :], "precast")
                for r0 in range(0, 2048, 128):
                    DMA("pool", wob[r0:r0 + 128, :], D["w_out"][r0:r0 + 128, :], "precast")
                for r0 in range(0, 4096, 128):
                    DMA("pool", w2b[r0:r0 + 128, :], D["w_ff_out"][r0:r0 + 128, :], "precast")
            for hh in range(2):
                h = hp * 2 + hh
                DMA("sync", Ka[hh][67:70, :], c3v[h, :, :], "ka%d%d" % (sl, hh), rd=[c3tok], wr=[Ka[hh]])
                DMA("sync", Qa[hh][64:67, :], c3v[h, :, TP:TT], "qa%d%d" % (sl, hh), rd=[c3tok], wr=[Qa[hh]])
        units.append(u_load)

        def u_k(G):
            bk_ = fbank()
            for k in range(8):
                MM(bk_, bk_[:, 0:512], wq[:, k, 128:256], hT_all.t[:, k, G * 512:(G + 1) * 512], [wq, hTg[G]], k == 0, k == 7)
            CP("dve", Ka[0][0:64, G * 512:(G + 1) * 512], bk_[0:64, 0:512], [bk_], [Ka[0]])
            CP("dve", Ka[1][0:64, G * 512:(G + 1) * 512], bk_[64:128, 0:512], [bk_], [Ka[1]])

        def u_q(G):
            bq_ = fbank()
            for k in range(8):
                MM(bq_, bq_[:, 0:512], wq[:, k, 0:128], hT_all.t[:, k, TP + G * 512:TP + (G + 1) * 512], [wq, hTg[4 + G]],
                   k == 0, k == 7)
            TS("dve", Qa[0][0:64, G * 512:(G + 1) * 512], bq_[0:64, 0:512], 0.125, None, ALU.mult, None, [bq_], [Qa[0]])
            TS("dve", Qa[1][0:64, G * 512:(G + 1) * 512], bq_[64:128, 0:512], 0.125, None, ALU.mult, None, [bq_], [Qa[1]])

        def u_v(G):
            bv = fbank()
            for k in range(8):
                MM(bv, bv[:, 0:512], wq[:, k, 256:384], hT_all.t[:, k, G * 512:(G + 1) * 512], [wq, hTg[G]], k == 0, k == 7)
            vt = vts[G % 2]
            CP("dve", vt[:, 0:512], bv[:, 0:512], [bv], [vt])
            b2 = fbank()
            for j in range(4):
                TR(b2, b2.tb[:, j * 128:(j + 1) * 128], vt[:, j * 128:(j + 1) * 128], identB[:], [vt, identB])
            src = b2.tb[:, 0:512].rearrange("p (b h d) -> p b h d", b=4, h=2)
            if G < 4:
                TS("dve", Va[:, G * 4:(G + 1) * 4, :, 0:64], src, valid, None, ALU.mult, None, [b2, flg], [Va])
            else:
                CP("dve", Va[:, G * 4:(G + 1) * 4, :, 0:64], src, [b2], [Va])
        for G in range(8):
            units.append((lambda G: (lambda: u_k(G)))(G))
        for G in range(4):
            units.append((lambda G: (lambda: u_q(G)))(G))
        for G in range(8):
            units.append((lambda G: (lambda: u_v(G)))(G))
        return units

    PAIRS = [(BK[0], BK[1], PS[:, 0:1024]), (BK[2], BK[3], PS[:, 1024:2048])]
    for u in proj_units(0):
        u()
    for hp in range(NH // 2):
        sl = hp % 2
        Va = VaS[sl]
        if hp + 1 < NH // 2:
            filler.extend(proj_units(hp + 1))
        for hh in range(2):
            h = hp * 2 + hh
            ka, qa = KaS[sl][hh], QaS[sl][hh]
            for G in range(4):
                po = BK[5 + ecnt[0] % 2]
                npre = 16 + 4 * G
                nkb = npre + 4
                steps = [([kb, kb + 1], 0, 512) for kb in range(0, npre, 2)] + \
                        [([npre + r], r, (4 - r) * 128) for r in range(4)]
                pend = None
                for si in range(len(steps) + 1):
                    cur = None
                    if si < len(steps):
                        kbs, q0, ncol = steps[si]
                        pt = pT[pcnt[0] % 3]
                        if len(kbs) == 2:
                            b0, b1, pview = PAIRS[pcnt[0] % 2]
                            for bi, kb in zip((b0, b1), kbs):
                                MM(bi, bi[:, 0:512], ka[:, kb * 128:(kb + 1) * 128], qa[:, G * 512:(G + 1) * 512],
                                   [ka, qa], True, True)
                            ACT(pt[:, 0:1024], pview, AF.Exp, [b0, b1], [pt])
                        else:
                            kb = kbs[0]
                            bs_ = BK[4 + 3 * (q0 % 2)]
                            MM(bs_, bs_[:, 0:ncol], ka[:, kb * 128:(kb + 1) * 128],
                               qa[:, G * 512 + q0 * 128:(G + 1) * 512], [ka, qa], True, False)
                            MM(bs_, bs_[:, 0:128], identB[:], maskB[:], [identB, maskB], False, True)
                            ACT(pt[:, 0:ncol], bs_[:, 0:ncol], AF.Exp, [bs_], [pt])
                        pcnt[0] += 1
                        cur = (kbs, q0, ncol, pt)
                    if pend is not None:
                        kbs_, q0_, ncol_, pt_ = pend
                        for i_, kb_ in enumerate(kbs_):
                            MM(po, po[:, q0_ * 128:512], Va[:, kb_, hh, :], pt_[:, i_ * 512:i_ * 512 + ncol_], [Va, pt_],
                               kb_ == 0, kb_ == nkb - 1)
                    pend = cur
                    tick()
                P.op("dve", (lambda po_: (lambda e: e.reciprocal(out=rbc[0:64, :], in_=po_[64:128, 0:512])))(po),
                     reads=[po], writes=[rbc])
                yv_ = yv[ecnt[0] % 2]
                TT_("dve", yv_[0:64, :], po[0:64, 0:512], rbc[0:64, :], ALU.mult, [po, rbc], [yv_])
                pr = (h % 2) * 64
                TT_("pool", ysq[0:64, :], yv_[0:64, :], yv_[0:64, :], ALU.mult, [yv_], [ysq])
                TS("dve", ymT_all.t[pr:pr + 64, 8 + h // 2, G * 512:(G + 1) * 512], yv_[0:64, :], atwh[0:64, h:h + 1], None,
                   ALU.mult, None, [yv_, atwh], [ymA[G * 4 + j] for j in range(4)])

                def ep2(G=G, e_=ecnt[0]):
                    bss = BK[4 + 3 * (e_ % 2)]
                    for j4 in range(4):
                        MM(bss, bss[:, j4:j4 + 1], ysq[0:64, j4 * 128:(j4 + 1) * 128], onesF[0:64, 0:1], [onesF, ysq], j4 == 0)
                    TT_("dve", sscol[:, G * 4:(G + 1) * 4], bss[:, 0:4], sscol[:, G * 4:(G + 1) * 4], ALU.add,
                        [bss, sscol], [sscol])

                deferred.append([8, ep2])
                ecnt[0] += 1
        while filler:
            filler.pop(0)()
    flush()

    A.top = markF
    g1b = alias("g1b", 0, 1024)
    g2b = alias("g2b", 1024, 2048)
    lnb = alias("lnb", 2048, 6144, F32, "p (v n) -> p v n", v=4)
    h2Tb = alias("h2Tb", 6144, 14336, BF16, "p (k t) -> p k t", k=8)
    xr_ = [sbw("xr%d" % i, 1024) for i in range(2)]
    t1s = [sbw("t1_%d" % i, 1024) for i in range(2)]
    t2 = sbw("t2", 1024)
    sts = [sbw("st%d" % i, 16) for i in range(2)]
    t1 = t1s[0]
    st_ = sts[0]
    markGH = A.top
    Wo = sbw("Wo", 8192, BF16, "p (k n) -> p k n", k=16)
    dg = [sbw("dg%d" % i, 128) for i in range(2)]

    ymS = [register(Buf("ymS2_%d" % b, ymT_all.t[:, 0:8, b * 128:(b + 1) * 128]), YM0, YM0 + 8192) for b in range(16)]
    spill = Buf("ymsd", None)
    spill.lw = ("s", "ymsp", P.semvals["ymsp"])
    DMA("sync", ymT_all.t[:, 0:8, :], ymsd.rearrange("p (k t) -> p k t", k=8), "ymrl", rd=[spill], wr=ymS)
    pctok = Buf("precast", None)
    pctok.lw = ("s", "precast", P.semvals["precast"])
    for v, nme in enumerate(["ln1_g", "ln1_b", "ln2_g", "ln2_b"]):
        DMA("sync", lnb[:, v, :], D[nme][0:1, :].partition_broadcast(128), "lnb%d" % v, wr=[lnb])
    for k0 in (0, 8):
        DMA("sync", Wo[:, k0:k0 + 8, :], wob[k0 * 128:(k0 + 8) * 128, :].rearrange("(k p) n -> p k n", p=128), "Wo",
            rd=[pctok], wr=[Wo])

    def col_bcast(c0, dst):
        for half in range(2):
            bank = nb()
            for j in range(4):
                k = half * 4 + j
                d_ = dg[k % 2]
                TS("dve", d_[:], identF[:], modp3[:, c0 + k:c0 + k + 1], None, ALU.mult, None, [identF, modp3], [d_])
                MM(bank, bank[:, j * 128:(j + 1) * 128], onesF[:], d_[:], [onesF, d_], j == 0)
            CP("act", dst[:, half * 512:(half + 1) * 512], bank[:, 0:512], [bank], [dst])

    col_bcast(0, g1b)
    col_bcast(8, g2b)

    def layer_norm(src, dst, gi, st_):
        MEMSET("dve", st_[:, 0:2], 0.0, [st_])
        ACT(t2[:], src[:], AF.Identity, [src], [st_], accum=st_[:, 0:1])
        ACT(t2[:], src[:], AF.Square, [src], [st_], accum=st_[:, 1:2])
        TS("dve", st_[:, 2:3], st_[:, 0:1], 1.0 / 1024.0, None, ALU.mult, None, [st_], [st_])
        TT_("dve", st_[:, 3:4], st_[:, 2:3], st_[:, 2:3], ALU.mult, [st_], [st_])
        STT(st_[:, 4:5], st_[:, 1:2], 1.0 / 1024.0, st_[:, 3:4], ALU.mult, ALU.subtract, [st_], [st_])
        ACT(st_[:, 5:6], st_[:, 4:5], AF.Sqrt, [st_], [st_], bias=EPS)
        P.op("dve", lambda e: e.reciprocal(out=st_[:, 6:7], in_=st_[:, 5:6]), reads=[st_], writes=[st_])
        TS("dve", dst[:], src[:], st_[:, 2:3], st_[:, 6:7], ALU.subtract, ALU.mult, [src, st_], [dst])
        TT_("dve", dst[:], dst[:], lnb[:, gi, :], ALU.mult, [dst, lnb], [dst])
        TT_("dve", dst[:], dst[:], lnb[:, gi + 1, :], ALU.add, [dst, lnb], [dst])

    x1toks = [Buf("x1dram%d" % i, None) for i in range(2)]
    pend_g = []
    for ob in range(16):
        xt = xr_[ob % 2]
        t1 = t1s[ob % 2]
        st_ = sts[ob % 2]
        DMA("sync", xt[:], D["xo"][ob * 128:(ob + 1) * 128, :], "xr%d" % (ob % 2), wr=[xt])
        ACT(st_[:, 9:10], sscol[:, ob:ob + 1], AF.Sqrt, [sscol], [st_], bias=EPS, scale=1.0 / 1024.0)
        P.op("dve", lambda e, st_=st_: e.reciprocal(out=st_[:, 10:11], in_=st_[:, 9:10]), reads=[st_], writes=[st_])
        for half in range(2):
            b1 = nb()
            for kt in range(8):
                MM(b1, b1[:, 0:512], ymT_all.t[:, kt, ob * 128:(ob + 1) * 128], Wo[:, kt, half * 512:(half + 1) * 512],
                   [ymS[ob], Wo], kt == 0, kt == 7)
            b2 = nb()
            for kt in range(8, 16):
                MM(b2, b2[:, 0:512], ymT_all.t[:, kt, ob * 128:(ob + 1) * 128], Wo[:, kt, half * 512:(half + 1) * 512],
                   [ymA[ob], Wo], kt == 8, kt == 15)
            hs = slice(half * 512, (half + 1) * 512)
            CP("act", t1[:, hs], b1[:, 0:512], [b1], [t1])
            STT(t1[:, hs], b2[:, 0:512], st_[:, 10:11], t1[:, hs], ALU.mult, ALU.add, [b2, st_, t1], [t1])
        while pend_g:
            pend_g.pop(0)()
        TT_("dve", t1[:], t1[:], g1b[:], ALU.mult, [t1, g1b], [t1])
        STT(t1[:], xt[:], ALPHA, t1[:], ALU.mult, ALU.add, [xt, t1], [t1])
        layer_norm(t1, xt, 0, st_)
        DMA("sync", x1s[ob * 128:(ob + 1) * 128, :], xt[:], "x1w%d" % (ob % 2), rd=[xt])

        def trg(xt=xt, ob=ob):
            for half in range(2):
                bt_ = nb()
                for jj in range(4):
                    k = half * 4 + jj
                    TR(bt_, bt_[:, jj * 128:(jj + 1) * 128], xt[:, k * 128:(k + 1) * 128], identF[:], [xt, identF])
                for jj in range(4):
                    k = half * 4 + jj
                    if jj % 2 == 0:
                        ACT(h2Tb[:, k, ob * 128:(ob + 1) * 128], bt_[:, jj * 128:(jj + 1) * 128], AF.Identity, [bt_, modp2],
                            [h2Tb], bias=modp2[:, k:k + 1], scale=modp2[:, 8 + k:9 + k])
                    else:
                        TS("dve", h2Tb[:, k, ob * 128:(ob + 1) * 128], bt_[:, jj * 128:(jj + 1) * 128], modp2[:, 8 + k:9 + k],
                           modp2[:, k:k + 1], ALU.mult, ALU.add, [bt_, modp2], [h2Tb])
        pend_g.append(trg)
    while pend_g:
        pend_g.pop(0)()
    for i in range(2):
        x1toks[i].lw = ("s", "x1w%d" % i, P.semvals["x1w%d" % i])

    W2 = alias("W2", YM0, YM0 + 16384, BF16, "p (f n) -> p f n", f=32)
    for f0 in range(0, 32, 8):
        DMA("sync", W2[:, f0:f0 + 8, :], w2b[f0 * 128:(f0 + 8) * 128, :].rearrange("(f p) n -> p f n", p=128), "W2",
            rd=[pctok], wr=[W2])
    A.top = markGH
    a1T = sbw("a1T", 8192, BF16, "p (f t) -> p f t", f=32)
    W1t = [sbw("W1t%d" % i, 2048, BF16, "p (k n) -> p k n", k=8) for i in range(2)]
    rl = [sbw("rl%d" % i, 512) for i in range(2)]
    wcnt = [0]
    for G in range(4):
        for c4 in range(8):
            w1 = W1t[wcnt[0] % len(W1t)]
            DMA("sync", w1[:, :, :], w1b[:, c4 * 512:(c4 + 1) * 512].rearrange("(k p) n -> p k n", p=128),
                "W1t%d" % (wcnt[0] % len(W1t)), rd=[pctok], wr=[w1])
            wcnt[0] += 1
            for f4 in range(4):
                f = c4 * 4 + f4
                bf_ = nb()
                for k in range(8):
                    MM(bf_, bf_[:, 0:512], w1[:, k, f4 * 128:(f4 + 1) * 128], h2Tb[:, k, G * 512:(G + 1) * 512], [w1, h2Tb],
                       k == 0, k == 7)
                r_ = rl[f % 2]
                ACT(r_[:], bf_[:, 0:512], AF.Relu, [bf_], [r_])
                TT_("dve", a1T[:, f, :], r_[:], r_[:], ALU.mult, [r_], [a1T])
        for tb in range(4):
            ob = G * 4 + tb
            xt = xr_[ob % 2]
            t1 = t1s[ob % 2]
            st_ = sts[ob % 2]
            DMA("sync", xt[:], x1s[ob * 128:(ob + 1) * 128, :], "xr%d" % (ob % 2), rd=x1toks, wr=[xt])
            for half in range(2):
                bo = nb()
                for f in range(32):
                    MM(bo, bo[:, 0:512], a1T[:, f, tb * 128:(tb + 1) * 128], W2[:, f, half * 512:(half + 1) * 512], [a1T, W2],
                       f == 0, f == 31)
                hs = slice(half * 512, (half + 1) * 512)
                TT_("dve", t1[:, hs], bo[:, 0:512], g2b[:, hs], ALU.mult, [bo, g2b], [t1])
            STT(t1[:], xt[:], ALPHA, t1[:], ALU.mult, ALU.add, [xt, t1], [t1])
            layer_norm(t1, xt, 2, st_)
            DMA("sync", out_d[ob * 128:(ob + 1) * 128, :], xt[:], "ow%d" % (ob % 2), rd=[xt])

    print("[kernel] ops:", {k: len(v) for k, v in P.ops.items()}, "cnt:", P.cnt, "nsem:", len(P.semkeys), flush=True)
    P.emit()
    st.close()
    return nc


_NC = [None]


def kernel(**inputs):
    x = np.ascontiguousarray(np.asarray(inputs["x"], dtype=np.float32))
    c = np.asarray(inputs["c"], dtype=np.float32)
    if _NC[0] is None:
        _NC[0] = build()
    nc = _NC[0]
    wmap = {}
    for n in WEIGHT_NAMES:
        wmap[n] = np.ascontiguousarray(np.asarray(inputs[n], dtype=np.float32).reshape(WEIGHT_SHAPES[n]))
    in_maps = []
    for core in range(8):
        b, h = core // 2, core % 2
        m = dict(wmap)
        m["xo"] = np.ascontiguousarray(x[b, h * TO:(h + 1) * TO])
        m["xp"] = np.ascontiguousarray(x[b, 0:TP])
        m["c"] = np.ascontiguousarray(c[b:b + 1])
        fl = np.zeros((128, 2), np.float32)
        fl[:, 0] = float(h)
        m["flags"] = fl
        in_maps.append(m)
    res = run_bass_kernel_spmd(nc, in_maps, core_ids=list(range(8)))
    out = np.empty((4, 4096, DM), np.float32)
    for core in range(8):
        b, h = core // 2, core % 2
        out[b, h * TO:(h + 1) * TO] = res.results[core]["out"]
    return out
```

```python
import contextlib
import numpy as np
import concourse.bass as bass
import concourse.mybir as mybir
from concourse.bass_utils import run_bass_kernel_spmd

F32 = mybir.dt.float32
BF16 = mybir.dt.bfloat16
AF = mybir.ActivationFunctionType
ALU = mybir.AluOpType

COMPUTE = ("pe", "act", "dve", "pool")
SELF_SYNC = ("act", "dve", "pool")
QUEUES = COMPUTE + ("sync",)


class Buf:
    __slots__ = ("name", "t", "tb", "lw", "rd")

    def __init__(self, name, t, tb=None):
        self.name = name
        self.t = t
        self.tb = tb
        self.lw = None
        self.rd = []

    def __getitem__(self, k):
        return self.t[k]


class Prog:
    def __init__(self, nc):
        self.nc = nc
        self.ops = {e: [] for e in QUEUES}
        self.cnt = {e: 0 for e in COMPUTE}
        self.waited = {e: {} for e in QUEUES}
        self.semvals = {}
        self.semkeys = []

    def _need(self, eng, waits, tok):
        if tok is None:
            return
        if tok[0] == "e":
            _, e2, idx = tok
            if e2 == eng and eng not in SELF_SYNC:
                return
            key = ("e", e2)
            val = idx
        else:
            _, sk, val = tok
            key = ("s", sk)
        if self.waited[eng].get(key, 0) >= val:
            return
        if waits.get(key, 0) < val:
            waits[key] = val

    def _collect(self, eng, reads, writes):
        waits = {}
        for b in reads:
            self._need(eng, waits, b.lw)
        for b in writes:
            self._need(eng, waits, b.lw)
            for tok in b.rd:
                self._need(eng, waits, tok)
        for k, v in waits.items():
            self.waited[eng][k] = v
        return waits

    def _mark(self, tok, reads, writes):
        for b in reads:
            b.rd.append(tok)
            if len(b.rd) > 12:
                best = {}
                for t in b.rd:
                    k = (t[0], t[1])
                    if k not in best or best[k][2] < t[2]:
                        best[k] = t
                b.rd = list(best.values())
        for b in writes:
            b.lw = tok
            b.rd = []

    def op(self, eng, fn, reads=(), writes=()):
        waits = self._collect(eng, reads, writes)
        self.cnt[eng] += 1
        tok = ("e", eng, self.cnt[eng])
        self._mark(tok, reads, writes)
        self.ops[eng].append((waits, fn, None))
        return tok

    def dma(self, q, fn, semkey, reads=(), writes=()):
        waits = self._collect(q, reads, writes)
        if semkey not in self.semvals:
            self.semvals[semkey] = 0
            self.semkeys.append(semkey)
        self.semvals[semkey] += 16
        tok = ("s", semkey, self.semvals[semkey])
        self._mark(tok, reads, writes)
        self.ops[q].append((waits, fn, semkey))
        return tok

    def emit(self):
        nc = self.nc
        with contextlib.ExitStack() as st:
            esem = {e: st.enter_context(nc.semaphore("se_" + e)) for e in COMPUTE}
            ssem = {k: st.enter_context(nc.semaphore("sd_%d" % i)) for i, k in enumerate(self.semkeys)}
            block = st.enter_context(nc.Block())

            def semof(key):
                return esem[key[1]] if key[0] == "e" else ssem[key[1]]

            def replay(engobj, name, extra=None):
                for waits, fn, semkey in self.ops[name]:
                    for key, val in waits.items():
                        engobj.wait_ge(semof(key), val)
                    ins = fn(engobj)
                    if semkey is not None:
                        ins.then_inc(ssem[semkey], 16)
                    else:
                        ins.then_inc(esem[name], 1)
                if extra:
                    extra(engobj)

            def final(engobj):
                for k in self.semkeys:
                    engobj.wait_ge(ssem[k], self.semvals[k])
                for e in COMPUTE:
                    if self.cnt[e]:
                        engobj.wait_ge(esem[e], self.cnt[e])

            @block.tensor
            def _(e):
                replay(e, "pe")

            @block.vector
            def _(e):
                replay(e, "dve")

            @block.gpsimd
            def _(e):
                replay(e, "pool")

            @block.scalar
            def _(e):
                replay(e, "act")

            @block.sync
            def _(e):
                replay(e, "sync", extra=final)


DM = 1024
TO = 2048
TP = 2048
TT = TO + TP
NH = 16
CZ, CX, CB, CC, CDT, CQ, CK, CV, CF = 0, 1024, 2048, 2304, 2560, 2576, 3600, 4624, 5648
ALPHA = 2.0 ** 0.25
EPS = 1e-5
NW = 52600
NEG = -30000.0

WEIGHT_NAMES = ["w_ada", "b_ada", "w_in", "conv_w", "conv_b", "dt_bias", "a_log", "d_skip", "ssm_norm_w",
                "f_bias", "attn_norm_w", "w_out", "ln1_g", "ln1_b", "w_ff_in", "w_ff_out", "ln2_g", "ln2_b"]
WEIGHT_SHAPES = {"w_ada": [1024, 6144], "b_ada": [1, 6144], "w_in": [1024, 5664], "conv_w": [4, 1536],
                 "conv_b": [1, 1536], "dt_bias": [1, 16], "a_log": [1, 16], "d_skip": [1, 16],
                 "ssm_norm_w": [1, 1024], "f_bias": [1, 16], "attn_norm_w": [1, 1024], "w_out": [2048, 1024],
                 "ln1_g": [1, 1024], "ln1_b": [1, 1024], "w_ff_in": [1024, 4096], "w_ff_out": [4096, 1024],
                 "ln2_g": [1, 1024], "ln2_b": [1, 1024]}


def build(stop_after=99):
    nc = bass.Bass("TRN2", target_bir_lowering=False)
    D = {}
    for n, shp in [("xo", [TO, DM]), ("xp", [TP, DM]), ("c", [1, DM]), ("flags", [128, 2])]:
        D[n] = nc.dram_tensor(n, shp, F32, kind="ExternalInput").ap()
    for n in WEIGHT_NAMES:
        D[n] = nc.dram_tensor(n, WEIGHT_SHAPES[n], F32, kind="ExternalInput").ap()
    out_d = nc.dram_tensor("out", [TO, DM], F32, kind="ExternalOutput").ap()
    x1s = nc.dram_tensor("x1s", [TO, DM], F32, kind="Internal").ap()
    c3d = nc.dram_tensor("c3d", [16, 3 * TT], BF16, kind="Internal").ap()
    ymsd = nc.dram_tensor("ymsd", [128, 8 * 2048], BF16, kind="Internal").ap()
    w1b = nc.dram_tensor("w1b", [1024, 4096], BF16, kind="Internal").ap()
    w2b = nc.dram_tensor("w2b", [4096, 1024], BF16, kind="Internal").ap()
    wob = nc.dram_tensor("wob", [2048, 1024], BF16, kind="Internal").ap()
    w_in = D["w_in"]

    st = contextlib.ExitStack()
    S = st.enter_context(nc.sbuf_tensor("S", [128, NW], F32))
    PS = st.enter_context(nc.psum_tensor("PS", [128, 4096], F32))
    P = Prog(nc)

    class A:
        top = 0

    regs = []

    def register(buf, lo, hi):
        for (l2, h2, b2) in regs:
            if l2 < hi and lo < h2 and b2 is not buf:
                if b2.lw is not None:
                    buf.rd.append(b2.lw)
                buf.rd.extend(b2.rd)
        regs.append((lo, hi, buf))
        return buf

    def alias(name, lo, hi, dt=F32, pat=None, **kw):
        assert hi <= NW, (name, hi)
        v = S[:, lo:hi]
        if dt == BF16:
            v = v.bitcast(BF16)
        if pat:
            v = v.rearrange(pat, **kw)
        return register(Buf(name, v), lo, hi)

    def sbw(name, words, dt=F32, pat=None, **kw):
        off = A.top
        A.top += words
        return alias(name, off, off + words, dt, pat, **kw)

    BK = [Buf("bk%d" % i, PS[:, 512 * i:512 * (i + 1)], PS[:, 512 * i:512 * (i + 1)].bitcast(BF16)) for i in range(8)]
    bkc = [0]

    ROT = [0, 1, 2, 3, 4, 7]

    def nb():
        b = BK[ROT[bkc[0] % len(ROT)]]
        bkc[0] += 1
        return b

    def MM(bank, out, lhsT, rhs, rd, start, stop=True):
        P.op("pe", lambda e: e.matmul(out, lhsT=lhsT, rhs=rhs, start=start, stop=stop, skip_group_check=True),
             reads=rd, writes=[bank])

    def TR(bank, out, in_, ident, rd):
        P.op("pe", lambda e: e.transpose(out=out, in_=in_, identity=ident), reads=rd, writes=[bank])

    def ACT(out, in_, func, rd, wr, bias=0.0, scale=1.0, accum=None):
        if accum is None:
            P.op("act", lambda e: e.activation(out=out, in_=in_, func=func, bias=bias, scale=scale), reads=rd, writes=wr)
        else:
            P.op("act", lambda e: e.activation(out=out, in_=in_, func=func, bias=bias, scale=scale, accum_out=accum),
                 reads=rd, writes=wr)

    def TT_(eng, out, in0, in1, op, rd, wr):
        P.op(eng, lambda e: e.tensor_tensor(out=out, in0=in0, in1=in1, op=op), reads=rd, writes=wr)

    def TS(eng, out, in0, s1, s2, op0, op1, rd, wr):
        if s2 is None:
            P.op(eng, lambda e: e.tensor_scalar(out=out, in0=in0, scalar1=s1, scalar2=None, op0=op0), reads=rd, writes=wr)
        else:
            P.op(eng, lambda e: e.tensor_scalar(out=out, in0=in0, scalar1=s1, scalar2=s2, op0=op0, op1=op1),
                 reads=rd, writes=wr)

    def STT(out, in0, scalar, in1, op0, op1, rd, wr):
        P.op("dve", lambda e: e.scalar_tensor_tensor(out=out, in0=in0, scalar=scalar, in1=in1, op0=op0, op1=op1),
             reads=rd, writes=wr)

    def CP(eng, out, in_, rd, wr):
        if eng == "act":
            P.op("act", lambda e: e.copy(out=out, in_=in_), reads=rd, writes=wr)
        else:
            P.op(eng, lambda e: e.tensor_copy(out=out, in_=in_), reads=rd, writes=wr)

    def MEMSET(eng, out, val, wr):
        P.op(eng, lambda e: e.memset(out, val), writes=wr)

    def DMA(q, out, in_, key, rd=(), wr=(), slow=False):
        if slow:
            P.dma(q, lambda e: e.dma_start(out=out, in_=in_, allow_slow_non_contiguous=True), key, reads=rd, writes=wr)
        else:
            P.dma(q, lambda e: e.dma_start(out=out, in_=in_), key, reads=rd, writes=wr)

    def wload(dst_buf, dst_ap_fn, rows0, c0, ncols, nk, key, src=None):
        src = w_in if src is None else src
        step = 8
        for k0 in range(0, nk, step):
            DMA("pool", dst_ap_fn(slice(k0, k0 + step)),
                src[rows0 + k0 * 128: rows0 + (k0 + step) * 128, c0:c0 + ncols].rearrange("(k p) n -> p k n", p=128),
                key, wr=[dst_buf])

    hT_all = sbw("hT", 16384, BF16, "p (k t) -> p k t", k=8)
    hTg = [register(Buf("hT%d" % g, hT_all.t[:, :, g * 512:(g + 1) * 512]), 0, 16384) for g in range(8)]
    YM0 = A.top
    ymT_all = sbw("ymT", 16384, BF16, "p (k t) -> p k t", k=16)
    ymS = [register(Buf("ymS%d" % b, ymT_all.t[:, 0:8, b * 128:(b + 1) * 128]), YM0, YM0 + 8192) for b in range(16)]
    ymA = [register(Buf("ymA%d" % b, ymT_all.t[:, 8:16, b * 128:(b + 1) * 128]), YM0 + 8192, YM0 + 16384) for b in range(16)]
    AT0 = YM0 + 8192

    identF = sbw("identF", 128)
    tri = sbw("tri", 128)
    onesF = sbw("onesF", 128)
    maskneg = sbw("maskneg", 128)
    identB = sbw("identB", 64, BF16)
    maskB = sbw("maskB", 64, BF16)
    convw = sbw("convw", 48, F32, "p (c k) -> p c k", k=4)
    convb = sbw("convb", 12)
    dtb = sbw("dtb", 16)
    Abc = sbw("Abc", 16)
    dsk = sbw("dsk", 16)
    fbb = sbw("fbb", 16)
    ssw = sbw("ssw", 8)
    atw = sbw("atw", 8)
    flg = sbw("flg", 2)
    ccol = sbw("ccol", 8)
    csil = sbw("csil", 4, BF16)
    modp1 = sbw("modp1", 16)
    modp2 = sbw("modp2", 16)
    modp3 = sbw("modp3", 16)
    atwh = sbw("atwh", 16)
    sscol = sbw("sscol", 16)
    valid = flg[:, 0:1]

    MEMSET("pool", identF[:], 0.0, [identF])
    P.op("pool", lambda e: e.affine_select(out=identF[:], in_=identF[:], pattern=[[-1, 128]], compare_op=ALU.not_equal,
                                           fill=1.0, base=0, channel_multiplier=1), reads=[identF], writes=[identF])
    MEMSET("pool", onesF[:], 1.0, [onesF])
    MEMSET("pool", tri[:], 1.0, [tri])
    P.op("pool", lambda e: e.affine_select(out=tri[:], in_=tri[:], pattern=[[1, 128]], compare_op=ALU.is_ge,
                                           fill=0.0, base=0, channel_multiplier=-1), reads=[tri], writes=[tri])
    MEMSET("pool", maskneg[:], 0.0, [maskneg])
    P.op("pool", lambda e: e.affine_select(out=maskneg[:], in_=maskneg[:], pattern=[[1, 128]], compare_op=ALU.is_ge,
                                           fill=NEG, base=0, channel_multiplier=-1), reads=[maskneg], writes=[maskneg])
    CP("pool", identB[:], identF[:], [identF], [identB])
    CP("pool", maskB[:], maskneg[:], [maskneg], [maskB])

    for k in range(4):
        DMA("sync", convw[:, :, k], D["conv_w"][k].rearrange("(c p) -> p c", p=128), "cw%d" % k, wr=[convw], slow=True)
    DMA("sync", convb[:], D["conv_b"][0].rearrange("(c p) -> p c", p=128), "cb", wr=[convb], slow=True)
    DMA("sync", ssw[:], D["ssm_norm_w"][0].rearrange("(c p) -> p c", p=128), "ssw", wr=[ssw], slow=True)
    DMA("sync", atw[:], D["attn_norm_w"][0].rearrange("(c p) -> p c", p=128), "atw", wr=[atw], slow=True)
    DMA("sync", atwh[0:64, :], D["attn_norm_w"][0].rearrange("(h d) -> d h", d=64), "atwh", wr=[atwh], slow=True)
    DMA("sync", ccol[:], D["c"][0].rearrange("(c p) -> p c", p=128), "ccol", wr=[ccol], slow=True)
    DMA("sync", dtb[:], D["dt_bias"][0:1, :].partition_broadcast(128), "dtb", wr=[dtb])
    DMA("sync", Abc[:], D["a_log"][0:1, :].partition_broadcast(128), "alog", wr=[Abc])
    DMA("sync", dsk[:], D["d_skip"][0:1, :].partition_broadcast(128), "dsk", wr=[dsk])
    DMA("sync", fbb[:], D["f_bias"][0:1, :].partition_broadcast(128), "fbb", wr=[fbb])
    DMA("sync", flg[:], D["flags"], "flg", wr=[flg])
    ACT(Abc[:], Abc[:], AF.Exp, [Abc], [Abc])
    TS("dve", Abc[:], Abc[:], -1.0, None, ALU.mult, None, [Abc], [Abc])
    ACT(csil[:], ccol[:], AF.Silu, [ccol], [csil])

    mark0 = A.top

    def ada_cols(c0, ncols, row, wt, brow):
        DMA("sync", brow[0:1, 0:ncols], D["b_ada"][0:1, c0:c0 + ncols], "brow", wr=[brow])
        for j in range(ncols // 512):
            DMA("pool", wt[:, :, :], D["w_ada"][:, c0 + j * 512: c0 + (j + 1) * 512].rearrange("(k p) n -> p k n", p=128),
                "wada", wr=[wt])
            bank = nb()
            for k in range(8):
                MM(bank, bank[0:1, 0:512], csil[:, k:k + 1], wt[:, k, :], [csil, wt], k == 0, k == 7)
            TT_("dve", row[0:1, j * 512:(j + 1) * 512], bank[0:1, 0:512], brow[0:1, j * 512:(j + 1) * 512], ALU.add,
                [bank, brow], [row])

    def row_to_cols(row, seg0, nseg, dst, dcol0, add1_from=None):
        bank = nb()
        for s in range(nseg):
            MM(bank, bank[:, s:s + 1], row[0:1, (seg0 + s) * 128:(seg0 + s + 1) * 128], onesF[0:1, 0:1], [row, onesF], s == 0)
        CP("dve", dst[:, dcol0:dcol0 + nseg], bank[:, 0:nseg], [bank], [dst])
        if add1_from is not None:
            TS("dve", dst[:, dcol0 + add1_from:dcol0 + nseg], dst[:, dcol0 + add1_from:dcol0 + nseg], 1.0, None, ALU.add, None,
               [dst], [dst])

    A.top = mark0
    Wz = sbw("Wz", 4096, BF16, "p (k n) -> p k n", k=8)
    markE = A.top
    row = sbw("row", 2048)
    brow = sbw("brow", 2048)
    wt_ada = sbw("wt_ada", 2048, BF16, "p (k n) -> p k n", k=8)
    ada_cols(0, 2048, row, wt_ada, brow)
    row_to_cols(row, 0, 16, modp1, 0, add1_from=8)
    Wc = alias("Wc", AT0 + 0, AT0 + 6144, BF16, "p (k n) -> p k n", k=8)
    Wdt = alias("Wdt", AT0 + 6144, AT0 + 6208, BF16, "p (k n) -> p k n", k=8)
    wload(Wc, lambda k: Wc[:, k, :], 0, CX, 1536, 8, "Wc")
    wload(Wdt, lambda k: Wdt[:, k, :], 0, CDT, 16, 8, "Wdt")
    wload(Wz, lambda k: Wz[:, k, :], 0, CZ, 1024, 8, "Wz")

    xin = [sbw("xin%d" % i, 1024) for i in range(2)]
    for blk in range(32):
        src = D["xp"][blk * 128:(blk + 1) * 128, :] if blk < 16 else D["xo"][(blk - 16) * 128:(blk - 15) * 128, :]
        xt = xin[blk % 2]
        DMA("sync", xt[:], src, "xin%d" % (blk % 2), wr=[xt])
        g = blk // 4
        t0 = blk * 128
        for half in range(2):
            bank = nb()
            for j in range(4):
                k = half * 4 + j
                TR(bank, bank[:, j * 128:(j + 1) * 128], xt[:, k * 128:(k + 1) * 128], identF[:], [xt, identF])
            for j in range(4):
                k = half * 4 + j
                if j % 2 == 0:
                    ACT(hT_all.t[:, k, t0:t0 + 128], bank[:, j * 128:(j + 1) * 128], AF.Identity, [bank, modp1], [hTg[g]],
                        bias=modp1[:, k:k + 1], scale=modp1[:, 8 + k:9 + k])
                else:
                    TS("dve", hT_all.t[:, k, t0:t0 + 128], bank[:, j * 128:(j + 1) * 128], modp1[:, 8 + k:9 + k],
                       modp1[:, k:k + 1], ALU.mult, ALU.add, [bank, modp1], [hTg[g]])

    rowB = alias("rowB", YM0, YM0 + 1024)
    browB = alias("browB", YM0 + 1024, YM0 + 2048)
    wtB = [alias("wtB%d" % i, YM0 + 2048 + i * 2048, YM0 + 4096 + i * 2048, BF16, "p (k n) -> p k n", k=8) for i in range(2)]

    def ada_part_load(part):
        c0 = 2048 + part * 1024
        DMA("sync", browB[0:1, 0:1024], D["b_ada"][0:1, c0:c0 + 1024], "browB", wr=[browB])
        for j in range(2):
            DMA("pool", wtB[j][:, :, :], D["w_ada"][:, c0 + j * 512: c0 + (j + 1) * 512].rearrange("(k p) n -> p k n", p=128),
                "wadaB%d" % j, wr=[wtB[j]])

    def ada_part_compute(part):
        for j in range(2):
            bank = nb()
            for k in range(8):
                MM(bank, bank[0:1, 0:512], csil[:, k:k + 1], wtB[j][:, k, :], [csil, wtB[j]], k == 0, k == 7)
            TT_("dve", rowB[0:1, j * 512:(j + 1) * 512], bank[0:1, 0:512], browB[0:1, j * 512:(j + 1) * 512], ALU.add,
                [bank, browB], [rowB])
        dst, dcol, add1 = [(modp3, 0, 0), (modp2, 0, None), (modp2, 8, 0), (modp3, 8, 0)][part]
        row_to_cols(rowB, 0, 8, dst, dcol, add1_from=add1)

    A.top = markE
    BT = alias("BT", AT0 + 6208, AT0 + 6720, BF16, "p (g t) -> p g t", g=2)
    CT = alias("CT", AT0 + 6720, AT0 + 7232, BF16, "p (g t) -> p g t", g=2)
    Btok = alias("Btok", AT0 + 7232, AT0 + 7744, BF16, "p (j n) -> p j n", j=4)
    xs_tok = sbw("xs_tok", 4096, F32, "p (j c) -> p j c", j=4)
    ubuf = [sbw("ubuf%d" % i, 516) for i in range(2)]
    halo = sbw("halo", 36, F32, "p (c k) -> p c k", k=3)
    acc0_off = A.top
    acc = [sbw("acc%d" % i, 512) for i in range(3)]
    xsT0_off = A.top
    xsT = [sbw("xsT%d" % i, 512) for i in range(2)]
    TAv = [S[:, acc0_off:acc0_off + 1024], S[:, xsT0_off:xsT0_off + 1024]]
    TAb = [[acc[0], acc[1]], [xsT[0], xsT[1]]]
    assert xsT0_off == acc0_off + 1536
    xc = sbw("xc", 512, BF16)
    xcd = sbw("xcd", 512, BF16)
    Sst = sbw("Sst", 1024)
    Sbf = sbw("Sbf", 512, BF16)
    zs = sbw("zs", 1024)
    y1 = sbw("y1", 1024)
    y2 = sbw("y2", 1024)
    MTb = sbw("MTb", 512, BF16, "p (h l) -> p h l", h=8)
    cbm = sbw("cbm", 256)
    Ust = sbw("Ust", 128)
    sms = [sbw("sm%d" % i, 64 * 8) for i in range(1)]
    sq = sbw("sq", 8)
    print("[kernel] SSD words free:", NW - A.top, flush=True)
    MEMSET("pool", Ust[:], 1.0, [Ust])
    P.op("pool", lambda e: e.affine_select(out=Ust[:], in_=Ust[:], pattern=[[-1, 128]], compare_op=ALU.is_gt,
                                           fill=0.0, base=0, channel_multiplier=1), reads=[Ust], writes=[Ust])

    MEMSET("pool", halo[:], 0.0, [halo])
    MEMSET("pool", Sst[:], 0.0, [Sst])

    cnt = [0]
    pend_fin = []
    SPAIRS = [(BK[0], BK[1], PS[:, 0:1024]), (BK[2], BK[3], PS[:, 1024:2048])]
    for G in range(8):
        own = G >= 4
        hg = hTg[G]
        tcol = slice(G * 512, (G + 1) * 512)
        if G == 4:
            hv = halo.t.rearrange("p c k -> p (c k)")
            TS("dve", hv, hv, valid, None, ALU.mult, None, [halo, flg], [halo])
            TS("dve", Sst[:], Sst[:], valid, None, ALU.mult, None, [Sst, flg], [Sst])
        cts = list(range(12)) if (own or G == 3) else list(range(10))
        if G < 4:
            ada_part_load(G)
        pend_c = []
        pend_d = []
        for ct in cts:
            i2 = cnt[0] % 2
            i3 = cnt[0] % 3
            cnt[0] += 1
            bank = nb()
            for k in range(8):
                MM(bank, bank[:, 0:512], Wc[:, k, ct * 128:(ct + 1) * 128], hT_all.t[:, k, tcol], [Wc, hg], k == 0, k == 7)
            while len(pend_d) > 2:
                pend_d.pop(0)()
            ub = ubuf[i2]
            CP("pool", ub[:, 0:3], halo[:, ct, :], [halo], [ub])
            CP("act", ub[:, 3:515], bank[:, 0:512], [bank], [ub])
            CP("pool", halo[:, ct, :], ub[:, 512:515], [ub], [halo])
            ac = acc[i3]
            ACT(ac[:], ub[:, 0:512], AF.Identity, [ub, convw, convb], [ac], bias=convb[:, ct:ct + 1], scale=convw[:, ct, 0:1])
            while len(pend_c) > 1:
                pend_c.pop(0)()
            for kk in range(1, 4):
                STT(ac[:], ub[:, kk:kk + 512], convw[:, ct, kk:kk + 1], ac[:], ALU.mult, ALU.add, [ub, convw, ac], [ac])
            if ct < 8:
                xt_ = xsT[i2]

                def cx(xt_=xt_, ac=ac):
                    ACT(xt_[:], ac[:], AF.Silu, [ac], [xt_])

                def dx(xt_=xt_, ct=ct):
                    b2 = nb()
                    for j in range(4):
                        TR(b2, b2[:, j * 128:(j + 1) * 128], xt_[:, j * 128:(j + 1) * 128], identF[:], [xt_, identF])
                    CP("dve" if ct % 2 else "act", xs_tok[:, :, ct * 128:(ct + 1) * 128],
                       b2[:, 0:512].rearrange("p (j c) -> p j c", j=4), [b2], [xs_tok])
                pend_c.append(cx)
                pend_d.append(dx)
            elif ct < 10:
                g = ct - 8

                def cb_(g=g, ac=ac):
                    ACT(BT[:, g, :], ac[:], AF.Silu, [ac], [BT])

                def db_(g=g):
                    b2 = nb()
                    for j in range(4):
                        TR(b2, b2.tb[:, j * 128:(j + 1) * 128], BT[:, g, j * 128:(j + 1) * 128], identB[:], [BT, identB])
                    CP("dve", Btok[:, :, g * 128:(g + 1) * 128], b2.tb[:, 0:512].rearrange("p (j c) -> p j c", j=4), [b2], [Btok])
                pend_c.append(cb_)
                pend_d.append(db_)
            else:
                g = ct - 10

                def cc_(g=g, ac=ac):
                    ACT(CT[:, g, :], ac[:], AF.Silu, [ac], [CT])
                pend_c.append(cc_)

        def flush_conv():
            while pend_c:
                pend_c.pop(0)()
            while pend_d:
                pend_d.pop(0)()
        if G < 4:
            ada_part_compute(G)
        bdt = nb()
        for j in range(4):
            ctok = slice((G * 4 + j) * 128, (G * 4 + j + 1) * 128)
            for k in range(8):
                MM(bdt, bdt[:, j * 16:(j + 1) * 16], hT_all.t[:, k, ctok], Wdt[:, k, :], [hg, Wdt], (j == 0 and k == 0), k == 7)
        flush_conv()
        sm = sms[0]
        xr4, dtt4, aa4, acs4, dout4, dte4, dch4, wv4 = [sm[:, i * 64:(i + 1) * 64] for i in range(8)]

        def v3(ap):
            return ap.rearrange("p (j h) -> p j h", j=4)
        TT_("dve", v3(xr4), v3(bdt[:, 0:64]), dtb[:].unsqueeze(1).to_broadcast([128, 4, 16]), ALU.add, [bdt, dtb], [sm])
        ACT(xr4, xr4, AF.Exp, [sm], [sm])
        ACT(dtt4, xr4, AF.Ln, [sm], [sm], bias=1.0)
        TT_("dve", v3(aa4), v3(dtt4), Abc[:].unsqueeze(1).to_broadcast([128, 4, 16]), ALU.mult, [sm, Abc], [sm])
        bcs = nb()
        MM(bcs, bcs[:, 0:64], tri[:], aa4, [tri, sm], True)
        MM(bcs, bcs[:, 64:128], onesF[:], aa4, [onesF, sm], False)
        CP("dve", acs4, bcs[:, 0:64], [bcs], [sm])
        ACT(dout4, bcs[:, 0:64], AF.Exp, [bcs], [sm])
        ACT(dch4, bcs[:, 64:128], AF.Exp, [bcs], [sm])
        TT_("dve", dte4, bcs[:, 64:128], acs4, ALU.subtract, [bcs, sm], [sm])
        ACT(dte4, dte4, AF.Exp, [sm], [sm])
        TT_("dve", wv4, dtt4, dte4, ALU.mult, [sm], [sm])
        for j in range(4):
            ch = G * 4 + j
            ctok = slice(ch * 128, (ch + 1) * 128)
            xr, dtt, aa, acs, dout, dte, dch, wv = [sm[:, i * 64 + j * 16:i * 64 + (j + 1) * 16] for i in range(8)]
            xsj = xs_tok[:, j, :].rearrange("p (h d) -> p h d", h=16)
            if not own:
                TT_("pool", xcd[:].rearrange("p (h d) -> p h d", h=16), xsj, wv.unsqueeze(2).to_broadcast([128, 16, 64]),
                    ALU.mult, [xs_tok, sm], [xcd])
            if own:
                CP("act", Sbf[:], Sst[:], [Sst], [Sbf])
                prs = []
                for g in range(2):
                    TT_("dve", TAv[g].rearrange("p (h l) -> p h l", h=8), tri[:].unsqueeze(1).to_broadcast([128, 8, 128]),
                        aa[:, g * 8:(g + 1) * 8].unsqueeze(2).to_broadcast([128, 8, 128]), ALU.mult, [tri, sm], TAb[g])
                TT_("pool", xc[:].rearrange("p (h d) -> p h d", h=16), xsj, dtt.unsqueeze(2).to_broadcast([128, 16, 64]),
                    ALU.mult, [xs_tok, sm] + TAb[1], [xc])
                TT_("pool", y2[:].rearrange("p (h d) -> p h d", h=16), xsj, dsk[:].unsqueeze(2).to_broadcast([128, 16, 64]),
                    ALU.mult, [xs_tok, dsk], [y2])
                TT_("pool", xcd[:].rearrange("p (h d) -> p h d", h=16), xsj, wv.unsqueeze(2).to_broadcast([128, 16, 64]),
                    ALU.mult, [xs_tok, sm], [xcd])
                for g in range(2):
                    b0, b1, pview = SPAIRS[g]
                    MM(b0, b0[:, 0:512], Ust[:], TAv[g][:, 0:512], [Ust] + TAb[g], True)
                    MM(b1, b1[:, 0:512], Ust[:], TAv[g][:, 512:1024], [Ust] + TAb[g], True)
                    ACT(pview, pview, AF.Exp, [b0, b1], [b0, b1])
                    prs.append((b0, b1, pview))
                bcb = BK[5]
                for g in range(2):
                    MM(bcb, bcb[:, g * 128:(g + 1) * 128], BT[:, g, j * 128:(j + 1) * 128], CT[:, g, j * 128:(j + 1) * 128],
                       [BT, CT], g == 0)
                TT_("dve", cbm[:].rearrange("p (g l) -> p g l", g=2), bcb[:, 0:256].rearrange("p (g l) -> p g l", g=2),
                    tri[:].unsqueeze(1).to_broadcast([128, 2, 128]), ALU.mult, [bcb, tri], [cbm])
                for half in range(2):
                    bz = BK[4 + 3 * half]
                    for k in range(8):
                        MM(bz, bz[:, 0:512], hT_all.t[:, k, ctok], Wz[:, k, half * 512:(half + 1) * 512], [hg, Wz], k == 0, k == 7)
                    ACT(zs[:, half * 512:(half + 1) * 512], bz[:, 0:512], AF.Silu, [bz], [zs])
                for g in range(2):
                    b0, b1, pview = prs[g]
                    TT_("dve", MTb[:], pview.rearrange("p (h l) -> p h l", h=8),
                        cbm[:, g * 128:(g + 1) * 128].unsqueeze(1).to_broadcast([128, 8, 128]), ALU.mult, [b0, b1, cbm], [MTb])
                    byd = BK[6]
                    for hh in range(8):
                        h = g * 8 + hh
                        MM(byd, byd[:, hh * 64:(hh + 1) * 64], MTb[:, hh, :], xc[:, h * 64:(h + 1) * 64], [MTb, xc], hh == 0)
                    if g == 0:
                        while pend_fin:
                            pend_fin.pop(0)()
                    boff = BK[4 + 3 * g]
                    MM(boff, boff[:, 0:512], CT[:, g, j * 128:(j + 1) * 128], Sbf[:, g * 512:(g + 1) * 512], [CT, Sbf], True)
                    ysl = slice(g * 512, (g + 1) * 512)
                    TT_("dve", y1[:, ysl].rearrange("p (h d) -> p h d", h=8), boff[:, 0:512].rearrange("p (h d) -> p h d", h=8),
                        dout[:, g * 8:(g + 1) * 8].unsqueeze(2).to_broadcast([128, 8, 64]), ALU.mult, [boff, sm], [y1])
                    TT_("dve", y1[:, ysl], byd[:, 0:512], y1[:, ysl], ALU.add, [byd, y1], [y1])
                TT_("dve", y2[:], y2[:], y1[:], ALU.add, [y2, y1], [y2])
                TT_("dve", y2[:], y2[:], zs[:], ALU.mult, [y2, zs], [y2])
                MEMSET("pool", sq[:, 0:1], 0.0, [sq])
                ACT(y1[:], y2[:], AF.Square, [y2], [y1, sq], accum=sq[:, 0:1])
                ACT(sq[:, 1:2], sq[:, 0:1], AF.Sqrt, [sq], [sq], bias=EPS, scale=1.0 / 1024.0)
                P.op("dve", lambda e: e.reciprocal(out=sq[:, 2:3], in_=sq[:, 1:2]), reads=[sq], writes=[sq])
                TS("dve", y1[:], y2[:], sq[:, 2:3], None, ALU.mult, None, [y2, sq], [y1])

                def fin(ob=ch - 16):
                    for half in range(2):
                        bt_ = BK[4 + 3 * half]
                        for jj in range(4):
                            kt = half * 4 + jj
                            TR(bt_, bt_[:, jj * 128:(jj + 1) * 128], y1[:, kt * 128:(kt + 1) * 128], identF[:], [y1, identF])
                        for jj in range(4):
                            kt = half * 4 + jj
                            ACT(ymT_all.t[:, kt, ob * 128:(ob + 1) * 128], bt_[:, jj * 128:(jj + 1) * 128], AF.Identity,
                                [bt_, ssw], [ymS[ob]], scale=ssw[:, kt:kt + 1])
                pend_fin.append(fin)
            for g in range(2):
                bs = nb()
                MM(bs, bs[:, 0:512], Btok[:, j, g * 128:(g + 1) * 128], xcd[:, g * 512:(g + 1) * 512], [Btok, xcd], True)
                ssl = slice(g * 512, (g + 1) * 512)
                TT_("pool", Sst[:, ssl].rearrange("p (h d) -> p h d", h=8), Sst[:, ssl].rearrange("p (h d) -> p h d", h=8),
                    dch[:, g * 8:(g + 1) * 8].unsqueeze(2).to_broadcast([128, 8, 64]), ALU.mult, [Sst, sm], [Sst])
                TT_("dve", Sst[:, ssl], bs[:, 0:512], Sst[:, ssl], ALU.add, [bs, Sst], [Sst])
    while pend_fin:
        pend_fin.pop(0)()

    A.top = mark0
    markF = A.top
    DMA("sync", ymsd.rearrange("p (k t) -> p k t", k=8), ymT_all.t[:, 0:8, :], "ymsp", rd=ymS)
    KaS = [[sbw("Ka0%d" % i, 2048, BF16) for i in range(2)],
           [alias("Ka1%d" % i, YM0 + i * 2048, YM0 + (i + 1) * 2048, BF16) for i in range(2)]]
    VaS = [sbw("Va0", 4096, BF16, "p (b h d) -> p b h d", b=32, h=2),
           alias("Va1", YM0 + 4096, YM0 + 8192, BF16, "p (b h d) -> p b h d", b=32, h=2)]
    QaS = [[sbw("Qa0%d" % i, 1024, BF16) for i in range(2)], None]
    Wqkv = sbw("Wqkv", 1536, BF16, "p (k n) -> p k n", k=8)
    pT = [sbw("pT%d" % i, 512, BF16) for i in range(3)]
    vts = [sbw("vts%d" % i, 256, BF16) for i in range(2)]
    print("[kernel] attention words free:", NW - A.top, flush=True)
    markF2 = A.top
    Wf = sbw("Wf", 64, BF16, "p (k n) -> p k n", k=8)
    lfa = sbw("lfa", 512)
    cumT = sbw("cumT", 512)
    r1 = sbw("r1", 512)
    c3s = sbw("c3s", 768, BF16, "p (r t) -> p r t", r=3)
    carry = sbw("carry", 1)

    wload(Wf, lambda k: Wf[:, k, :], 0, CF, 16, 8, "Wf")
    MEMSET("dve", carry[:], 0.0, [carry])
    MEMSET("pool", sscol[:], 0.0, [sscol])
    bf = nb()
    for blk in range(32):
        for k in range(8):
            MM(bf, bf[:, blk * 16:(blk + 1) * 16], hT_all.t[:, k, blk * 128:(blk + 1) * 128], Wf[:, k, :], [hTg[blk // 4], Wf],
               (blk == 0 and k == 0), k == 7)
    TT_("dve", lfa[:].rearrange("p (b h) -> p b h", b=32), bf[:, 0:512].rearrange("p (b h) -> p b h", b=32),
        fbb[:].unsqueeze(1).to_broadcast([128, 32, 16]), ALU.add, [bf, fbb], [lfa])
    ACT(lfa[:], lfa[:], AF.Exp, [lfa], [lfa], scale=-1.0)
    ACT(lfa[:], lfa[:], AF.Ln, [lfa], [lfa], bias=1.0)
    TS("dve", lfa[:], lfa[:], -1.0, None, ALU.mult, None, [lfa], [lfa])
    for q4 in range(8):
        bc = nb()
        for bq in range(4):
            blk = q4 * 4 + bq
            MM(bc, bc[0:16, bq * 128:(bq + 1) * 128], lfa[:, blk * 16:(blk + 1) * 16], tri[:], [lfa, tri], bq == 0)
        for bq in range(4):
            TS("dve", cumT[0:16, bq * 128:(bq + 1) * 128], bc[0:16, bq * 128:(bq + 1) * 128], carry[0:16, 0:1], None, ALU.add, None,
               [bc, carry], [cumT])
            CP("dve", carry[0:16, 0:1], cumT[0:16, bq * 128 + 127:bq * 128 + 128], [cumT], [carry])
        CP("dve", c3s[0:16, 0, :], cumT[0:16, :], [cumT], [c3s])
        TT_("dve", r1[0:16, :], cumT[0:16, :], c3s[0:16, 0, :], ALU.subtract, [cumT, c3s], [r1])
        CP("dve", c3s[0:16, 1, :], r1[0:16, :], [r1], [c3s])
        TT_("dve", r1[0:16, :], r1[0:16, :], c3s[0:16, 1, :], ALU.subtract, [r1, c3s], [r1])
        CP("dve", c3s[0:16, 2, :], r1[0:16, :], [r1], [c3s])
        DMA("sync", c3d.rearrange("h (r t) -> h r t", r=3)[:, :, q4 * 512:(q4 + 1) * 512], c3s[0:16, :, :], "c3w", rd=[c3s])
    c3tok = Buf("c3dram", None)
    c3tok.lw = ("s", "c3w", P.semvals["c3w"])
    c3v = c3d.rearrange("h (r t) -> h r t", r=3)
    A.top = markF2
    rbc = sbw("rbc", 512)
    yv = [sbw("yv%d" % i, 512) for i in range(2)]
    ysq = sbw("ysq", 512)
    QaS[1] = [sbw("Qa1%d" % i, 1024, BF16) for i in range(2)]

    for Va in VaS:
        MEMSET("pool", Va[:, :, :, 64:128], 1.0, [Va])
        TS("pool", Va[:, 0:16, :, 64:128], Va[:, 0:16, :, 64:128], valid, None, ALU.mult, None, [Va, flg], [Va])
    for sl in range(2):
        for i in range(2):
            MEMSET("pool", KaS[sl][i][64:128, :], 0.0, [KaS[sl][i]])
            MEMSET("pool", KaS[sl][i][64:70, :], 1.0, [KaS[sl][i]])
            MEMSET("pool", QaS[sl][i][64:128, :], 0.0, [QaS[sl][i]])
            MEMSET("pool", QaS[sl][i][64:70, :], -1.0, [QaS[sl][i]])

    pcnt = [0]
    ecnt = [0]
    fcnt = [0]
    deferred = []
    filler = []
    tcnt = [0]

    def tick():
        for d_ in deferred:
            d_[0] -= 1
        while deferred and deferred[0][0] <= 0:
            deferred.pop(0)[1]()
        tcnt[0] += 1
        if filler and tcnt[0] % 5 == 0:
            filler.pop(0)()

    def flush():
        while deferred:
            deferred.pop(0)[1]()

    def fbank():
        b = BK[4 + 3 * (fcnt[0] % 2)]
        fcnt[0] += 1
        return b

    def proj_units(hp):
        sl = hp % 2
        Ka, Qa, Va, wq = KaS[sl], QaS[sl], VaS[sl], Wqkv
        units = []

        def u_load():
            for i, c0 in enumerate((CQ, CK, CV)):
                wload(wq, (lambda i: (lambda k: wq[:, k, i * 128:(i + 1) * 128]))(i), 0, c0 + hp * 128, 128, 8, "Wq")
            if hp == 0:
                for r0 in range(0, 1024, 128):
                    DMA("pool", w1b[r0:r0 + 128, :], D["w_ff_in"][r0:r0 + 128, :], "precast")
                for r0 in range(0, 2048, 128):
                    DMA("pool", wob[r0:r0 + 128, :], D["w_out"][r0:r0 + 128, :], "precast")
                for r0 in range(0, 4096, 128):
                    DMA("pool", w2b[r0:r0 + 128, :], D["w_ff_out"][r0:r0 + 128, :], "precast")
            for hh in range(2):
                h = hp * 2 + hh
                DMA("sync", Ka[hh][67:70, :], c3v[h, :, :], "ka%d%d" % (sl, hh), rd=[c3tok], wr=[Ka[hh]])
                DMA("sync", Qa[hh][64:67, :], c3v[h, :, TP:TT], "qa%d%d" % (sl, hh), rd=[c3tok], wr=[Qa[hh]])
        units.append(u_load)

        def u_k(G):
            bk_ = fbank()
            for k in range(8):
                MM(bk_, bk_[:, 0:512], wq[:, k, 128:256], hT_all.t[:, k, G * 512:(G + 1) * 512], [wq, hTg[G]], k == 0, k == 7)
            CP("dve", Ka[0][0:64, G * 512:(G + 1) * 512], bk_[0:64, 0:512], [bk_], [Ka[0]])
            CP("dve", Ka[1][0:64, G * 512:(G + 1) * 512], bk_[64:128, 0:512], [bk_], [Ka[1]])

        def u_q(G):
            bq_ = fbank()
            for k in range(8):
                MM(bq_, bq_[:, 0:512], wq[:, k, 0:128], hT_all.t[:, k, TP + G * 512:TP + (G + 1) * 512], [wq, hTg[4 + G]],
                   k == 0, k == 7)
            TS("dve", Qa[0][0:64, G * 512:(G + 1) * 512], bq_[0:64, 0:512], 0.125, None, ALU.mult, None, [bq_], [Qa[0]])
            TS("dve", Qa[1][0:64, G * 512:(G + 1) * 512], bq_[64:128, 0:512], 0.125, None, ALU.mult, None, [bq_], [Qa[1]])

        def u_v(G):
            bv = fbank()
            for k in range(8):
                MM(bv, bv[:, 0:512], wq[:, k, 256:384], hT_all.t[:, k, G * 512:(G + 1) * 512], [wq, hTg[G]], k == 0, k == 7)
            vt = vts[G % 2]
            CP("dve", vt[:, 0:512], bv[:, 0:512], [bv], [vt])
            b2 = fbank()
            for j in range(4):
                TR(b2, b2.tb[:, j * 128:(j + 1) * 128], vt[:, j * 128:(j + 1) * 128], identB[:], [vt, identB])
            src = b2.tb[:, 0:512].rearrange("p (b h d) -> p b h d", b=4, h=2)
            if G < 4:
                TS("dve", Va[:, G * 4:(G + 1) * 4, :, 0:64], src, valid, None, ALU.mult, None, [b2, flg], [Va])
            else:
                CP("dve", Va[:, G * 4:(G + 1) * 4, :, 0:64], src, [b2], [Va])
        for G in range(8):
            units.append((lambda G: (lambda: u_k(G)))(G))
        for G in range(4):
            units.append((lambda G: (lambda: u_q(G)))(G))
        for G in range(8):
            units.append((lambda G: (lambda: u_v(G)))(G))
        return units

    PAIRS = [(BK[0], BK[1], PS[:, 0:1024]), (BK[2], BK[3], PS[:, 1024:2048])]
    for u in proj_units(0):
        u()
    for hp in range(NH // 2):
        sl = hp % 2
        Va = VaS[sl]
        if hp + 1 < NH // 2:
            filler.extend(proj_units(hp + 1))
        for hh in range(2):
            h = hp * 2 + hh
            ka, qa = KaS[sl][hh], QaS[sl][hh]
            for G in range(4):
                po = BK[5 + ecnt[0] % 2]
                npre = 16 + 4 * G
                nkb = npre + 4
                steps = [([kb, kb + 1], 0, 512) for kb in range(0, npre, 2)] + \
                        [([npre + r], r, (4 - r) * 128) for r in range(4)]
                pend = None
                for si in range(len(steps) + 1):
                    cur = None
                    if si < len(steps):
                        kbs, q0, ncol = steps[si]
                        pt = pT[pcnt[0] % 3]
                        if len(kbs) == 2:
                            b0, b1, pview = PAIRS[pcnt[0] % 2]
                            for bi, kb in zip((b0, b1), kbs):
                                MM(bi, bi[:, 0:512], ka[:, kb * 128:(kb + 1) * 128], qa[:, G * 512:(G + 1) * 512],
                                   [ka, qa], True, True)
                            ACT(pt[:, 0:1024], pview, AF.Exp, [b0, b1], [pt])
                        else:
                            kb = kbs[0]
                            bs_ = BK[4 + 3 * (q0 % 2)]
                            MM(bs_, bs_[:, 0:ncol], ka[:, kb * 128:(kb + 1) * 128],
                               qa[:, G * 512 + q0 * 128:(G + 1) * 512], [ka, qa], True, False)
                            MM(bs_, bs_[:, 0:128], identB[:], maskB[:], [identB, maskB], False, True)
                            ACT(pt[:, 0:ncol], bs_[:, 0:ncol], AF.Exp, [bs_], [pt])
                        pcnt[0] += 1
                        cur = (kbs, q0, ncol, pt)
                    if pend is not None:
                        kbs_, q0_, ncol_, pt_ = pend
                        for i_, kb_ in enumerate(kbs_):
                            MM(po, po[:, q0_ * 128:512], Va[:, kb_, hh, :], pt_[:, i_ * 512:i_ * 512 + ncol_], [Va, pt_],
                               kb_ == 0, kb_ == nkb - 1)
                    pend = cur
                    tick()
                P.op("dve", (lambda po_: (lambda e: e.reciprocal(out=rbc[0:64, :], in_=po_[64:128, 0:512])))(po),
                     reads=[po], writes=[rbc])
                yv_ = yv[ecnt[0] % 2]
                TT_("dve", yv_[0:64, :], po[0:64, 0:512], rbc[0:64, :], ALU.mult, [po, rbc], [yv_])
                pr = (h % 2) * 64
                TT_("dve", ysq[0:64, :], yv_[0:64, :], yv_[0:64, :], ALU.mult, [yv_], [ysq])
                TS("dve", ymT_all.t[pr:pr + 64, 8 + h // 2, G * 512:(G + 1) * 512], yv_[0:64, :], atwh[0:64, h:h + 1], None,
                   ALU.mult, None, [yv_, atwh], [ymA[G * 4 + j] for j in range(4)])

                def ep2(G=G, e_=ecnt[0]):
                    bss = BK[4 + 3 * (e_ % 2)]
                    for j4 in range(4):
                        MM(bss, bss[:, j4:j4 + 1], ysq[0:64, j4 * 128:(j4 + 1) * 128], onesF[0:64, 0:1], [onesF, ysq], j4 == 0)
                    TT_("dve", sscol[:, G * 4:(G + 1) * 4], bss[:, 0:4], sscol[:, G * 4:(G + 1) * 4], ALU.add,
                        [bss, sscol], [sscol])

                deferred.append([8, ep2])
                ecnt[0] += 1
        while filler:
            filler.pop(0)()
    flush()

    A.top = markF
    g1b = alias("g1b", 0, 1024)
    g2b = alias("g2b", 1024, 2048)
    lnb = alias("lnb", 2048, 6144, F32, "p (v n) -> p v n", v=4)
    h2Tb = alias("h2Tb", 6144, 14336, BF16, "p (k t) -> p k t", k=8)
    xr_ = [sbw("xr%d" % i, 1024) for i in range(2)]
    t1s = [sbw("t1_%d" % i, 1024) for i in range(2)]
    t2 = sbw("t2", 1024)
    sts = [sbw("st%d" % i, 16) for i in range(2)]
    t1 = t1s[0]
    st_ = sts[0]
    markGH = A.top
    Wo = sbw("Wo", 8192, BF16, "p (k n) -> p k n", k=16)
    dg = [sbw("dg%d" % i, 128) for i in range(2)]

    ymS = [register(Buf("ymS2_%d" % b, ymT_all.t[:, 0:8, b * 128:(b + 1) * 128]), YM0, YM0 + 8192) for b in range(16)]
    spill = Buf("ymsd", None)
    spill.lw = ("s", "ymsp", P.semvals["ymsp"])
    DMA("sync", ymT_all.t[:, 0:8, :], ymsd.rearrange("p (k t) -> p k t", k=8), "ymrl", rd=[spill], wr=ymS)
    pctok = Buf("precast", None)
    pctok.lw = ("s", "precast", P.semvals["precast"])
    for v, nme in enumerate(["ln1_g", "ln1_b", "ln2_g", "ln2_b"]):
        DMA("sync", lnb[:, v, :], D[nme][0:1, :].partition_broadcast(128), "lnb%d" % v, wr=[lnb])
    for k0 in (0, 8):
        DMA("sync", Wo[:, k0:k0 + 8, :], wob[k0 * 128:(k0 + 8) * 128, :].rearrange("(k p) n -> p k n", p=128), "Wo",
            rd=[pctok], wr=[Wo])

    def col_bcast(c0, dst):
        for half in range(2):
            bank = nb()
            for j in range(4):
                k = half * 4 + j
                d_ = dg[k % 2]
                TS("dve", d_[:], identF[:], modp3[:, c0 + k:c0 + k + 1], None, ALU.mult, None, [identF, modp3], [d_])
                MM(bank, bank[:, j * 128:(j + 1) * 128], onesF[:], d_[:], [onesF, d_], j == 0)
            CP("act", dst[:, half * 512:(half + 1) * 512], bank[:, 0:512], [bank], [dst])

    col_bcast(0, g1b)
    col_bcast(8, g2b)

    def layer_norm(src, dst, gi, st_):
        MEMSET("dve", st_[:, 0:2], 0.0, [st_])
        ACT(t2[:], src[:], AF.Identity, [src], [st_], accum=st_[:, 0:1])
        ACT(t2[:], src[:], AF.Square, [src], [st_], accum=st_[:, 1:2])
        TS("dve", st_[:, 2:3], st_[:, 0:1], 1.0 / 1024.0, None, ALU.mult, None, [st_], [st_])
        TT_("dve", st_[:, 3:4], st_[:, 2:3], st_[:, 2:3], ALU.mult, [st_], [st_])
        STT(st_[:, 4:5], st_[:, 1:2], 1.0 / 1024.0, st_[:, 3:4], ALU.mult, ALU.subtract, [st_], [st_])
        ACT(st_[:, 5:6], st_[:, 4:5], AF.Sqrt, [st_], [st_], bias=EPS)
        P.op("dve", lambda e: e.reciprocal(out=st_[:, 6:7], in_=st_[:, 5:6]), reads=[st_], writes=[st_])
        TS("dve", dst[:], src[:], st_[:, 2:3], st_[:, 6:7], ALU.subtract, ALU.mult, [src, st_], [dst])
        TT_("dve", dst[:], dst[:], lnb[:, gi, :], ALU.mult, [dst, lnb], [dst])
        TT_("dve", dst[:], dst[:], lnb[:, gi + 1, :], ALU.add, [dst, lnb], [dst])

    x1toks = [Buf("x1dram%d" % i, None) for i in range(2)]
    pend_g = []
    for ob in range(16):
        xt = xr_[ob % 2]
        t1 = t1s[ob % 2]
        st_ = sts[ob % 2]
        DMA("sync", xt[:], D["xo"][ob * 128:(ob + 1) * 128, :], "xr%d" % (ob % 2), wr=[xt])
        ACT(st_[:, 9:10], sscol[:, ob:ob + 1], AF.Sqrt, [sscol], [st_], bias=EPS, scale=1.0 / 1024.0)
        P.op("dve", lambda e, st_=st_: e.reciprocal(out=st_[:, 10:11], in_=st_[:, 9:10]), reads=[st_], writes=[st_])
        for half in range(2):
            b1 = nb()
            for kt in range(8):
                MM(b1, b1[:, 0:512], ymT_all.t[:, kt, ob * 128:(ob + 1) * 128], Wo[:, kt, half * 512:(half + 1) * 512],
                   [ymS[ob], Wo], kt == 0, kt == 7)
            b2 = nb()
            for kt in range(8, 16):
                MM(b2, b2[:, 0:512], ymT_all.t[:, kt, ob * 128:(ob + 1) * 128], Wo[:, kt, half * 512:(half + 1) * 512],
                   [ymA[ob], Wo], kt == 8, kt == 15)
            hs = slice(half * 512, (half + 1) * 512)
            CP("act", t1[:, hs], b1[:, 0:512], [b1], [t1])
            STT(t1[:, hs], b2[:, 0:512], st_[:, 10:11], t1[:, hs], ALU.mult, ALU.add, [b2, st_, t1], [t1])
        while pend_g:
            pend_g.pop(0)()
        TT_("dve", t1[:], t1[:], g1b[:], ALU.mult, [t1, g1b], [t1])
        STT(t1[:], xt[:], ALPHA, t1[:], ALU.mult, ALU.add, [xt, t1], [t1])
        layer_norm(t1, xt, 0, st_)
        DMA("sync", x1s[ob * 128:(ob + 1) * 128, :], xt[:], "x1w%d" % (ob % 2), rd=[xt])

        def trg(xt=xt, ob=ob):
            for half in range(2):
                bt_ = nb()
                for jj in range(4):
                    k = half * 4 + jj
                    TR(bt_, bt_[:, jj * 128:(jj + 1) * 128], xt[:, k * 128:(k + 1) * 128], identF[:], [xt, identF])
                for jj in range(4):
                    k = half * 4 + jj
                    if jj % 2 == 0:
                        ACT(h2Tb[:, k, ob * 128:(ob + 1) * 128], bt_[:, jj * 128:(jj + 1) * 128], AF.Identity, [bt_, modp2],
                            [h2Tb], bias=modp2[:, k:k + 1], scale=modp2[:, 8 + k:9 + k])
                    else:
                        TS("dve", h2Tb[:, k, ob * 128:(ob + 1) * 128], bt_[:, jj * 128:(jj + 1) * 128], modp2[:, 8 + k:9 + k],
                           modp2[:, k:k + 1], ALU.mult, ALU.add, [bt_, modp2], [h2Tb])
        pend_g.append(trg)
    while pend_g:
        pend_g.pop(0)()
    for i in range(2):
        x1toks[i].lw = ("s", "x1w%d" % i, P.semvals["x1w%d" % i])

    W2 = alias("W2", YM0, YM0 + 16384, BF16, "p (f n) -> p f n", f=32)
    for f0 in range(0, 32, 8):
        DMA("sync", W2[:, f0:f0 + 8, :], w2b[f0 * 128:(f0 + 8) * 128, :].rearrange("(f p) n -> p f n", p=128), "W2",
            rd=[pctok], wr=[W2])
    A.top = markGH
    a1T = sbw("a1T", 8192, BF16, "p (f t) -> p f t", f=32)
    W1t = [sbw("W1t%d" % i, 2048, BF16, "p (k n) -> p k n", k=8) for i in range(2)]
    rl = [sbw("rl%d" % i, 512) for i in range(2)]
    wcnt = [0]
    for G in range(4):
        for c4 in range(8):
            w1 = W1t[wcnt[0] % len(W1t)]
            DMA("sync", w1[:, :, :], w1b[:, c4 * 512:(c4 + 1) * 512].rearrange("(k p) n -> p k n", p=128),
                "W1t%d" % (wcnt[0] % len(W1t)), rd=[pctok], wr=[w1])
            wcnt[0] += 1
            for f4 in range(4):
                f = c4 * 4 + f4
                bf_ = nb()
                for k in range(8):
                    MM(bf_, bf_[:, 0:512], w1[:, k, f4 * 128:(f4 + 1) * 128], h2Tb[:, k, G * 512:(G + 1) * 512], [w1, h2Tb],
                       k == 0, k == 7)
                r_ = rl[f % 2]
                ACT(r_[:], bf_[:, 0:512], AF.Relu, [bf_], [r_])
                TT_("dve", a1T[:, f, :], r_[:], r_[:], ALU.mult, [r_], [a1T])
        for tb in range(4):
            ob = G * 4 + tb
            xt = xr_[ob % 2]
            t1 = t1s[ob % 2]
            st_ = sts[ob % 2]
            DMA("sync", xt[:], x1s[ob * 128:(ob + 1) * 128, :], "xr%d" % (ob % 2), rd=x1toks, wr=[xt])
            for half in range(2):
                bo = nb()
                for f in range(32):
                    MM(bo, bo[:, 0:512], a1T[:, f, tb * 128:(tb + 1) * 128], W2[:, f, half * 512:(half + 1) * 512], [a1T, W2],
                       f == 0, f == 31)
                hs = slice(half * 512, (half + 1) * 512)
                TT_("dve", t1[:, hs], bo[:, 0:512], g2b[:, hs], ALU.mult, [bo, g2b], [t1])
            STT(t1[:], xt[:], ALPHA, t1[:], ALU.mult, ALU.add, [xt, t1], [t1])
            layer_norm(t1, xt, 2, st_)
            DMA("sync", out_d[ob * 128:(ob + 1) * 128, :], xt[:], "ow%d" % (ob % 2), rd=[xt])

    print("[kernel] ops:", {k: len(v) for k, v in P.ops.items()}, "cnt:", P.cnt, "nsem:", len(P.semkeys), flush=True)
    P.emit()
    st.close()
    return nc


_NC = [None]


def kernel(**inputs):
    x = np.ascontiguousarray(np.asarray(inputs["x"], dtype=np.float32))
    c = np.asarray(inputs["c"], dtype=np.float32)
    if _NC[0] is None:
        _NC[0] = build()
    nc = _NC[0]
    wmap = {}
    for n in WEIGHT_NAMES:
        wmap[n] = np.ascontiguousarray(np.asarray(inputs[n], dtype=np.float32).reshape(WEIGHT_SHAPES[n]))
    in_maps = []
    for core in range(8):
        b, h = core // 2, core % 2
        m = dict(wmap)
        m["xo"] = np.ascontiguousarray(x[b, h * TO:(h + 1) * TO])
        m["xp"] = np.ascontiguousarray(x[b, 0:TP])
        m["c"] = np.ascontiguousarray(c[b:b + 1])
        fl = np.zeros((128, 2), np.float32)
        fl[:, 0] = float(h)
        m["flags"] = fl
        in_maps.append(m)
    res = run_bass_kernel_spmd(nc, in_maps, core_ids=list(range(8)))
    out = np.empty((4, 4096, DM), np.float32)
    for core in range(8):
        b, h = core // 2, core % 2
        out[b, h * TO:(h + 1) * TO] = res.results[core]["out"]
    return out
```
